# Optimizing a Trainium2 kernel written in Bass

```python
import math
import jax, jax.numpy as jnp
from jax import lax
import numpy as np

D_MODEL = 1024
BATCH = 16
SEQ = 2048
DEPTH = 2
DEC_BATCH = 16
DEC_SEQ = 64
PAST_LEN = 4096

CHUNK = 64
GDN_HEADS = 4
GDN_DK = 128
GDN_DV = 128
GDN_CONV = 4
GDN_QKV = GDN_HEADS * (2 * GDN_DK + GDN_DV)
GDN_BLOCK = CHUNK
MLP_HEADS = 4
MLP_HEAD_DIM = 64
MLP_WIDTH = MLP_HEADS * MLP_HEAD_DIM
MLP_CHUNK = 128
SWA_HEADS = 4
SWA_KV_HEADS = 2
SWA_GROUP = SWA_HEADS // SWA_KV_HEADS
SWA_HEAD_DIM = 64
WINDOW = 128
SWA_BACK = WINDOW // CHUNK
SWA_SCALE = SWA_HEAD_DIM ** -0.5
MIX_WIDTH = GDN_HEADS * GDN_DV + MLP_WIDTH + SWA_HEADS * SWA_HEAD_DIM
D_FF = -(-8 * D_MODEL // (3 * 256)) * 256
IN_SIZES = (GDN_QKV, GDN_HEADS * GDN_DV, GDN_HEADS, GDN_HEADS, MLP_WIDTH, MLP_WIDTH,
            SWA_HEADS * SWA_HEAD_DIM, SWA_KV_HEADS * SWA_HEAD_DIM, SWA_KV_HEADS * SWA_HEAD_DIM)
IN_COLS = sum(IN_SIZES)
EPS = 1e-6

kernel_name = 'hybrid_gdn_gmlp_swa_stream_step'


def _rms_norm(x, g):
    xf = x.astype(jnp.float32)
    y = xf * lax.rsqrt(jnp.mean(xf * xf, axis=-1, keepdims=True) + EPS)
    return (y * g.astype(jnp.float32)).astype(x.dtype)


def _layer_norm(x, g, b):
    xf = x.astype(jnp.float32)
    mu = jnp.mean(xf, axis=-1, keepdims=True)
    xc = xf - mu
    y = xc * lax.rsqrt(jnp.mean(xc * xc, axis=-1, keepdims=True) + EPS)
    return (y * g.astype(jnp.float32) + b.astype(jnp.float32)).astype(x.dtype)


def _l2norm(x):
    xf = x.astype(jnp.float32)
    return xf * lax.rsqrt(jnp.sum(xf * xf, axis=-1, keepdims=True) + EPS)


def _causal_dwconv(x_ext, w):
    return lax.conv_general_dilated(x_ext, w[:, None, :], window_strides=(1,), padding='VALID',
                                    dimension_numbers=('NWC', 'WIO', 'NWC'),
                                    feature_group_count=x_ext.shape[-1])


def _gated_delta_blocks(q, k, v, g, beta, s0, block):
    bsz, L, H, DK = q.shape
    DV = v.shape[-1]
    n = L // block

    def to_blocks(t):
        t = t.reshape((bsz, n, block, H) + t.shape[3:])
        return jnp.moveaxis(t, (1, 3), (0, 2))

    qb, kb, vb, gb, bb = to_blocks(q), to_blocks(k), to_blocks(v), to_blocks(g), to_blocks(beta)
    gc = jnp.cumsum(gb, axis=-1)
    tri_incl = jnp.tril(jnp.ones((block, block), bool))
    tri_strict = jnp.tril(jnp.ones((block, block), bool), -1)
    decay = jnp.exp(jnp.where(tri_incl, gc[..., :, None] - gc[..., None, :], -jnp.inf))
    kbeta = kb * bb[..., None]
    a_low = jnp.where(tri_strict, jnp.einsum('nbhid,nbhjd->nbhij', kbeta, kb) * decay, 0.0)
    rhs = jnp.concatenate([vb * bb[..., None], kbeta * jnp.exp(gc)[..., None]], axis=-1)
    sol = lax.linalg.triangular_solve(a_low + jnp.eye(block, dtype=a_low.dtype), rhs,
                                      left_side=True, lower=True, unit_diagonal=True)
    u_b, w_b = sol[..., :DV], sol[..., DV:]
    attn_intra = jnp.where(tri_incl, jnp.einsum('nbhid,nbhjd->nbhij', qb, kb) * decay, 0.0)

    def step(S, xs):
        q_i, k_i, u_i, w_i, gc_i, a_i = xs
        v_new = u_i - jnp.einsum('bhcd,bhde->bhce', w_i, S)
        o = (jnp.einsum('bhcd,bhde->bhce', q_i * jnp.exp(gc_i)[..., None], S)
             + jnp.einsum('bhij,bhje->bhie', a_i, v_new))
        g_last = gc_i[..., -1]
        S = (S * jnp.exp(g_last)[..., None, None]
             + jnp.einsum('bhcd,bhce->bhde', k_i * jnp.exp(g_last[..., None] - gc_i)[..., None], v_new))
        return S, o

    S, o = lax.scan(step, s0, (qb, kb, u_b, w_b, gc, attn_intra))
    o = jnp.moveaxis(o, (0, 2), (1, 3)).reshape(bsz, L, H, DV)
    return o, S


def _gdn_mixer(qkv_raw, z, b_raw, a_raw, conv_prev, s0, conv_w, a_log, dt_bias, norm_g, block):
    bsz, L, _ = qkv_raw.shape
    ext = jnp.concatenate([conv_prev.astype(qkv_raw.dtype), qkv_raw], axis=1)
    qkv = jax.nn.silu(_causal_dwconv(ext, conv_w.astype(ext.dtype)))
    q, k, v = jnp.split(qkv, [GDN_HEADS * GDN_DK, 2 * GDN_HEADS * GDN_DK], axis=-1)
    q = _l2norm(q.reshape(bsz, L, GDN_HEADS, GDN_DK)) * (GDN_DK ** -0.5)
    k = _l2norm(k.reshape(bsz, L, GDN_HEADS, GDN_DK))
    v = v.reshape(bsz, L, GDN_HEADS, GDN_DV).astype(jnp.float32)
    beta = jax.nn.sigmoid(b_raw.astype(jnp.float32))
    g = -jnp.exp(a_log.astype(jnp.float32)) * jax.nn.softplus(a_raw.astype(jnp.float32) + dt_bias.astype(jnp.float32))
    o, s_new = _gated_delta_blocks(q, k, v, g, beta, s0.astype(jnp.float32), block)
    gate = jax.nn.silu(z.astype(jnp.float32)).reshape(bsz, L, GDN_HEADS, GDN_DV)
    o = _rms_norm(o, norm_g) * gate
    return o.reshape(bsz, L, GDN_HEADS * GDN_DV).astype(qkv_raw.dtype), s_new, ext[:, -(GDN_CONV - 1):]


def _gmlp_mixer(u_raw, v_raw, ln_g, ln_b, ws, bs, first):
    bsz, L, _ = u_raw.shape
    u = jax.nn.gelu(u_raw).reshape(bsz, L, MLP_HEADS, MLP_HEAD_DIM)
    v = _layer_norm(jax.nn.gelu(v_raw).reshape(bsz, L, MLP_HEADS, MLP_HEAD_DIM), ln_g, ln_b)
    cpos = jnp.arange(MLP_CHUNK) // CHUNK
    ws_m = jnp.where(cpos[:, None] >= cpos[None, :], ws, 0)
    if first:
        n = L // MLP_CHUNK
        vb = v.reshape(bsz, n, MLP_CHUNK, MLP_HEADS, MLP_HEAD_DIM)
        s = jnp.einsum('hij,bnjhd->bnihd', ws_m, vb) + bs.T[:, :, None]
        s = s.reshape(bsz, L, MLP_HEADS, MLP_HEAD_DIM)
    else:
        s = jnp.einsum('hij,bjhd->bihd', ws_m[:, :L, :L], v) + bs[:, :L].T[:, :, None]
    return (u * s).reshape(bsz, L, MLP_WIDTH), v


def _sink_softmax(s, sinks):
    sk = jnp.broadcast_to(sinks.astype(jnp.float32).reshape(SWA_KV_HEADS, SWA_GROUP, 1, 1), s.shape[:-1] + (1,))
    return jax.nn.softmax(jnp.concatenate([s, sk], axis=-1), axis=-1)[..., :-1]


def _swa_prompt(q, k, v, sinks):
    bsz, L = q.shape[:2]
    n = L // CHUNK
    nk = (SWA_BACK + 1) * CHUNK
    qb = q.reshape(bsz, n, CHUNK, SWA_KV_HEADS, SWA_GROUP, SWA_HEAD_DIM)

    def band(t):
        pad = jnp.zeros((bsz, SWA_BACK * CHUNK) + t.shape[2:], t.dtype)
        tp = jnp.concatenate([pad, t], axis=1).reshape(bsz, n + SWA_BACK, CHUNK, SWA_KV_HEADS, SWA_HEAD_DIM)
        return jnp.concatenate([tp[:, j:j + n] for j in range(SWA_BACK + 1)], axis=2)

    kb, vb = band(k), band(v)
    kpos = (jnp.arange(n)[:, None] - SWA_BACK) * CHUNK + jnp.arange(nk)[None, :]
    s = jnp.einsum('bnqkgd,bnskd->bnkgqs', qb, kb).astype(jnp.float32) * SWA_SCALE
    s = jnp.where((kpos >= 0)[None, :, None, None, None, :], s, -jnp.inf)
    p = _sink_softmax(s, sinks).astype(v.dtype)
    o = jnp.einsum('bnkgqs,bnskd->bnqkgd', p, vb)
    return o.reshape(bsz, L, SWA_HEADS * SWA_HEAD_DIM)


def _swa_sample(q, k, v, k_prev, v_prev, sinks):
    bsz, T = q.shape[:2]
    k_all = jnp.concatenate([k_prev.astype(k.dtype), k], axis=1)
    v_all = jnp.concatenate([v_prev.astype(v.dtype), v], axis=1)
    qg = q.reshape(bsz, T, SWA_KV_HEADS, SWA_GROUP, SWA_HEAD_DIM)
    s = jnp.einsum('btkgd,bskd->bkgts', qg, k_all).astype(jnp.float32) * SWA_SCALE
    p = _sink_softmax(s, sinks).astype(v.dtype)
    o = jnp.einsum('bkgts,bskd->btkgd', p, v_all).reshape(bsz, T, SWA_HEADS * SWA_HEAD_DIM)
    return o, k_all, v_all


def _layer(x, first, conv_prev, s0, k_prev, v_prev, norm1_g, w_in, conv_w, a_log, dt_bias, gdn_norm_g,
           ln_g, ln_b, ws, bs, q_norm_g, k_norm_g, sinks, w_out, norm2_g, w_gate, w_up, w_down):
    bsz, L, _ = x.shape
    h = _rms_norm(x, norm1_g)
    parts = jnp.split(h @ w_in, np.cumsum(IN_SIZES)[:-1].tolist(), axis=-1)
    qkv_raw, z, b_raw, a_raw, u_raw, vm_raw, sq, sk, sv = parts
    o_a, s_new, conv_new = _gdn_mixer(qkv_raw, z, b_raw, a_raw, conv_prev, s0, conv_w, a_log, dt_bias,
                                      gdn_norm_g, GDN_BLOCK if first else L)
    o_b, v_rows = _gmlp_mixer(u_raw, vm_raw, ln_g, ln_b, ws, bs, first)
    q = _rms_norm(sq.reshape(bsz, L, SWA_HEADS, SWA_HEAD_DIM), q_norm_g)
    k = _rms_norm(sk.reshape(bsz, L, SWA_KV_HEADS, SWA_HEAD_DIM), k_norm_g)
    v = sv.reshape(bsz, L, SWA_KV_HEADS, SWA_HEAD_DIM)
    if first:
        o_c = _swa_prompt(q, k, v, sinks)
        k_all, v_all = k, v
    else:
        o_c, k_all, v_all = _swa_sample(q, k, v, k_prev, v_prev, sinks)
    x = x + jnp.concatenate([o_a, o_b, o_c], axis=-1) @ w_out
    h = _rms_norm(x, norm2_g)
    x = x + (jax.nn.silu(h @ w_gate) * (h @ w_up)) @ w_down
    return x, (k_all[:, -WINDOW:], v_all[:, -WINDOW:], s_new.astype(x.dtype), conv_new, v_rows)


def setup_inputs(seed: int = 0) -> dict:
    key = jax.random.key(seed)
    ks = jax.random.split(key, 24)
    f32 = jnp.float32

    def nrm(k, shape, scale=1.0):
        return jax.random.normal(k, shape, f32) * scale

    def gain(k, shape):
        return 1.0 + 0.02 * jax.random.normal(k, shape, f32)

    dt = jnp.exp(jax.random.uniform(ks[10], (DEPTH, GDN_HEADS), f32, math.log(1e-3), math.log(1e-1)))
    return {
        'x_prompt': nrm(ks[0], (BATCH, SEQ, D_MODEL)),
        'x_sample': nrm(ks[1], (DEC_BATCH, DEC_SEQ, D_MODEL)),
        'cache_swa_k': nrm(ks[2], (DEPTH, DEC_BATCH, WINDOW, SWA_KV_HEADS, SWA_HEAD_DIM)),
        'cache_swa_v': nrm(ks[3], (DEPTH, DEC_BATCH, WINDOW, SWA_KV_HEADS, SWA_HEAD_DIM)),
        'state_gdn': nrm(ks[4], (DEPTH, DEC_BATCH, GDN_HEADS, GDN_DK, GDN_DV), 0.2),
        'state_gdn_conv': nrm(ks[5], (DEPTH, DEC_BATCH, GDN_CONV - 1, GDN_QKV)),
        'norm1_g': gain(ks[6], (DEPTH, D_MODEL)),
        'w_in': nrm(ks[7], (DEPTH, D_MODEL, IN_COLS), D_MODEL ** -0.5),
        'gdn_conv_w': nrm(ks[8], (DEPTH, GDN_CONV, GDN_QKV), GDN_CONV ** -0.5),
        'gdn_a_log': jnp.log(jax.random.uniform(ks[9], (DEPTH, GDN_HEADS), f32, 1.0, 16.0)),
        'gdn_dt_bias': dt + jnp.log(-jnp.expm1(-dt)),
        'gdn_norm_g': gain(ks[11], (DEPTH, GDN_DV)),
        'mlp_ln_g': gain(ks[12], (DEPTH, MLP_HEADS, MLP_HEAD_DIM)),
        'mlp_ln_b': nrm(ks[13], (DEPTH, MLP_HEADS, MLP_HEAD_DIM), 0.02),
        'mlp_ws': nrm(ks[14], (DEPTH, MLP_HEADS, MLP_CHUNK, MLP_CHUNK), MLP_CHUNK ** -0.5),
        'mlp_bs': gain(ks[15], (DEPTH, MLP_HEADS, MLP_CHUNK)),
        'swa_q_norm_g': gain(ks[16], (DEPTH, SWA_HEAD_DIM)),
        'swa_k_norm_g': gain(ks[17], (DEPTH, SWA_HEAD_DIM)),
        'swa_sinks': nrm(ks[18], (DEPTH, SWA_HEADS), 0.5),
        'w_out': nrm(ks[19], (DEPTH, MIX_WIDTH, D_MODEL), MIX_WIDTH ** -0.5),
        'norm2_g': gain(ks[20], (DEPTH, D_MODEL)),
        'ffn_w_gate': nrm(ks[21], (DEPTH, D_MODEL, D_FF), D_MODEL ** -0.5),
        'ffn_w_up': nrm(ks[22], (DEPTH, D_MODEL, D_FF), D_MODEL ** -0.5),
        'ffn_w_down': nrm(ks[23], (DEPTH, D_FF, D_MODEL), D_FF ** -0.5),
    }


def reference(x_prompt, x_sample, cache_swa_k, cache_swa_v, state_gdn, state_gdn_conv,
              norm1_g, w_in, gdn_conv_w, gdn_a_log, gdn_dt_bias, gdn_norm_g,
              mlp_ln_g, mlp_ln_b, mlp_ws, mlp_bs, swa_q_norm_g, swa_k_norm_g, swa_sinks,
              w_out, norm2_g, ffn_w_gate, ffn_w_up, ffn_w_down):
    yp, ys = x_prompt, x_sample
    p_k, p_v, p_s, p_c = [], [], [], []
    s_k, s_v, s_s, s_c, s_m = [], [], [], [], []
    for l in range(DEPTH):
        wl = (norm1_g[l], w_in[l], gdn_conv_w[l], gdn_a_log[l], gdn_dt_bias[l], gdn_norm_g[l],
              mlp_ln_g[l], mlp_ln_b[l], mlp_ws[l], mlp_bs[l], swa_q_norm_g[l], swa_k_norm_g[l],
              swa_sinks[l], w_out[l], norm2_g[l], ffn_w_gate[l], ffn_w_up[l], ffn_w_down[l])
        conv0 = jnp.zeros((yp.shape[0], GDN_CONV - 1, GDN_QKV), yp.dtype)
        s0 = jnp.zeros((yp.shape[0], GDN_HEADS, GDN_DK, GDN_DV), jnp.float32)
        yp, (k_, v_, st_, cv_, _) = _layer(yp, True, conv0, s0, None, None, *wl)
        p_k.append(k_); p_v.append(v_); p_s.append(st_); p_c.append(cv_)
        ys, (k_, v_, st_, cv_, m_) = _layer(ys, False, state_gdn_conv[l], state_gdn[l],
                                            cache_swa_k[l], cache_swa_v[l], *wl)
        s_k.append(k_); s_v.append(v_); s_s.append(st_); s_c.append(cv_); s_m.append(m_)
    return (yp, ys, jnp.stack(p_k), jnp.stack(p_v), jnp.stack(p_s), jnp.stack(p_c),
            jnp.stack(s_k), jnp.stack(s_v), jnp.stack(s_s), jnp.stack(s_c), jnp.stack(s_m))
```

```python
import numpy as np
from contextlib import ExitStack
import concourse.bass as bass
import concourse.mybir as mybir
from concourse.bass_utils import run_bass_kernel_spmd

F32 = mybir.dt.float32
BF16 = mybir.dt.bfloat16
AF = mybir.ActivationFunctionType
ALU = mybir.AluOpType
AX = mybir.AxisListType
COMPUTE = ("pe", "act", "dve", "pool")

D = 1024; L = 2; SEQ = 2048; TS = 64; DFF = 2816; NFF = 22
NCORE = 8
BIG = 30000.0


class Op:
    __slots__ = ("eng", "fn", "reads", "writes", "dma", "lidx", "waits", "need_inc",
                 "incval", "dmaval", "clock", "gidx", "cost", "start", "finish", "deps")


class Prog:
    def __init__(self):
        self.ops = []
        self.state = {}
        self.children = {}
        self.dma_count = {}

    def add(self, eng, fn, reads=(), writes=(), dma=None, cost=None):
        op = Op()
        op.eng = eng; op.fn = fn
        op.cost = cost
        op.reads = [tuple(r) if isinstance(r, (tuple, list)) else (r,) for r in reads]
        op.writes = [tuple(w) if isinstance(w, (tuple, list)) else (w,) for w in writes]
        op.dma = dma
        op.gidx = len(self.ops)
        op.need_inc = False
        self.ops.append(op)
        return op

    def _conflicts(self, key):
        out = []
        for n in range(1, len(key) + 1):
            k = key[:n]
            if k in self.state:
                out.append(k)
        for k in self.children.get(key, ()):
            if k != key and k in self.state:
                out.append(k)
        return out

    def _touch(self, key):
        if key not in self.state:
            self.state[key] = [None, []]
            for n in range(1, len(key)):
                self.children.setdefault(key[:n], set()).add(key)

    def _analyze(self):
        for op in self.ops:
            deps = {}

            def adddep(a, kind):
                if a is None or a is op:
                    return
                old = deps.get(a.gidx)
                if old is None or (kind == "RAW" and old[1] != "RAW"):
                    deps[a.gidx] = (a, kind)

            for r in op.reads:
                self._touch(r)
                for k in self._conflicts(r):
                    adddep(self.state[k][0], "RAW")
            for w in op.writes:
                self._touch(w)
                for k in self._conflicts(w):
                    st = self.state[k]
                    adddep(st[0], "WAW")
                    for rd in st[1]:
                        adddep(rd, "WAR")
            for r in op.reads:
                self.state[r][1].append(op)
            for w in op.writes:
                self.state[w] = [op, []]
                for k in list(self.children.get(w, ())):
                    if k in self.state:
                        self.state[k] = [op, []]
            op.waits = list(deps.values())

    @staticmethod
    def _needed(a, b, kind):
        if a.dma is not None or b.dma is not None:
            return True
        if a.eng != b.eng:
            return True
        if a.eng == "pe":
            return False
        return kind == "RAW"

    def _schedule(self):
        import heapq
        ops = self.ops
        n = len(ops)
        DEF = {"pe": 0.14, "act": 0.45, "dve": 0.4, "pool": 0.5, "sp": 0.1}
        succ = [[] for _ in range(n)]
        indeg = [0] * n
        for op in ops:
            op.deps = [a for (a, kind) in op.waits]
            indeg[op.gidx] = len(op.deps)
            for a in op.deps:
                succ[a.gidx].append(op)
        import os
        mode = os.environ.get("MK_SCHED", "hlf")
        rank = [0.0] * n
        for op in reversed(ops):
            c = op.cost if op.cost is not None else DEF[op.eng]
            m = 0.0
            for sck in succ[op.gidx]:
                if rank[sck.gidx] > m:
                    m = rank[sck.gidx]
            rank[op.gidx] = c + m
        engines = sorted(set(o.eng for o in ops))
        free_at = {e: 0.0 for e in engines}
        dma_free = [0.0]
        avail = {e: [] for e in engines}
        ready_t = [0.0] * n
        for op in ops:
            if indeg[op.gidx] == 0:
                heapq.heappush(avail[op.eng], op.gidx)
        done = 0
        HOP = 0.25
        while done < n:
            best = None
            for e in engines:
                h = avail[e]
                if not h:
                    continue
                t = free_at[e]
                cand = None; cand_key = None
                for gi in (h[:256] if len(h) > 256 else h):
                    rt = ready_t[gi]
                    if mode == "hlf":
                        key = (max(rt, t), -rank[gi], gi)
                    else:
                        key = (max(rt, t), gi, gi)
                    if cand_key is None or key < cand_key:
                        cand_key = key; cand = gi
                if best is None or cand_key < best[0]:
                    best = (cand_key, e, cand)
            (st_, _k2, _k3), e, gi = best
            avail[e].remove(gi); heapq.heapify(avail[e])
            op = ops[gi]
            c = op.cost if op.cost is not None else DEF[op.eng]
            op.start = st_
            if op.dma is not None:
                free_at[e] = st_ + 0.08
                xfer = max(0.0, c - 2.2) * 0.6
                t0_ = max(st_, dma_free[0])
                dma_free[0] = t0_ + xfer
                op.finish = t0_ + xfer + 2.2
            else:
                free_at[e] = st_ + c
                op.finish = st_ + c
            done += 1
            for sck in succ[gi]:
                k = sck.gidx
                lat = op.finish + (HOP if (sck.eng != op.eng or op.dma is not None) else 0.0)
                if lat > ready_t[k]:
                    ready_t[k] = lat
                indeg[k] -= 1
                if indeg[k] == 0:
                    heapq.heappush(avail[sck.eng], k)
        order = sorted(ops, key=lambda o: (o.start, o.gidx))
        self.sim_span = max(o.finish for o in ops)
        self.ops = order

    def finalize(self, reorder=True):
        self._analyze()
        if reorder:
            self._schedule()
        for op in self.ops:
            if op.dma is not None:
                self.dma_count[op.dma] = self.dma_count.get(op.dma, 0) + 1
                op.dmaval = 16 * self.dma_count[op.dma]
        engines = sorted(set(o.eng for o in self.ops))
        cnt = {e: 0 for e in engines}
        for op in self.ops:
            cnt[op.eng] += 1
            op.lidx = cnt[op.eng]
        known = {e: {} for e in engines}
        pos = {}
        for i_, op in enumerate(self.ops):
            pos[op.gidx] = i_
        for op in self.ops:
            kn = known[op.eng]
            need = []
            for (a, kind) in sorted(op.waits, key=lambda t: -pos[t[0].gidx]):
                if not self._needed(a, op, kind):
                    continue
                if a.dma is not None:
                    src = ("dma", a.dma); val = a.dmaval
                else:
                    src = a.eng; val = a.lidx
                if kn.get(src, 0) >= val:
                    continue
                need.append(a)
                a.need_inc = True
                kn[src] = max(kn.get(src, 0), val)
                for s, v in a.clock.items():
                    if kn.get(s, 0) < v:
                        kn[s] = v
            op.waits = need
            op.clock = dict(kn)
        incc = {e: 0 for e in engines}
        for op in self.ops:
            if op.dma is None and op.need_inc:
                incc[op.eng] += 1
                op.incval = incc[op.eng]
        self.streams = {e: [] for e in engines}
        for op in self.ops:
            self.streams[op.eng].append(op)
        self.stats = {e: (cnt[e], incc[e]) for e in engines}

    def run_engine(self, e, engobj, sems, dma_sems):
        for op in self.streams.get(e, []):
            w = {}
            for a in op.waits:
                if a.dma is not None:
                    s = dma_sems[a.dma]; v = a.dmaval
                else:
                    s = sems[a.eng]; v = a.incval
                k = id(s)
                if k not in w or w[k][1] < v:
                    w[k] = (s, v)
            ws = list(w.values())
            emb = None
            if ws and op.dma is None and e in ("act", "dve", "pool"):
                emb = ws.pop()
            for (s, v) in ws:
                engobj.wait_ge(s, v)
            ins = op.fn(engobj)
            if emb is not None:
                ins._wait_ge(emb[0], emb[1])
            if op.dma is not None:
                ins.then_inc(dma_sems[op.dma], 16)
            elif op.need_inc:
                ins.then_inc(sems[op.eng], 1)


def _consts():
    c = {}
    c["identF"] = np.eye(128, dtype=np.float32)
    half = (np.arange(128) // 64)
    same = half[:, None] == half[None, :]
    p = np.arange(128)[:, None]; f = np.arange(128)[None, :]
    mbI = np.where(same & (f >= p), 0.0, -BIG)
    mbS = np.where(same & (f > p), 0.0, -BIG)
    mbS2 = np.where(same & (p > f), 0.0, BIG)
    c["mb3"] = np.concatenate([mbI, mbS, mbS2], axis=1).astype(np.float32)
    c["ones64"] = np.where(same, 1.0 / 64, 0.0).astype(np.float32)
    ind = np.zeros((128, 4, 128), np.float32)
    for h in range(4):
        ind[:, h, h] = 1.0
    c["indH"] = ind.reshape(128, 512)
    sel = np.zeros((128, 4, 128), np.float32)
    for h in range(4):
        sel[h, h, :] = 1.0
    c["sel"] = sel.reshape(128, 512)
    m = np.ones((4, 256), np.float32); m[:, ::64] = 0.0
    c["scanmask"] = m
    cv = np.zeros((128, 8), np.float32)
    cv[:, 0] = 1e-6
    cv[:, 1] = np.log(128.0 ** -0.5)
    cv[:, 2] = 1.0
    c["cvals"] = cv
    return c


IN_OFF = dict(qkv=0, z=1536, b=2048, a=2052, u=2056, vm=2312, sq=2568, sk=2824, sv=2952)


def _layout_weights(inp):
    w_in = np.asarray(inp["w_in"], np.float32)
    cols = list(range(0, 2048))
    cols += list(range(IN_OFF["u"], IN_OFF["u"] + 256))
    sq = IN_OFF["sq"]
    cols += list(range(sq, sq + 64)) + list(range(sq + 128, sq + 192))
    cols += list(range(sq + 64, sq + 128)) + list(range(sq + 192, sq + 256))
    cols += list(range(IN_OFF["sk"], IN_OFF["sk"] + 128))
    wF = w_in[:, :, cols].reshape(L, 8, 128, 21 * 128).transpose(0, 2, 1, 3)
    colsT = list(range(IN_OFF["a"], IN_OFF["a"] + 4)) + list(range(IN_OFF["vm"], IN_OFF["vm"] + 256)) \
        + list(range(IN_OFF["b"], IN_OFF["b"] + 4)) + list(range(IN_OFF["sv"], IN_OFF["sv"] + 128))
    wT = w_in[:, :, colsT].reshape(L, 8, 128, 392).transpose(0, 2, 1, 3)
    out = {}
    out["w_inF"] = np.ascontiguousarray(wF)
    out["w_inT"] = np.ascontiguousarray(wT)
    out["w_outL"] = np.ascontiguousarray(np.asarray(inp["w_out"], np.float32).reshape(L, 8, 128, D).transpose(0, 2, 1, 3))
    out["w_gateL"] = np.ascontiguousarray(np.asarray(inp["ffn_w_gate"], np.float32).reshape(L, 8, 128, DFF).transpose(0, 2, 1, 3))
    out["w_upL"] = np.ascontiguousarray(np.asarray(inp["ffn_w_up"], np.float32).reshape(L, 8, 128, DFF).transpose(0, 2, 1, 3))
    out["w_downL"] = np.ascontiguousarray(np.asarray(inp["ffn_w_down"], np.float32).reshape(L, NFF, 128, D).transpose(0, 2, 1, 3))
    f = lambda k: np.asarray(inp[k], np.float32)
    out["g1c"] = np.ascontiguousarray(f("norm1_g").reshape(L, 8, 128).transpose(2, 0, 1).reshape(128, L * 8))
    out["g2c"] = np.ascontiguousarray(f("norm2_g").reshape(L, 8, 128).transpose(2, 0, 1).reshape(128, L * 8))
    out["convc"] = np.ascontiguousarray(f("gdn_conv_w").reshape(L, 4, 12, 128).transpose(3, 0, 2, 1).reshape(128, L * 48))
    out["alog"] = np.ascontiguousarray(f("gdn_a_log").T)
    out["dtb"] = np.ascontiguousarray(f("gdn_dt_bias").T)
    out["gng"] = np.ascontiguousarray(f("gdn_norm_g").T)
    out["lng"] = np.ascontiguousarray(f("mlp_ln_g").reshape(L, 256))
    out["lnb"] = np.ascontiguousarray(f("mlp_ln_b").reshape(L, 256))
    out["ws"] = np.ascontiguousarray(f("mlp_ws"))
    out["bs"] = np.ascontiguousarray(f("mlp_bs"))
    out["qng"] = np.ascontiguousarray(np.tile(f("swa_q_norm_g"), (1, 2)).T)
    out["kng"] = np.ascontiguousarray(np.tile(f("swa_k_norm_g"), (1, 2)).T)
    out["sinks"] = np.ascontiguousarray(f("swa_sinks"))
    return out


def build():
    nc = bass.Bass("TRN2", target_bir_lowering=False)
    P = Prog()
    es = ExitStack()

    def din(name, shape):
        return nc.dram_tensor(name, list(shape), F32, kind="ExternalInput").ap()

    def dout(name, shape):
        return nc.dram_tensor(name, list(shape), F32, kind="ExternalOutput").ap()

    xp = din("xp", [2, SEQ, D]); xs_ = din("xs", [2, TS, D])
    cK = din("cK", [L, 2, 128, 128]); cV = din("cV", [L, 2, 128, 128])
    sG = din("sG", [L, 2, 4, 128, 128]); sC = din("sC", [L, 2, 3, 1536])
    w_inF = din("w_inF", [L, 128, 8, 21 * 128]); w_inT = din("w_inT", [L, 128, 8, 392])
    w_outL = din("w_outL", [L, 128, 8, D]); w_gateL = din("w_gateL", [L, 128, 8, DFF])
    w_upL = din("w_upL", [L, 128, 8, DFF]); w_downL = din("w_downL", [L, 128, NFF, D])
    g1c = din("g1c", [128, L * 8]); g2c = din("g2c", [128, L * 8]); convc = din("convc", [128, L * 48])
    alog = din("alog", [4, L]); dtb = din("dtb", [4, L]); gng = din("gng", [128, L])
    lng = din("lng", [L, 256]); lnb = din("lnb", [L, 256]); ws = din("ws", [L, 4, 128, 128]); bs = din("bs", [L, 4, 128])
    qng = din("qng", [128, L]); kng = din("kng", [128, L]); sinks = din("sinks", [L, 4])
    c_ident = din("identF", [128, 128]); c_mb3 = din("mb3", [128, 384]); c_ones64 = din("ones64", [128, 128])
    c_indH = din("indH", [128, 512]); c_sel = din("sel", [128, 512]); c_scan = din("scanmask", [4, 256]); c_cv = din("cvals", [128, 8])

    yp = dout("yp", [2, SEQ, D]); ys = dout("ys", [2, TS, D])
    o_kp = dout("o_kp", [L, 2, 128, 128]); o_vp = dout("o_vp", [L, 2, 128, 128])
    o_gp = dout("o_gp", [L, 2, 4, 128, 128]); o_cp = dout("o_cp", [L, 2, 3, 1536])
    o_ks = dout("o_ks", [L, 2, 128, 128]); o_vs = dout("o_vs", [L, 2, 128, 128])
    o_gs = dout("o_gs", [L, 2, 4, 128, 128]); o_cs = dout("o_cs", [L, 2, 3, 1536])
    o_ms = dout("o_ms", [L, 2, TS, 256])

    def sb(name, shape, dt=F32):
        return es.enter_context(nc.sbuf_tensor("sb_" + name, list(shape), dt))

    banks = [es.enter_context(nc.psum_tensor("bank%d" % i, [128, 512], F32)) for i in range(8)]
    rr = [0]

    def pb():
        pool = dpool[0]
        rr[0] = (rr[0] + 1) % len(pool)
        return pool[rr[0]]

    def B(i):
        return banks[i], ("ps", i)

    rrA = [0]

    def pbA():
        i = rrA[0]
        rrA[0] = (rrA[0] + 1) % 3
        return i

    pbB = pbA
    dpool = [[3, 4, 5]]

    def interleave(gens):
        gens = list(gens)
        while gens:
            for g_ in list(gens):
                try:
                    next(g_)
                except StopIteration:
                    gens.remove(g_)

    sems = {e: es.enter_context(nc.semaphore("s_" + e)) for e in COMPUTE}
    dma_sems = {}

    def dsem(key):
        if key not in dma_sems:
            dma_sems[key] = es.enter_context(nc.semaphore("d%d" % len(dma_sems)))
        return key

    def fsz(ap):
        n = 1
        for d in ap.shape[1:]:
            n *= int(d)
        return n

    def mm(out, lhsT, rhs, start, r, w, tr=False):
        c = max(64, fsz(rhs)) * 0.00052 + 0.02
        if tr and rhs.dtype == F32:
            c *= 2.0
        P.add("pe", lambda e: e.matmul(out=out, lhsT=lhsT, rhs=rhs, start=start, stop=True, is_transpose=(True if tr else None)), reads=r, writes=w, cost=c)

    def act(out, in_, func, r, w, scale=None, bias=None):
        kw = {}
        if scale is not None:
            kw["scale"] = scale
        if bias is not None:
            kw["bias"] = bias
        P.add("act", lambda e: e.activation(out=out, in_=in_, func=func, **kw), reads=r, writes=w, cost=0.2 + fsz(out) * 0.00085)

    def tt(out, in0, in1, op, r, w, eng="dve"):
        P.add(eng, lambda e: e.tensor_tensor(out=out, in0=in0, in1=in1, op=op), reads=r, writes=w,
              cost=(0.1 + fsz(out) * 0.00105) if eng == "dve" else (0.15 + fsz(out) * 0.0021))

    def stt(out, in0, scalar, in1, op0, op1, r, w):
        P.add("dve", lambda e: e.scalar_tensor_tensor(out=out, in0=in0, scalar=scalar, in1=in1, op0=op0, op1=op1), reads=r, writes=w,
              cost=0.1 + fsz(out) * 0.00105)

    def ts(out, in0, s1, op0, r, w, s2=None, op1=None, eng="dve"):
        if op1 is None:
            P.add(eng, lambda e: e.tensor_scalar(out=out, in0=in0, scalar1=s1, scalar2=None, op0=op0), reads=r, writes=w, cost=0.1 + fsz(out) * 0.0007)
        else:
            P.add(eng, lambda e: e.tensor_scalar(out=out, in0=in0, scalar1=s1, scalar2=s2, op0=op0, op1=op1), reads=r, writes=w, cost=0.1 + fsz(out) * 0.0007)

    def cp(out, in_, r, w, eng="dve"):
        if eng == "act":
            P.add("act", lambda e: e.copy(out=out, in_=in_), reads=r, writes=w, cost=0.2 + fsz(out) * 0.00085)
        else:
            P.add(eng, lambda e: e.tensor_copy(out=out, in_=in_), reads=r, writes=w,
                  cost=(0.1 + fsz(out) * 0.0008) if eng == "dve" else (0.12 + fsz(out) * 0.0016))

    def mset(ap, val, w, eng="pool"):
        P.add(eng, lambda e: e.memset(ap, val), writes=w, cost=0.1 + fsz(ap) * 0.001)

    store_keys = set()

    def dma(eng, out, in_, r, w, key, store=False, slow=False):
        if store:
            store_keys.add(key)
        nbytes = fsz(out) * 128 * (2 if out.dtype == BF16 else 4)
        c = 2.2 + nbytes / 180e3
        if slow:
            P.add(eng, lambda e: e.dma_start(out=out, in_=in_, allow_slow_non_contiguous=True), reads=r, writes=w, dma=dsem(key), cost=c)
        else:
            P.add(eng, lambda e: e.dma_start(out=out, in_=in_), reads=r, writes=w, dma=dsem(key), cost=c)

    dummy = sb("dummyk", [128, 8])

    def barrier(keys, reads=()):
        P.add("pool", lambda e: e.memset(dummy[:], 0.0), reads=list(reads), writes=keys)

    SETUP = ("setup",)
    nset = [0]

    def setup_load(out, in_, eng="sp"):
        nset[0] += 1
        dma(eng, out, in_, [], [("setup", nset[0])], "setup")

    identF = sb("identF", [128, 128]); identB = sb("identB", [128, 128], BF16); negIB = sb("negIB", [128, 128], BF16)
    mb3 = sb("mb3", [128, 384], BF16)
    ones64 = sb("ones64", [128, 128], BF16)
    onesD = sb("onesD", [128, 128], BF16); onesDV = sb("onesDV", [128, 128], BF16); onesB = sb("onesB", [128, 128], BF16)
    indH = sb("indH", [128, 512], BF16)
    selB = sb("selB", [128, 512], BF16)
    scanm = sb("scanm", [4, 256]); cv = sb("cv", [128, 8])
    g1 = sb("g1", [128, L * 8]); g2 = sb("g2", [128, L * 8]); cw = sb("cw", [128, L * 48])
    alog_s = sb("alog_s", [4, L]); dtb_s = sb("dtb_s", [4, L]); nega = sb("nega", [4, L]); gng_s = sb("gng_s", [128, L])
    lng_bc = sb("lng_bc", [128, L, 256]); lnb_bc = sb("lnb_bc", [128, L, 256])
    qng_s = sb("qng_s", [128, L]); kng_s = sb("kng_s", [128, L]); esink = sb("esink", [128, L * 4])
    wsn = sb("wsn", [128, 128]); wsT = sb("wsT", [128, L, 2, 4, 128], BF16)
    bsbc = sb("bsbc", [128, L, 2, 2, 128])

    for (t, src) in ((mb3, c_mb3), (ones64, c_ones64), (indH, c_indH), (selB, c_sel)):
        setup_load(t[:], src, eng="pool")
    for (t, src) in ((identF, c_ident), (scanm, c_scan),
                     (cv, c_cv), (g1, g1c), (g2, g2c), (cw, convc), (alog_s, alog), (dtb_s, dtb), (gng_s, gng),
                     (qng_s, qng), (kng_s, kng)):
        setup_load(t[:], src)
    for l in range(L):
        setup_load(lng_bc[:, l, :], lng[l:l + 1, :].partition_broadcast(128))
        setup_load(lnb_bc[:, l, :], lnb[l:l + 1, :].partition_broadcast(128))
        setup_load(esink[:, l * 4:(l + 1) * 4], sinks[l:l + 1, :].partition_broadcast(128))
        for pr in range(2):
            for hh in range(2):
                h = 2 * pr + hh
                setup_load(bsbc[hh * 64:(hh + 1) * 64, l, 0, pr, :], bs[l, h:h + 1, :].partition_broadcast(64))
                for s in range(2):
                    setup_load(bsbc[hh * 64:(hh + 1) * 64, l, 1, pr, s * 64:(s + 1) * 64], bs[l, h:h + 1, 0:64].partition_broadcast(64))
    cp(identB[:], identF[:], [SETUP], ["identB"])
    ts(negIB[:], identF[:], -1.0, ALU.mult, [SETUP], ["negIB"])
    barrier([("mb3",), ("ones64",), ("indH",), ("selB",)], reads=[SETUP])
    mset(onesD[:], 1.0 / 1024, ["onesD"]); mset(onesDV[:], 1.0 / 128, ["onesDV"]); mset(onesB[:], 1.0, ["onesB"])
    act(nega[:], alog_s[:], AF.Exp, [SETUP], ["nega"])
    ts(nega[:], nega[:], -1.0, ALU.mult, ["nega"], ["nega"])
    act(esink[:], esink[:], AF.Exp, [SETUP], ["esinkx"])
    for l in range(L):
        for h in range(4):
            for var in range(2):
                key = dsem("wsn")
                if var == 0:
                    dma("sp", wsn[:], ws[l, h], [], ["wsn"], "wsn")
                    mset(wsn[0:64, 64:128], 0.0, ["wsn"])
                else:
                    mset(wsn[:], 0.0, ["wsn"])
                    dma("sp", wsn[0:64, 0:64], ws[l, h, 0:64, 0:64], [], [("wsn", 0)], "wsn")
                    dma("sp", wsn[64:128, 64:128], ws[l, h, 0:64, 0:64], [], [("wsn", 1)], "wsn")
                bi = pb(); bk, br = B(bi)
                mm(bk[:, 0:128], wsn[:], identF[:], True, ["wsn", SETUP], [br], tr=True)
                cp(wsT[:, l, var, h, :], bk[:, 0:128], [br], [("wsT", l, var, h)])

    xtok = [sb("xtok0", [128, D])] * 2
    xTs = [sb("xT%d" % i, [128, 8, 256]) for i in range(3)]
    hT = sb("hT", [128, 8, 256], BF16)
    sqb = [sb("sqb%d" % i, [128, 256], BF16) for i in range(2)]
    lnv = sb("lnv", [128, 256]); rstd = sb("rstd", [128, 256])
    raw = [sb("raw%d" % i, [128, 2, 131]) for i in range(2)]
    hist = sb("hist", [128, L, 12, 2, 3])
    cacc = [sb("cacc%d" % i, [128, 2, 128]) for i in range(2)]
    sl = sb("sl", [128, 12, 256], BF16)
    qh = sb("qh", [128, 4, 256], BF16); qg = sb("qg", [128, 4, 256], BF16); kh = sb("kh", [128, 4, 256], BF16)
    kn = sb("kn", [128, 4, 256], BF16); kdT = sb("kdT", [128, 4, 256], BF16); vbT = sb("vbT", [128, 4, 256], BF16)
    zgs = [sb("zg%d" % i, [128, 4, 256], BF16) for i in range(2)]; uT = sb("uT", [128, 2, 256])
    bvt = sb("bvt", [128, 2, 4, 128], BF16); kdt = sb("kdt", [128, 2, 4, 128], BF16)
    vnew = sb("vnew", [128, 2, 4, 128], BF16); tmpp = sb("tmpp", [128, 4, 128], BF16)
    ATP = sb("ATP", [128, 2, 4, 256], BF16)
    Qm = [[sb("Qm%d_%d" % (q, i), [128, 4, 128], BF16) for i in range(2)] for q in range(2)]
    Pm = [[sb("Pm%d_%d" % (q, i), [128, 4, 128], BF16) for i in range(2)] for q in range(2)]
    Rb = [sb("Rb%d" % q, [128, 4, 128], BF16) for q in range(2)]
    sqbW = sb("sqbW", [128, 256], BF16); lnvW = sb("lnvW", [128, 256]); rstdW = sb("rstdW", [128, 256])
    TT = sb("TT", [128, 2, 4, 128], BF16)
    E12 = sb("E12", [128, 2, 256]); E3 = sb("E3", [128, 2, 128])
    gcT = sb("gcT", [128, 2, 8])
    eglbc = sb("eglbc", [128, 4, 4])
    oT = sb("oT", [128, 4, 256]); OT = sb("OT", [128, 8, 256], BF16)
    actT = sb("actT", [128, NFF, 256], BF16); sg = [sb("sg%d" % i, [128, 256]) for i in range(2)]
    S = sb("S", [128, L, 2, 4, 128]); Sb = sb("Sb", [128, L, 2, 4, 128], BF16)
    rA = sb("rA", [4, 256]); rB_ = sb("rB", [4, 256]); rG = rA; rGC = sb("rGC", [4, 256]); rL2 = rB_
    rGCB = sb("rGCB", [4, 256]); rT0 = sb("rT0", [4, 256]); rT1 = sb("rT1", [4, 256]); rFK = sb("rFK", [4, 256]); rFQ = sb("rFQ", [4, 256])
    rhiF = sb("rhi", [128, 2, 256], BF16); rloF = sb("rlo", [128, 2, 256], BF16)
    facF = sb("fac", [128, 6, 256], BF16)
    rhi = rhiF[0:4]; rlo = rloF[0:4]; fac = facF[0:4]
    mset(rhiF[:], 0.0, ["rhi"]); mset(rloF[:], 0.0, ["rlo"]); mset(facF[:], 0.0, ["fac"])
    swx = sb("swx", [128, 3, 256])
    qAB = sb("qAB", [128, 2, 256], BF16)
    kTh = sb("kTh", [128, L, 2, 2, 128], BF16)
    kTf = sb("kTf", [128, 256])
    Vp = sb("Vp", [128, L, 2, 2, 2, 192], BF16)
    vf = sb("vf", [128, 2, 128])
    PT = [sb("PT%d" % i, [128, 4, 128], BF16) for i in range(2)]
    rec = sb("rec", [128, 4, 128])
    kout = sb("kout", [128, 128])
    tokS = sb("tokS", [128, 2, 388])
    vmg = sb("vmg", [128, 256]); xc = sb("xc", [128, 256]); sqv = vmg; st4 = sb("st4", [128, 8])
    vpad = sb("vpad", [128, 2, 4, 128], BF16); tmpb = sb("tmpb", [128, 128])
    NSLOT = 3; SLOT = 4096
    NSLAB = 28
    wscr = nc.dram_tensor("wscr", [L * NSLAB, 128, SLOT], BF16, kind="Internal").ap()
    ring = [sb("ring%d" % i, [128, SLOT], BF16) for i in range(NSLOT)]

    mset(hist[:], 0.0, ["hist"]); mset(S[:], 0.0, ["S"]); mset(Sb[:], 0.0, ["Sb"])
    mset(Vp[:], 0.0, ["Vp"]); mset(vpad[:], 0.0, ["vpad"]); mset(kTh[:], 0.0, ["kTh"])
    mset(tmpp[:], 0.0, ["tmpp"]); mset(vnew[:], 0.0, ["vnew"])

    SLAB_IDX = {}
    for g in range(1):
        SLAB_IDX[("inT", 0)] = 0
    for g in range(6):
        SLAB_IDX[("inF", g)] = 1 + g
    for g in range(2):
        SLAB_IDX[("out", g)] = 7 + g
    for g in range(11):
        SLAB_IDX[("gu", g)] = 9 + g
    for g in range(8):
        SLAB_IDX[("down", g)] = 20 + g

    def slab_list():
        ntile = SEQ // 128
        order = [(0, 0)] + [x for t in range(ntile) for x in ((t + 1, 0), (t, 1))] + [(ntile, 1)]
        d1 = lambda l: [("inT", l, 0)] + [("inF", l, g) for g in range(6)]
        d2 = lambda l: [("out", l, g) for g in range(2)] + [("gu", l, g) for g in range(11)] + [("down", l, g) for g in range(8)]
        lst = []
        prev = None
        for (tile, l) in order:
            lst += d1(l)
            if prev is not None:
                lst += d2(prev)
            prev = l
        lst += d2(prev)
        return lst

    slabs = slab_list()
    seen_slabs = set()
    issued = [0]
    used = [0]

    def issue_slab(i):
        kind, l, g = slabs[i]
        slot = i % NSLOT
        R = ring[slot]
        key = ("ring", slot)
        sid = l * NSLAB + SLAB_IDX[(kind, g)]
        first_use = sid not in seen_slabs
        seen_slabs.add(sid)
        if not first_use:
            dma("sp", R[:, :], wscr[sid], [("wscr", sid)], [("ring", slot, 0)], key)
            return
        if kind == "inF":
            n = min(4, 21 - 4 * g)
            dst = R[:, 0:8 * n * 128].rearrange("p (k c) -> p k c", k=8)
            dma("pool", dst, w_inF[l, :, :, g * 512:g * 512 + n * 128], [], [("ring", slot, 0)], key)
        elif kind == "inT":
            dst = R[:, 0:8 * 392].rearrange("p (k c) -> p k c", k=8)
            dma("pool", dst, w_inT[l], [], [("ring", slot, 0)], key)
        elif kind == "out":
            dst = R[:, 0:4096].rearrange("p (k c) -> p k c", k=8)
            dma("pool", dst, w_outL[l, :, :, g * 512:(g + 1) * 512], [], [("ring", slot, 0)], key)
        elif kind == "gu":
            dst = R[:, 0:2048].rearrange("p (k c) -> p k c", k=8)
            dma("pool", dst, w_gateL[l, :, :, g * 256:(g + 1) * 256], [], [("ring", slot, 0)], key)
            dst2 = R[:, 2048:4096].rearrange("p (k c) -> p k c", k=8)
            dma("pool", dst2, w_upL[l, :, :, g * 256:(g + 1) * 256], [], [("ring", slot, 1)], key)
        else:
            dst = R[:, 0:NFF * 128].rearrange("p (k c) -> p k c", k=NFF)
            dma("pool", dst, w_downL[l, :, :, g * 128:(g + 1) * 128], [], [("ring", slot, 0)], key)
        dma("sp", wscr[sid], R[:, :], [("ring", slot)], [("wscr", sid)], ("wscr_st", slot))

    def next_slab(kind, l, g):
        i = used[0]
        assert slabs[i] == (kind, l, g), (slabs[i], kind, l, g)
        while issued[0] < min(len(slabs), i + NSLOT):
            issue_slab(issued[0]); issued[0] += 1
        used[0] += 1
        slot = i % NSLOT
        return ring[slot], ("ring", slot)

    def rmsnorm(l, gcols, N, xT, xi):
        bi = pb(); bk, br = B(bi)
        for kc in range(8):
            if kc % 2 == 0:
                act(hT[:, kc, 0:N], xT[:, kc, 0:N], AF.Square, [("xT", xi, kc)], [("hT", kc)])
            else:
                tt(hT[:, kc, 0:N], xT[:, kc, 0:N], xT[:, kc, 0:N], ALU.mult, [("xT", xi, kc)], [("hT", kc)], eng="pool")
        for kc in range(8):
            mm(bk[:, 0:N], onesD[:], hT[:, kc, 0:N], kc == 0, [("hT", kc), "onesD"], [br])
        act(lnv[:, 0:N], bk[:, 0:N], AF.Ln, [br, SETUP], ["lnv"], bias=cv[:, 0:1])
        act(rstd[:, 0:N], lnv[:, 0:N], AF.Exp, ["lnv"], ["rstd"], scale=-0.5)
        for kc in range(8):
            stt(hT[:, kc, 0:N], xT[:, kc, 0:N], gcols[:, l * 8 + kc:l * 8 + kc + 1], rstd[:, 0:N], ALU.mult, ALU.mult,
                [("xT", xi, kc), "rstd", SETUP], [("hT", kc)])

    def load_x(mode, st, N, xT, xi):
        for q in range(2 if mode == "p" else 1):
            xt = xtok[q]
            if mode == "p":
                dma("sp", xt[:], xp[q, st * 128:(st + 1) * 128, :], [], [("xtok", 0)], ("xtok", 0))
            else:
                dma("sp", xt[:], xs_.rearrange("s t d -> (s t) d"), [], [("xtok", 0)], ("xtok", 0))
            for half in range(2):
                bi = pb(); bk, br = B(bi)
                for j in range(4):
                    kc = half * 4 + j
                    mm(bk[:, j * 128:(j + 1) * 128], xt[:, kc * 128:(kc + 1) * 128], identF[:], j == 0, [("xtok", 0), SETUP], [br], tr=True)
                dst = xT[:, half * 4:half * 4 + 4, q * 128:(q + 1) * 128]
                src = bk[:].rearrange("p (k c) -> p k c", k=4)
                if half == 0:
                    P.add("act", lambda e, dst=dst, src=src: e.copy(out=dst, in_=src), reads=[br], writes=[("xT", xi, half * 4 + j) for j in range(4)])
                else:
                    cp(dst, src, [br], [("xT", xi, half * 4 + j) for j in range(4)])

    def store_y(mode, st, N, xT, xi):
        for q in range(2 if mode == "p" else 1):
            xt = xtok[q]
            for half in range(2):
                bi = pb(); bk, br = B(bi)
                for j in range(4):
                    kc = half * 4 + j
                    mm(bk[:, j * 128:(j + 1) * 128], xT[:, kc, q * 128:(q + 1) * 128], identF[:], j == 0, [("xT", xi, kc), SETUP], [br], tr=True)
                if half == 0:
                    P.add("act", lambda e, xt=xt, bk=bk: e.copy(out=xt[:, 0:512], in_=bk[:]), reads=[br], writes=[("xtok", 0, 0)])
                else:
                    cp(xt[:, 512:1024], bk[:], [br], [("xtok", 0, 1)])
            if mode == "p":
                dma("sp", yp[q, st * 128:(st + 1) * 128, :], xt[:], [("xtok", 0)], [], ("xtok", 0), store=True)
            else:
                dma("sp", ys.rearrange("s t d -> (s t) d"), xt[:], [("xtok", 0)], [], ("xtok", 0), store=True)

    def stl(mode, st, l, xT, xi, par, preloaded=False):
        prompt = mode == "p"
        N = 256 if prompt else 128
        NT = 2 if prompt else 1
        T = 128 if prompt else 64
        last = prompt and st == SEQ // 128 - 1
        var = 0 if prompt else 1
        cur = st % 2 if prompt else 0
        hsl = 1 - cur
        nch = N // 64
        tokb = []; fq_units = []; alloc = [pb]; rows_part3 = None; factor_unit = None
        zg = zgs[par]; zgk = "zg%d" % par

        def D1_stream():
            nonlocal tokb, fq_units, rows_part3, factor_unit
            dpool[0] = [4, 5]
            if l == 0 and not preloaded:
                load_x(mode, st, N, xT, xi)
            rmsnorm(l, g1, N, xT, xi)
            yield

            R, rk = next_slab("inT", l, 0)
            Rv = R[:, 0:8 * 392].rearrange("p (k c) -> p k c", k=8)
            ba, bra = B(pb()); bb_, brb = B(pb())
            for kc in range(8):
                mm(ba[:, 0:N], Rv[:, kc, 0:128], hT[:, kc, 0:N], kc == 0, [rk, ("hT", kc)], [bra])
            for kc in range(8):
                mm(bb_[:, 0:N], Rv[:, kc, 260:388], hT[:, kc, 0:N], kc == 0, [rk, ("hT", kc)], [brb])
            act(rA[:, 0:N], ba[0:4, 0:N], AF.Exp, [bra, SETUP], ["rA"], bias=dtb_s[:, l:l + 1])
            act(rB_[:, 0:N], bb_[0:4, 0:N], AF.Exp, [brb], ["rB"], scale=-1.0)
            act(rA[:, 0:N], rA[:, 0:N], AF.Ln, ["rA", SETUP], ["rA"], bias=cv[0:4, 2:3])
            act(rL2[:, 0:N], rB_[:, 0:N], AF.Ln, ["rB", SETUP], ["rB", "rL2"], bias=cv[0:4, 2:3])
            tokb = []
            yield
            for q in range(NT):
                bt, brt = B(pb())
                for kc in range(8):
                    mm(bt[:, 0:388], hT[:, kc, q * 128:(q + 1) * 128], Rv[:, kc, 4:392], kc == 0, [rk, ("hT", kc)], [brt])
                P.add("act", lambda e, q=q, bt=bt: e.copy(out=tokS[:, q, :], in_=bt[:, 0:388]), reads=[brt], writes=[("tokS", q)])
                tokb.append((tokS[:, q, :], ("tokS", q)))
            ts(rG[:, 0:N], rA[:, 0:N], nega[:, l:l + 1], ALU.mult, ["rA", "nega"], ["rG", "rA"])
            P.add("dve", lambda e: e.tensor_tensor_scan(out=rGC[:, 0:N], data0=scanm[:, 0:N], data1=rG[:, 0:N], initial=0.0,
                                                        op0=ALU.mult, op1=ALU.add), reads=["rG", SETUP], writes=["rGC"])
            tt(rGCB[:, 0:N], rGC[:, 0:N], rL2[:, 0:N], ALU.subtract, ["rGC", "rL2"], ["rGCB"])
            for i, (src, nm) in enumerate(((rGC, "rGC"), (rGCB, "rGCB"))):
                cp(rhi[:, i, 0:N], src[:, 0:N], [nm], [("rhi", i)])
                tt(rlo[:, i, 0:N], src[:, 0:N], rhi[:, i, 0:N], ALU.subtract, [nm, ("rhi", i)], [("rlo", i)])
            gc3 = rGC[:, 0:N].rearrange("p (c t) -> p c t", t=64)

            def rows_part2():
                act(rFQ[:, 0:N], bq[0:4, 0:N], AF.Ln, [brq, SETUP], ["rFQ"], bias=cv[0:4, 0:1])
                act(rFK[:, 0:N], bkk[0:4, 0:N], AF.Ln, [brk, SETUP], ["rFK"], bias=cv[0:4, 0:1])
                act(rFQ[:, 0:N], rFQ[:, 0:N], AF.Exp, ["rFQ", SETUP], ["rFQ"], scale=-0.5, bias=cv[0:4, 1:2])
                act(rFK[:, 0:N], rFK[:, 0:N], AF.Exp, ["rFK"], ["rFK"], scale=-0.5)
                act(rT0[:, 0:N], rGC[:, 0:N], AF.Exp, ["rGC"], ["rT0"])
                act(rT1[:, 0:N], rGCB[:, 0:N], AF.Exp, ["rGCB"], ["rT1"])
                act(fac[:, 5, 0:N], rL2[:, 0:N], AF.Exp, ["rL2"], [("fac", 5)], scale=-1.0)
                cp(fac[:, 0, 0:N], rFQ[:, 0:N], ["rFQ"], [("fac", 0)])
                tt(fac[:, 1, 0:N], rFQ[:, 0:N], rT0[:, 0:N], ALU.mult, ["rFQ", "rT0"], [("fac", 1)])
                cp(fac[:, 2, 0:N], rFK[:, 0:N], ["rFK"], [("fac", 2)])
                stt(fac[:, 3, 0:N], rT1[:, 0:N], -1.0, rFK[:, 0:N], ALU.mult, ALU.mult, ["rT1", "rFK"], [("fac", 3)])
                tt(rT0[:, 0:N].rearrange("p (c t) -> p c t", t=64), gc3[:, :, 63:64].to_broadcast([4, nch, 64]), gc3, ALU.subtract,
                   ["rGC", "rT0"], ["rT0"])
                act(rT0[:, 0:N], rT0[:, 0:N], AF.Exp, ["rT0"], ["rT0"])
                tt(fac[:, 4, 0:N], rT0[:, 0:N], rFK[:, 0:N], ALU.mult, ["rT0", "rFK"], [("fac", 4)])
                act(rT1[:, 0:nch], gc3[:, :, 63], AF.Exp, ["rGC", "rT1"], ["rT1"])
                egl_b = sb_egl
                cp(egl_b[:, 0, 0:nch], rT1[:, 0:nch], ["rT1"], ["eglb"])
                tt(egl_b[:, 1, 0:nch], rT1[:, 0:nch], egl_b[:, 0, 0:nch], ALU.subtract, ["rT1", "eglb"], ["eglb2"])

            def rows_part3():
                egl_b = sb_egl
                for h in range(4):
                    bk, br = B(pbA())
                    mm(bk[:, 0:nch], selB[:, h * 128:(h + 1) * 128], sb_eglF[:, 0, 0:nch], True, ["selB", "eglb"], [br])
                    mm(bk[:, 0:nch], selB[:, h * 128:(h + 1) * 128], sb_eglF[:, 1, 0:nch], False, ["selB", "eglb2"], [br])
                    cp(eglbc[:, h, 0:nch], bk[:, 0:nch], [br], [("eglbc", h)])
                for q in range(NT):
                    bk, br = B(pbA())
                    mm(bk[:, 0:4], rGC[:, q * 128:(q + 1) * 128], identF[0:4, 0:4], True, ["rGC", SETUP], [br], tr=True)
                    mm(bk[:, 4:8], rGCB[:, q * 128:(q + 1) * 128], identF[0:4, 0:4], False, ["rGCB", SETUP], [br], tr=True)
                    ts(gcT[:, q, 0:4], bk[:, 0:4], -1.0, ALU.mult, [br], [("gcT", q)])
                    cp(gcT[:, q, 4:8], bk[:, 4:8], [br], [("gcT", q)])

            targets = ((qh, "qh", 0, 0), (qg, "qg", 0, 1), (kh, "kh", 4, 2), (kn, "kn", 4, 3), (kdT, "kdT", 4, 4))
            fq_units = [(dst, nm, c0, fi, h) for (dst, nm, c0, fi) in targets for h in range(4)]

            def factor_unit(u):
                dst, nm, c0, fi, h = u
                bk, br = B(alloc[0]())
                mm(bk[:, 0:N], selB[:, h * 128:(h + 1) * 128], facF[:, fi, 0:N], True, ["selB", ("fac", fi)], [br])
                tt(dst[:, h, 0:N], sl[:, c0 + h, 0:N], bk[:, 0:N], ALU.mult, [("sl", c0 + h), br], [(nm, h)])

            bq = banks[3][:, 0:256]; brq = ("ps", 3); bkk = banks[3][:, 256:512]; brk = ("ps", 3)
            pending = []
            deferred = []
            for g in range(6):
                R, rk = next_slab("inF", l, g)
                n = min(4, 21 - 4 * g)
                Rv = R[:, 0:8 * n * 128].rearrange("p (k c) -> p k c", k=8)
                for j in range(n):
                    c = 4 * g + j
                    bi = pb(); bk, br = B(bi)
                    for kc in range(8):
                        mm(bk[:, 0:N], Rv[:, kc, j * 128:(j + 1) * 128], hT[:, kc, 0:N], kc == 0, [rk, ("hT", kc)], [br])
                    for f_ in pending:
                        f_()
                    pending = []
                    if c < 12:
                        rb = raw[c % 2]; ca = cacc[c % 2]; rkx = ("raw", c % 2); ckx = ("cacc", c % 2)
                        cp(rb[:, :, 0:3], hist[:, l, c, :, :], [("hist", l, c)], [rkx], eng="pool")
                        act(rb[:, :, 3:3 + T], bk[:, 0:N].rearrange("p (s t) -> p s t", s=2), AF.Copy, [br], [rkx])
                        cp(hist[:, l, c, :, :], rb[:, :, T:T + 3], [rkx], [("hist", l, c)], eng="pool")
                    for f_ in deferred:
                        f_()
                    deferred = []
                    if c < 12:
                        def epi(c=c, rb=rb, ca=ca, rkx=rkx, ckx=ckx):
                            w0 = l * 48 + c * 4
                            ts(ca[:, :, 0:T], rb[:, :, 0:T], cw[:, w0:w0 + 1], ALU.mult, [rkx, SETUP], [ckx])
                            for tp in range(1, 4):
                                stt(ca[:, :, 0:T], rb[:, :, tp:tp + T], cw[:, w0 + tp:w0 + tp + 1], ca[:, :, 0:T], ALU.mult, ALU.add,
                                    [rkx, ckx, SETUP], [ckx])
                            act(sl[:, c, 0:N].rearrange("p (s t) -> p s t", s=2), ca[:, :, 0:T], AF.Silu, [ckx], [("sl", c)])
                            if c < 8:
                                s = sqb[c % 2]
                                tt(s[:, 0:N], sl[:, c, 0:N], sl[:, c, 0:N], ALU.mult, [("sl", c)], [("sqb", c % 2)], eng="pool")
                                h = c % 4
                                tgt, tr_ = (bq, brq) if c < 4 else (bkk, brk)

                                def ssq_mm(tgt=tgt, tr_=tr_, h=h, s=s, c=c):
                                    mm(tgt[:, 0:N], indH[:, h * 128:(h + 1) * 128], s[:, 0:N], h == 0, [("sqb", c % 2), "indH"], [tr_])
                                pending.append(ssq_mm)
                        deferred.append(epi)
                    elif c < 16:
                        act(zg[:, c - 12, 0:N], bk[:, 0:N], AF.Silu, [br], [(zgk, c - 12)])
                    elif c < 18:
                        act(uT[:, c - 16, 0:N], bk[:, 0:N], AF.Gelu_apprx_tanh, [br], [("uT", c - 16)])
                    else:
                        act(swx[:, c - 18, 0:N], bk[:, 0:N], AF.Copy, [br], [("swx", c - 18)])
                    if c == 10:
                        rows_part2()
                    yield
            for f_ in pending:
                f_()
            fq_units += [(vbT, "vbT", 8, 5, h) for h in range(4)]
            yield


        def fac_stream():
            alloc[0] = pbA
            rows_part3()
            yield
            while fq_units:
                factor_unit(fq_units.pop(0))
                if len(fq_units) % 2 == 0:
                    yield

        def gdn_pre():
            for q in range(NT):
                qc = slice(q * 128, (q + 1) * 128)
                bk, br = B(pbA())
                bkb = bk[:].bitcast(BF16)
                for h in range(4):
                    mm(bkb[:, h * 128:(h + 1) * 128], kdT[:, h, qc], identB[:], h == 0, [("kdT", h), "identB"], [br], tr=True)
                cp(kdt[:, q, :, :], bkb[:, 0:512].rearrange("p (h d) -> p h d", h=4), [br], [("kdt", q)])
                bk, br = B(pbA())
                bkb = bk[:].bitcast(BF16)
                for h in range(4):
                    mm(bkb[:, h * 128:(h + 1) * 128], vbT[:, h, qc], identB[:], h == 0, [("vbT", h), "identB"], [br], tr=True)
                P.add("act", lambda e, q=q, bkb=bkb: e.copy(out=bvt[:, q, :, :], in_=bkb[:, 0:512].rearrange("p (h d) -> p h d", h=4)),
                      reads=[br], writes=[("bvt", q)])
                yield
                for hp in range(2):
                    bG, brG = B(pbA())
                    bZ = [B(pbA()) for _ in range(2)]
                    for hh in range(2):
                        h = 2 * hp + hh
                        mm(bG[:, hh * 256:hh * 256 + 128], kh[:, h, qc], qh[:, h, qc], hh == 0, [("kh", h), ("qh", h)], [brG])
                        mm(bG[:, hh * 256 + 128:hh * 256 + 256], kh[:, h, qc], kh[:, h, qc], False, [("kh", h)], [brG])
                        z, zr = bZ[hh]
                        mm(z[:, 0:384], identB[:], mb3[:], True, ["identB", "mb3"], [zr])
                        sh = selB[:, h * 128:(h + 1) * 128]
                        for (o, i) in ((0, 0), (128, 1), (256, 0)):
                            mm(z[:, o:o + 128], sh, rhiF[:, i, qc], False, ["selB", ("rhi", i)], [zr])
                            mm(z[:, o:o + 128], sh, rloF[:, i, qc], False, ["selB", ("rlo", i)], [zr])
                        act(E12[:, hh, :], z[:, 0:256], AF.Exp, [zr, ("gcT", q)], [("E12", hh)], bias=gcT[:, q, h:h + 1])
                        act(E3[:, hh, :], z[:, 256:384], AF.Exp, [zr, ("gcT", q)], [("E3", hh)], scale=-1.0, bias=gcT[:, q, 4 + h:5 + h])
                    tt(ATP[:, q, 2 * hp:2 * hp + 2, :], bG[:].rearrange("p (h c) -> p h c", h=2), E12[:, :, :], ALU.mult,
                       [brG, "E12"], [("ATP", q, hp)])
                    tt(Qm[q][0][:, 2 * hp:2 * hp + 2, :], bG[:].rearrange("p (h c) -> p h c", h=2)[:, :, 128:256], E3[:, :, :], ALU.mult,
                       [brG, "E3"], [("Qm", q, 0, hp)])
                    yield
            for m in range(6):
                for q in range(NT):
                    bR, brR = banks[6 + q], ("ps", 6 + q)
                    Qc = Qm[q][m % 2]; Qk = ("Qm", q, m % 2)
                    if m == 0:
                        Pc = ATP[:, q, :, 128:256]; Pk = ("ATP", q)
                    else:
                        Pc = Pm[q][m % 2][:, :, :]; Pk = ("Pm", q, m % 2)
                    for h in range(4):
                        if m == 0:
                            mm(bR[:, h * 128:(h + 1) * 128], identB[:], identB[:], h == 0, ["identB"], [brR])
                            mm(bR[:, h * 128:(h + 1) * 128], Qc[:, h, :], negIB[:], False, [Qk, "negIB"], [brR])
                        else:
                            mm(bR[:, h * 128:(h + 1) * 128], Qc[:, h, :], Rb[q][:, h, :], False, [Qk, ("Rb", q)], [brR])
                    if m < 5:
                        bP, brP = B(pbA()); bQ, brQ = B(pbA())
                        for h in range(4):
                            if m < 4:
                                mm(bP[:, h * 128:(h + 1) * 128], Qc[:, h, :], Pc[:, h, :], h == 0, [Qk, Pk], [brP])
                            mm(bQ[:, h * 128:(h + 1) * 128], Pc[:, h, :], Qc[:, h, :], h == 0, [Qk, Pk], [brQ])
                        nx = (m + 1) % 2
                        if m < 4:
                            P.add("act", lambda e, nx=nx, bP=bP, q=q: e.copy(out=Pm[q][nx][:, :, :], in_=bP[:].rearrange("p (h c) -> p h c", h=4)),
                                  reads=[brP], writes=[("Pm", q, nx)])
                        cp(Qm[q][nx][:, :, :], bQ[:].rearrange("p (h c) -> p h c", h=4), [brQ], [("Qm", q, nx)])
                        P.add("act", lambda e, bR=bR, q=q: e.copy(out=Rb[q][:, :, :], in_=bR[:].rearrange("p (h c) -> p h c", h=4)),
                              reads=[brR], writes=[("Rb", q)])
                    else:
                        cp(TT[:, q, :, :], bR[:].rearrange("p (h c) -> p h c", h=4), [brR], [("TT", q)])
                    yield

        def gdn_chain():
            if prompt:
                steps = [(s, s, hf) for hf in range(2) for s in range(2)]
            else:
                steps = [(s, 0, s) for s in range(2)]
            for (s, q, hf) in steps:
                rows = slice(hf * 64, (hf + 1) * 64)
                c0 = q * 128 + hf * 64
                ch = c0 // 64
                qc = slice(q * 128, (q + 1) * 128)
                Sk = ("S", l, s); Sbk = ("Sb", l, s)
                bKS, brKS = B(pbA()); bO, brO = B(pbA())
                for h in range(4):
                    mm(bKS[:, h * 128:(h + 1) * 128], kn[:, h, qc], Sb[:, l, s, h, :], h == 0, [("kn", h), Sbk], [brKS])
                for h in range(4):
                    mm(bO[:, h * 64:(h + 1) * 64], Sb[:, l, s, h, :], qg[:, h, c0:c0 + 64], h == 0, [("qg", h), Sbk], [brO])
                tt(tmpp[rows, :, :], bKS[rows, :].rearrange("p (h d) -> p h d", h=4), bvt[rows, q, :, :], ALU.add,
                   [brKS, ("bvt", q)], [("tmpp", hf)])
                yield
                bV, brV = B(pbA())
                for h in range(4):
                    mm(bV[:, h * 128:(h + 1) * 128], TT[rows, q, h, :], tmpp[rows, h, :], h == 0, [("TT", q), ("tmpp", hf)], [brV])
                P.add("act", lambda e, rows=rows, q=q, bV=bV: e.copy(out=vnew[rows, q, :, :], in_=bV[rows, :].rearrange("p (h d) -> p h d", h=4)),
                      reads=[brV], writes=[("vnew", q, hf)])
                yield
                for h in range(4):
                    mm(bO[:, h * 64:(h + 1) * 64], vnew[rows, q, h, :], ATP[rows, q, h, hf * 64:hf * 64 + 64], False,
                       [("vnew", q, hf), ("ATP", q)], [brO])
                P.add("act", lambda e, c0=c0, bO=bO: e.copy(out=oT[:, :, c0:c0 + 64], in_=bO[:, 0:256].rearrange("p (h c) -> p h c", h=4)),
                      reads=[brO], writes=[("oT", ch)])
                bD, brD = B(pbA())
                for h in range(4):
                    mm(bD[:, h * 128:(h + 1) * 128], kdt[rows, q, h, :], vnew[rows, q, h, :], h == 0, [("kdt", q), ("vnew", q, hf)], [brD])
                for h in range(4):
                    stt(S[:, l, s, h, :], S[:, l, s, h, :], eglbc[:, h, ch:ch + 1], bD[:, h * 128:(h + 1) * 128], ALU.mult, ALU.add,
                        [Sk, ("eglbc", h), brD], [Sk])
                P.add("act", lambda e, s=s: e.copy(out=Sb[:, l, s, :, :], in_=S[:, l, s, :, :]), reads=[Sk], writes=[Sbk])
                yield
            for h in range(4):
                s_ = sqbW
                act(s_[:, 0:N], oT[:, h, 0:N], AF.Square, ["oT"], ["sqbW"])
                yield
                bk, br = B(pbA())
                mm(bk[:, 0:N], onesDV[:], s_[:, 0:N], True, ["sqbW", "onesDV"], [br])
                act(lnvW[:, 0:N], bk[:, 0:N], AF.Ln, [br, SETUP], ["lnvW"], bias=cv[:, 0:1])
                act(rstdW[:, 0:N], lnvW[:, 0:N], AF.Exp, ["lnvW"], ["rstdW"], scale=-0.5)
                stt(lnvW[:, 0:N], oT[:, h, 0:N], gng_s[:, l:l + 1], rstdW[:, 0:N], ALU.mult, ALU.mult, ["oT", "rstdW", "lnvW", SETUP], ["lnvW"])
                tt(OT[:, h, 0:N], lnvW[:, 0:N], zg[:, h, 0:N], ALU.mult, ["lnvW", (zgk, h)], [("OT", h)])
                yield

        def gmlp():
            for q in range(NT):
                bt, brt = tokb[q]
                act(vmg[:], bt[:, 0:256], AF.Gelu_apprx_tanh, [brt], ["vmg"])
                if prompt:
                    P.add("act", lambda e, q=q, bt=bt: e.copy(out=Vp[:, l, q, cur, :, 64:128], in_=bt[:, 260:388].rearrange("p (g d) -> p g d", g=2)),
                          reads=[brt], writes=[("Vp", l, q, cur)])
                else:
                    for s in range(2):
                        rws = slice(s * 64, (s + 1) * 64)
                        P.add("act", lambda e, s=s, rws=rws, bt=bt: e.copy(out=Vp[rws, l, s, 1, :, 64:128], in_=bt[rws, 260:388].rearrange("p (g d) -> p g d", g=2)),
                              reads=[brt], writes=[("Vp", l, s, 1)])
                if last or not prompt:
                    cp(vf[:, q, :], bt[:, 260:388], [brt], [("vf", q)])
                v3 = vmg[:].rearrange("p (h d) -> p h d", h=4)
                P.add("dve", lambda e, v3=v3: e.tensor_reduce(out=st4[:, 0:4], in_=v3, axis=AX.X, op=ALU.add), reads=["vmg"], writes=[("st4", 0)])
                ts(st4[:, 0:4], st4[:, 0:4], -1.0 / 64, ALU.mult, [("st4", 0)], [("st4", 0)])
                x3 = xc[:].rearrange("p (h d) -> p h d", h=4)
                tt(x3, v3, st4[:, 0:4].unsqueeze(2).to_broadcast([128, 4, 64]), ALU.add, ["vmg", ("st4", 0)], ["xc"])
                tt(sqv[:], xc[:], xc[:], ALU.mult, ["xc", "vmg"], ["sqv", "vmg"])
                P.add("dve", lambda e: e.tensor_reduce(out=st4[:, 4:8], in_=sqv[:].rearrange("p (h d) -> p h d", h=4), axis=AX.X, op=ALU.add),
                      reads=["sqv"], writes=[("st4", 1)])
                yield
                act(st4[:, 4:8], st4[:, 4:8], AF.Ln, [("st4", 1), SETUP], [("st4", 1)], scale=1.0 / 64, bias=cv[:, 0:1])
                act(st4[:, 4:8], st4[:, 4:8], AF.Exp, [("st4", 1)], [("st4", 1)], scale=-0.5)
                tt(x3, x3, st4[:, 4:8].unsqueeze(2).to_broadcast([128, 4, 64]), ALU.mult, ["xc", ("st4", 1)], ["xc"])
                tt(xc[:], xc[:], lng_bc[:, l, :], ALU.mult, ["xc", SETUP], ["xc"])
                tt(xc[:], xc[:], lnb_bc[:, l, :], ALU.add, ["xc", SETUP], ["xc"])
                if not prompt:
                    for s in range(2):
                        dma("sp", o_ms[l, s], xc[s * 64:(s + 1) * 64, :], ["xc"], [], "o_ms", store=True)
                vp4 = vpad[:, q, :, :].rearrange("p (a b) c -> p a b c", b=2)
                x4 = xc[:].rearrange("p (a b d) -> p a b d", a=2, b=2)
                cp(vp4[:, :, 0, 0:64], x4[:, :, 0, :], ["xc"], [("vpad", q, 0)])
                cp(vp4[:, :, 1, 64:128], x4[:, :, 1, :], ["xc"], [("vpad", q, 1)])
                yield
                for pr in range(2):
                    bk, br = B(pbB())
                    mm(bk[:, 0:128], vpad[:, q, 2 * pr, :], wsT[:, l, var, 2 * pr, :], True, [("vpad", q), ("wsT", l, var, 2 * pr)], [br])
                    mm(bk[:, 0:128], vpad[:, q, 2 * pr + 1, :], wsT[:, l, var, 2 * pr + 1, :], False, [("vpad", q), ("wsT", l, var, 2 * pr + 1)], [br])
                    tt(tmpb[:], bk[:, 0:128], bsbc[:, l, var, pr, :], ALU.add, [br, SETUP, "tmpb"], ["tmpb"])
                    tt(OT[:, 4 + pr, q * 128:(q + 1) * 128], uT[:, pr, q * 128:(q + 1) * 128], tmpb[:], ALU.mult, [("uT", pr), "tmpb"], [("OT", 4 + pr, q)])
                yield

        def swa():
            for i in range(3):
                s_ = sqbW
                act(s_[:, 0:N], swx[:, i, 0:N], AF.Square, [("swx", i)], ["sqbW"])
                bk, br = B(pbB())
                mm(bk[:, 0:N], ones64[:], s_[:, 0:N], True, ["sqbW", "ones64"], [br])
                act(lnvW[:, 0:N], bk[:, 0:N], AF.Ln, [br, SETUP], ["lnvW"], bias=cv[:, 0:1])
                act(rstdW[:, 0:N], lnvW[:, 0:N], AF.Exp, ["lnvW"], ["rstdW"], scale=-0.5)
                if i < 2:
                    stt(qAB[:, i, 0:N], swx[:, i, 0:N], qng_s[:, l:l + 1], rstdW[:, 0:N], ALU.mult, ALU.mult, [("swx", i), "rstdW", SETUP], [("qAB", i)])
                else:
                    stt(kTf[:, 0:N], swx[:, 2, 0:N], kng_s[:, l:l + 1], rstdW[:, 0:N], ALU.mult, ALU.mult, [("swx", 2), "rstdW", SETUP], ["kTf"])
                    if prompt:
                        for s in range(2):
                            cp(kTh[:, l, s, cur, :], kTf[:, s * 128:(s + 1) * 128], ["kTf"], [("kTh", l, s, cur)])
                    else:
                        cp(kTh[:, l, 0, 1, :], kTf[:, 0:128], ["kTf"], [("kTh", l, 0, 1)])
                yield
            if not prompt:
                for s in range(2):
                    dma("sp", kout[:], cK[l, s], [], ["kout"], "kout")
                    bk, br = B(pbB())
                    mm(bk[:, 0:128], kout[:], identF[:], True, ["kout", SETUP], [br], tr=True)
                    cp(kTh[:, l, s, 0, :], bk[:, 0:128], [br], [("kTh", l, s, 0)])
                    dma("pool", Vp[:, l, s, 0, :, 64:128], cV[l, s].rearrange("t (g d) -> t g d", g=2), [], [("Vp", l, s, 0)], ("vcache", s))
                    dma("sp", o_ks[l, s, 0:64, :], cK[l, s, 64:128, :], [], [], "o_cache", store=True)
                    dma("sp", o_vs[l, s, 0:64, :], cV[l, s, 64:128, :], [], [], "o_cache", store=True)
                    yield
        def swa_att():
            for s in range(2):
                bO, brO = banks[6], ("ps", 6)
                bS, brS = banks[7], ("ps", 7)
                if prompt:
                    qcols = slice(s * 128, (s + 1) * 128); NQ = 128
                    ktiles = ([("h", hsl)] if st > 0 else []) + [("c", cur)]
                else:
                    qcols = slice(s * 64, (s + 1) * 64); NQ = 64
                    ktiles = [("h", 0), ("c", 1)]
                first = True
                for ki, (kk, slot) in enumerate(ktiles):
                    pt = PT[ki % 2]; ptk = ("PT", ki % 2)
                    if prompt or kk == "h":
                        krows = slice(0, 128); kT_ = kTh[:, l, s, slot, :]; kkey = ("kTh", l, s, slot); vkey = ("Vp", l, s, slot)
                        vs_ = s; vslot = slot
                    else:
                        krows = slice(s * 64, (s + 1) * 64); kT_ = kTh[:, l, 0, 1, :]; kkey = ("kTh", l, 0, 1); vkey = ("Vp", l, s, 1)
                        vs_ = s; vslot = 1
                    for g in range(2):
                        bk, br = B(pbB())
                        gr = slice(g * 64, (g + 1) * 64)
                        for hh in range(2):
                            h = 2 * g + hh
                            mm(bk[:, hh * NQ:(hh + 1) * NQ], kT_[gr, :], qAB[gr, hh, qcols], hh == 0, [kkey, ("qAB", hh)], [br])
                        act(pt[:, 2 * g:2 * g + 2, 0:NQ], bk[:, 0:2 * NQ].rearrange("p (h c) -> p h c", h=2), AF.Exp, [br], [(ptk[0], ptk[1], g)], scale=0.125)
                    if prompt:
                        if kk == "h":
                            mset(pt[0:64, :, 64:128], 0.0, [ptk])
                        else:
                            mset(pt[64:128, :, 0:64], 0.0, [ptk])
                    yield
                    for h in range(4):
                        mm(bS[:, h * NQ:(h + 1) * NQ], onesB[krows, :], pt[krows, h, 0:NQ], first and h == 0, [ptk, "onesB"], [brS])
                    for h in range(4):
                        g = h // 2
                        lo = 64 if h % 2 == 0 else 0
                        mm(bO[:, g * NQ:(g + 1) * NQ], Vp[krows, l, vs_, vslot, g, lo:lo + 128], pt[krows, h, 0:NQ], first and h == 0, [ptk, vkey], [brO])
                    first = False
                    yield
                for h in range(4):
                    ts(rec[:, h, 0:NQ], bS[:, h * NQ:(h + 1) * NQ], esink[:, l * 4 + h:l * 4 + h + 1], ALU.add, [brS, "esinkx"], [("rec", h)])
                P.add("dve", lambda e, NQ=NQ: e.reciprocal(out=rec[:, :, 0:NQ], in_=rec[:, :, 0:NQ]), reads=["rec"], writes=["rec"])
                for g in range(2):
                    for hh in range(2):
                        rws = slice(hh * 64, (hh + 1) * 64)
                        tt(OT[rws, 6 + g, qcols], bO[rws, g * NQ:(g + 1) * NQ], rec[rws, 2 * g + hh, 0:NQ], ALU.mult, [brO, "rec"], [("OT", 6 + g, s, hh)])
                yield

        def outputs():
            if last or not prompt:
                for q in range(NT):
                    bk, br = B(pbA())
                    mm(bk[:, 0:128], kTf[:, q * 128:(q + 1) * 128], identF[:], True, ["kTf", SETUP], [br], tr=True)
                    cp(kout[:], bk[:, 0:128], [br, "kout"], ["kout"])
                    if prompt:
                        dma("sp", o_kp[l, q], kout[:], ["kout"], [], "kout", store=True)
                        dma("sp", o_vp[l, q], vf[:, q, :], [("vf", q)], [], ("o_v", q), store=True)
                    else:
                        for s in range(2):
                            dma("sp", o_ks[l, s, 64:128, :], kout[s * 64:(s + 1) * 64, :], ["kout"], [], "kout", store=True)
                            dma("sp", o_vs[l, s, 64:128, :], vf[s * 64:(s + 1) * 64, 0, :], [("vf", 0)], [], ("o_v", 0), store=True)
                for s in range(2):
                    og = o_gp if prompt else o_gs
                    dma("sp", og[l, s].rearrange("h k v -> k h v"), S[:, l, s, :, :], [("S", l, s)], [], "o_g", store=True)
                    oc = o_cp if prompt else o_cs
                    for c in range(12):
                        dma("sp", oc[l, s, :, c * 128:(c + 1) * 128].rearrange("t p -> p t"), hist[:, l, c, s, :], [("hist", l, c)], [], "o_c", store=True, slow=True)

        def Mh_stream():
            yield from fac_stream()
            yield from gdn_pre()
            yield from gmlp()
            yield from swa()

        def Mt_stream():
            yield from gdn_chain()
            yield from swa_att()
            outputs()
            yield

        def D2_stream():
            dpool[0] = [3, 4, 5]
            for g in range(2):
                R, rk = next_slab("out", l, g)
                Rv = R[:, 0:4096].rearrange("p (k c) -> p k c", k=8)
                for j in range(4):
                    c = 4 * g + j
                    bk, br = B(pb())
                    for kc in range(8):
                        mm(bk[:, 0:N], Rv[:, kc, j * 128:(j + 1) * 128], OT[:, kc, 0:N], kc == 0, [rk, ("OT", kc)], [br])
                    tt(xT[:, c, 0:N], xT[:, c, 0:N], bk[:, 0:N], ALU.add, [("xT", xi, c), br], [("xT", xi, c)])
            yield
            rmsnorm(l, g2, N, xT, xi)
            yield
            for g in range(11):
                R, rk = next_slab("gu", l, g)
                Gv = R[:, 0:2048].rearrange("p (k c) -> p k c", k=8)
                Uv = R[:, 2048:4096].rearrange("p (k c) -> p k c", k=8)
                for j in range(2):
                    c = 2 * g + j
                    bGU, brGU = B(pb())
                    bG = bGU[:, 0:256]; bU = bGU[:, 256:512]
                    for kc in range(8):
                        mm(bG[:, 0:N], Gv[:, kc, j * 128:(j + 1) * 128], hT[:, kc, 0:N], kc == 0, [rk, ("hT", kc)], [brGU])
                    for kc in range(8):
                        mm(bU[:, 0:N], Uv[:, kc, j * 128:(j + 1) * 128], hT[:, kc, 0:N], False, [rk, ("hT", kc)], [brGU])
                    sgt = sg[c % 2]
                    act(sgt[:, 0:N], bG[:, 0:N], AF.Tanh, [brGU], [("sg", c % 2)], scale=0.5)
                    stt(sgt[:, 0:N], sgt[:, 0:N], 1.0, bG[:, 0:N], ALU.add, ALU.mult, [("sg", c % 2), brGU], [("sg", c % 2)])
                    tt(actT[:, c, 0:N], sgt[:, 0:N], bU[:, 0:N], ALU.mult, [("sg", c % 2), brGU], [("actT", c)])
                    yield
            for g in range(8):
                R, rk = next_slab("down", l, g)
                Rv = R[:, 0:NFF * 128].rearrange("p (k c) -> p k c", k=NFF)
                c = g
                bk, br = B(pb())
                for kc in range(NFF):
                    mm(bk[:, 0:N], Rv[:, kc, :], actT[:, kc, 0:N], kc == 0, [rk, ("actT", kc)], [br])
                stt(xT[:, c, 0:N], bk[:, 0:N], 0.5, xT[:, c, 0:N], ALU.mult, ALU.add, [("xT", xi, c), br], [("xT", xi, c)])
                yield
            if l == L - 1:
                store_y(mode, st, N, xT, xi)
            yield

        return D1_stream(), Mh_stream(), Mt_stream(), D2_stream()

    sb_eglF = sb("egl_b", [128, 2, 8], BF16)
    sb_egl = sb_eglF[0:4]
    mset(sb_eglF[:], 0.0, ["eglb", "eglb2"])

    NTILE = SEQ // 128
    order = [(0, 0)] + [x for t in range(NTILE) for x in ((t + 1, 0), (t, 1))] + [(NTILE, 1)]

    def load_sample_state(l):
        barrier([("S", l), ("hist", l), ("Sb", l)])
        for s in range(2):
            dma("sp", S[:, l, s, :, :], sG[l, s].rearrange("h k v -> k h v"), [], [("S", l, s)], ("sload", l, s))
            P.add("act", lambda e, l=l, s=s: e.copy(out=Sb[:, l, s, :, :], in_=S[:, l, s, :, :]), reads=[("S", l, s)], writes=[("Sb", l, s)])
            for c in range(12):
                dma("sp", hist[:, l, c, s, :], sC[l, s, :, c * 128:(c + 1) * 128].rearrange("t p -> p t"), [], [("hist", l, c, s)], ("hload", l), slow=True)
        barrier([("hist", l)])

    prev = None
    pre = set()
    for k, (tile, l) in enumerate(order):
        if tile == NTILE:
            if prev is not None:
                interleave([prev[0]])
            load_sample_state(l)
            D1g, Mh, Mt, D2g = stl("s", 0, l, xTs[tile % 3], tile % 3, k % 2, preloaded=(k in pre))
            interleave([D1g])
        else:
            D1g, Mh, Mt, D2g = stl("p", tile, l, xTs[tile % 3], tile % 3, k % 2, preloaded=(k in pre))
            interleave([D1g] + ([prev[0]] if prev is not None else []))
        if k + 1 < len(order) and order[k + 1][1] == 0:
            nt = order[k + 1][0]
            dpool[0] = [3, 4, 5]
            if nt == NTILE:
                load_x("s", 0, 128, xTs[nt % 3], nt % 3)
            else:
                load_x("p", nt, 256, xTs[nt % 3], nt % 3)
            pre.add(k + 1)
        interleave(([prev[1]] if prev is not None else []) + [Mh])
        prev = (Mt, D2g)
    interleave([prev[0]])
    interleave([prev[1]])
    assert used[0] == len(slabs), (used[0], len(slabs))

    P.finalize()
    out_keys = sorted(store_keys, key=str)
    with nc.Block() as block:
        @block.sync
        def _(e):
            P.run_engine("sp", e, sems, dma_sems)
            for k in out_keys:
                e.wait_ge(dma_sems[k], 16 * P.dma_count[k])

        @block.scalar
        def _(e):
            P.run_engine("act", e, sems, dma_sems)

        @block.vector
        def _(e):
            P.run_engine("dve", e, sems, dma_sems)

        @block.gpsimd
        def _(e):
            P.run_engine("pool", e, sems, dma_sems)

        @block.tensor
        def _(e):
            P.run_engine("pe", e, sems, dma_sems)
    es.close()
    return nc, P


_CACHE = {}


def kernel(**inputs):
    if "nc" not in _CACHE:
        _CACHE["nc"] = build()
    nc, P = _CACHE["nc"]
    cst = _consts()
    wl = _layout_weights(inputs)
    f = lambda k: np.asarray(inputs[k], np.float32)
    xpa = f("x_prompt"); xsa = f("x_sample")
    ck = f("cache_swa_k").reshape(L, 16, 128, 128); cvv = f("cache_swa_v").reshape(L, 16, 128, 128)
    sg_ = f("state_gdn"); sc_ = f("state_gdn_conv")
    in_maps = []
    for c in range(NCORE):
        b = slice(2 * c, 2 * c + 2)
        m = {"xp": np.ascontiguousarray(xpa[b]), "xs": np.ascontiguousarray(xsa[b]),
             "cK": np.ascontiguousarray(ck[:, b]), "cV": np.ascontiguousarray(cvv[:, b]),
             "sG": np.ascontiguousarray(sg_[:, b]), "sC": np.ascontiguousarray(sc_[:, b])}
        m.update(wl); m.update(cst)
        in_maps.append(m)
    res = run_bass_kernel_spmd(nc, in_maps, core_ids=list(range(NCORE)))
    r = res.results
    cat = lambda k, ax: np.concatenate([np.asarray(r[c][k], np.float32) for c in range(NCORE)], axis=ax)
    yp = cat("yp", 0); ys = cat("ys", 0)
    kp = cat("o_kp", 1).reshape(L, 16, 128, 2, 64); vp = cat("o_vp", 1).reshape(L, 16, 128, 2, 64)
    gp = cat("o_gp", 1); cpp = cat("o_cp", 1)
    ks = cat("o_ks", 1).reshape(L, 16, 128, 2, 64); vs = cat("o_vs", 1).reshape(L, 16, 128, 2, 64)
    gs = cat("o_gs", 1); cs = cat("o_cs", 1)
    ms = cat("o_ms", 1).reshape(L, 16, TS, 4, 64)
    return (yp, ys, kp, vp, gp, cpp, ks, vs, gs, cs, ms)
```

```python
import numpy as np
from contextlib import ExitStack
import concourse.bass as bass
import concourse.mybir as mybir
from concourse.bass_utils import run_bass_kernel_spmd

F32 = mybir.dt.float32
BF16 = mybir.dt.bfloat16
AF = mybir.ActivationFunctionType
ALU = mybir.AluOpType
AX = mybir.AxisListType
COMPUTE = ("pe", "act", "dve", "pool")
NOPRUNE = False

D = 1024; L = 2; SEQ = 2048; TS = 64; DFF = 2816; NFF = 22
NCORE = 8
BIG = 30000.0


class Op:
    __slots__ = ("eng", "fn", "reads", "writes", "dma", "lidx", "waits", "need_inc",
                 "incval", "dmaval", "clock", "gidx", "cost", "start", "finish", "deps")


class Prog:
    def __init__(self):
        self.ops = []
        self.state = {}
        self.children = {}
        self.dma_count = {}

    def add(self, eng, fn, reads=(), writes=(), dma=None, cost=None):
        op = Op()
        op.eng = eng; op.fn = fn
        op.cost = cost
        op.reads = [tuple(r) if isinstance(r, (tuple, list)) else (r,) for r in reads]
        op.writes = [tuple(w) if isinstance(w, (tuple, list)) else (w,) for w in writes]
        op.dma = dma
        op.gidx = len(self.ops)
        op.need_inc = False
        self.ops.append(op)
        return op

    def _conflicts(self, key):
        out = []
        for n in range(1, len(key) + 1):
            k = key[:n]
            if k in self.state:
                out.append(k)
        for k in self.children.get(key, ()):
            if k != key and k in self.state:
                out.append(k)
        return out

    def _touch(self, key):
        if key not in self.state:
            self.state[key] = [None, []]
            for n in range(1, len(key)):
                self.children.setdefault(key[:n], set()).add(key)

    def _analyze(self):
        for op in self.ops:
            deps = {}

            def adddep(a, kind):
                if a is None or a is op:
                    return
                old = deps.get(a.gidx)
                if old is None or (kind == "RAW" and old[1] != "RAW"):
                    deps[a.gidx] = (a, kind)

            for r in op.reads:
                self._touch(r)
                for k in self._conflicts(r):
                    adddep(self.state[k][0], "RAW")
            for w in op.writes:
                self._touch(w)
                for k in self._conflicts(w):
                    st = self.state[k]
                    adddep(st[0], "WAW")
                    for rd in st[1]:
                        adddep(rd, "WAR")
            for r in op.reads:
                self.state[r][1].append(op)
            for w in op.writes:
                self.state[w] = [op, []]
                for k in list(self.children.get(w, ())):
                    if k in self.state:
                        self.state[k] = [op, []]
            op.waits = list(deps.values())

    @staticmethod
    def _needed(a, b, kind):
        if a.dma is not None or b.dma is not None:
            return True
        if a.eng != b.eng:
            return True
        if a.eng == "pe":
            return False
        return kind == "RAW"

    def _schedule(self):
        import heapq
        ops = self.ops
        n = len(ops)
        DEF = {"pe": 0.14, "act": 0.45, "dve": 0.4, "pool": 0.5, "sp": 0.1}
        succ = [[] for _ in range(n)]
        indeg = [0] * n
        fence = getattr(self, "fence", None)
        last_pre = {}
        prev_post = {}
        for op in ops:
            op.deps = [a for (a, kind) in op.waits]
            if fence is not None:
                if op.gidx < fence:
                    last_pre[op.eng] = op
                else:
                    if op.eng in prev_post:
                        op.deps.append(prev_post[op.eng])
                    else:
                        op.deps.extend(last_pre.values())
                    prev_post[op.eng] = op
                    op.deps = list({id(a): a for a in op.deps}.values())
            indeg[op.gidx] = len(op.deps)
            for a in op.deps:
                succ[a.gidx].append(op)
        engines = sorted(set(o.eng for o in ops))
        free_at = {e: 0.0 for e in engines}
        avail = {e: [] for e in engines}
        ready_t = [0.0] * n
        for op in ops:
            if indeg[op.gidx] == 0:
                heapq.heappush(avail[op.eng], op.gidx)
        done = 0
        HOP = 0.4
        while done < n:
            best = None
            for e in engines:
                h = avail[e]
                if not h:
                    continue
                t = free_at[e]
                cand = None; cand_key = None
                for gi in (h[:256] if len(h) > 256 else h):
                    rt = ready_t[gi]
                    key = (max(rt, t), gi)
                    if cand_key is None or key < cand_key:
                        cand_key = key; cand = gi
                if best is None or cand_key < best[0]:
                    best = (cand_key, e, cand)
            (st_, gi_), e, gi = best
            avail[e].remove(gi); heapq.heapify(avail[e])
            op = ops[gi]
            c = op.cost if op.cost is not None else DEF[op.eng]
            op.start = st_
            if op.dma is not None:
                free_at[e] = st_ + 0.08
                op.finish = st_ + c
            else:
                free_at[e] = st_ + c
                op.finish = st_ + c
            done += 1
            for sck in succ[gi]:
                k = sck.gidx
                lat = op.finish + (HOP if (sck.eng != op.eng or op.dma is not None) else 0.0)
                if lat > ready_t[k]:
                    ready_t[k] = lat
                indeg[k] -= 1
                if indeg[k] == 0:
                    heapq.heappush(avail[sck.eng], k)
        order = sorted(ops, key=lambda o: (o.start, o.gidx))
        self.sim_span = max(o.finish for o in ops)
        self.ops = order

    def finalize(self, reorder=True):
        self._analyze()
        if reorder:
            self._schedule()
        for op in self.ops:
            if op.dma is not None:
                self.dma_count[op.dma] = self.dma_count.get(op.dma, 0) + 1
                op.dmaval = 16 * self.dma_count[op.dma]
        engines = sorted(set(o.eng for o in self.ops))
        cnt = {e: 0 for e in engines}
        for op in self.ops:
            cnt[op.eng] += 1
            op.lidx = cnt[op.eng]
        known = {e: {} for e in engines}
        pos = {}
        for i_, op in enumerate(self.ops):
            pos[op.gidx] = i_
        for op in self.ops:
            kn = known[op.eng]
            need = []
            for (a, kind) in sorted(op.waits, key=lambda t: -pos[t[0].gidx]):
                if not self._needed(a, op, kind):
                    continue
                if a.dma is not None:
                    src = ("dma", a.dma); val = a.dmaval
                else:
                    src = a.eng; val = a.lidx
                if kn.get(src, 0) >= val and not NOPRUNE:
                    continue
                need.append(a)
                a.need_inc = True
                kn[src] = max(kn.get(src, 0), val)
                for s, v in a.clock.items():
                    if kn.get(s, 0) < v:
                        kn[s] = v
            op.waits = need
            op.clock = dict(kn)
        incc = {e: 0 for e in engines}
        for op in self.ops:
            if op.dma is None and op.need_inc:
                incc[op.eng] += 1
                op.incval = incc[op.eng]
        self.streams = {e: [] for e in engines}
        for op in self.ops:
            self.streams[op.eng].append(op)
        self.stats = {e: (cnt[e], incc[e]) for e in engines}

    def run_engine(self, e, engobj, sems, dma_sems):
        for op in self.streams.get(e, []):
            w = {}
            for a in op.waits:
                if a.dma is not None:
                    s = dma_sems[a.dma]; v = a.dmaval
                else:
                    s = sems[a.eng]; v = a.incval
                k = id(s)
                if k not in w or w[k][1] < v:
                    w[k] = (s, v)
            for (s, v) in w.values():
                engobj.wait_ge(s, v)
            ins = op.fn(engobj)
            if op.dma is not None:
                ins.then_inc(dma_sems[op.dma], 16)
            elif op.need_inc:
                ins.then_inc(sems[op.eng], 1)


def _consts():
    c = {}
    c["identF"] = np.eye(128, dtype=np.float32)
    half = (np.arange(128) // 64)
    same = half[:, None] == half[None, :]
    p = np.arange(128)[:, None]; f = np.arange(128)[None, :]
    mbI = np.where(same & (f >= p), 0.0, -BIG)
    mbS = np.where(same & (f > p), 0.0, -BIG)
    mbS2 = np.where(same & (p > f), 0.0, BIG)
    c["mb3"] = np.concatenate([mbI, mbS, mbS2], axis=1).astype(np.float32)
    c["ones64"] = np.where(same, 1.0 / 64, 0.0).astype(np.float32)
    ind = np.zeros((128, 4, 128), np.float32)
    for h in range(4):
        ind[:, h, h] = 1.0
    c["indH"] = ind.reshape(128, 512)
    sel = np.zeros((128, 4, 128), np.float32)
    for h in range(4):
        sel[h, h, :] = 1.0
    c["sel"] = sel.reshape(128, 512)
    m = np.ones((4, 256), np.float32); m[:, ::64] = 0.0
    c["scanmask"] = m
    cv = np.zeros((128, 8), np.float32)
    cv[:, 0] = 1e-6
    cv[:, 1] = np.log(128.0 ** -0.5)
    cv[:, 2] = 1.0
    c["cvals"] = cv
    return c


IN_OFF = dict(qkv=0, z=1536, b=2048, a=2052, u=2056, vm=2312, sq=2568, sk=2824, sv=2952)


def _layout_weights(inp):
    w_in = np.asarray(inp["w_in"], np.float32)
    cols = list(range(0, 2048))
    cols += list(range(IN_OFF["u"], IN_OFF["u"] + 256))
    sq = IN_OFF["sq"]
    cols += list(range(sq, sq + 64)) + list(range(sq + 128, sq + 192))
    cols += list(range(sq + 64, sq + 128)) + list(range(sq + 192, sq + 256))
    cols += list(range(IN_OFF["sk"], IN_OFF["sk"] + 128))
    wF = w_in[:, :, cols].reshape(L, 8, 128, 21 * 128).transpose(0, 2, 1, 3)
    colsT = list(range(IN_OFF["a"], IN_OFF["a"] + 4)) + list(range(IN_OFF["vm"], IN_OFF["vm"] + 256)) \
        + list(range(IN_OFF["b"], IN_OFF["b"] + 4)) + list(range(IN_OFF["sv"], IN_OFF["sv"] + 128))
    wT = w_in[:, :, colsT].reshape(L, 8, 128, 392).transpose(0, 2, 1, 3)
    out = {}
    out["w_inF"] = np.ascontiguousarray(wF)
    out["w_inT"] = np.ascontiguousarray(wT)
    out["w_outL"] = np.ascontiguousarray(np.asarray(inp["w_out"], np.float32).reshape(L, 8, 128, D).transpose(0, 2, 1, 3))
    out["w_gateL"] = np.ascontiguousarray(np.asarray(inp["ffn_w_gate"], np.float32).reshape(L, 8, 128, DFF).transpose(0, 2, 1, 3))
    out["w_upL"] = np.ascontiguousarray(np.asarray(inp["ffn_w_up"], np.float32).reshape(L, 8, 128, DFF).transpose(0, 2, 1, 3))
    out["w_downL"] = np.ascontiguousarray(np.asarray(inp["ffn_w_down"], np.float32).reshape(L, NFF, 128, D).transpose(0, 2, 1, 3))
    f = lambda k: np.asarray(inp[k], np.float32)
    out["g1c"] = np.ascontiguousarray(f("norm1_g").reshape(L, 8, 128).transpose(2, 0, 1).reshape(128, L * 8))
    out["g2c"] = np.ascontiguousarray(f("norm2_g").reshape(L, 8, 128).transpose(2, 0, 1).reshape(128, L * 8))
    out["convc"] = np.ascontiguousarray(f("gdn_conv_w").reshape(L, 4, 12, 128).transpose(3, 0, 2, 1).reshape(128, L * 48))
    out["alog"] = np.ascontiguousarray(f("gdn_a_log").T)
    out["dtb"] = np.ascontiguousarray(f("gdn_dt_bias").T)
    out["gng"] = np.ascontiguousarray(f("gdn_norm_g").T)
    out["lng"] = np.ascontiguousarray(f("mlp_ln_g").reshape(L, 256))
    out["lnb"] = np.ascontiguousarray(f("mlp_ln_b").reshape(L, 256))
    out["ws"] = np.ascontiguousarray(f("mlp_ws"))
    out["bs"] = np.ascontiguousarray(f("mlp_bs"))
    out["qng"] = np.ascontiguousarray(np.tile(f("swa_q_norm_g"), (1, 2)).T)
    out["kng"] = np.ascontiguousarray(np.tile(f("swa_k_norm_g"), (1, 2)).T)
    out["sinks"] = np.ascontiguousarray(f("swa_sinks"))
    return out


def build():
    nc = bass.Bass("TRN2", target_bir_lowering=False)
    P = Prog()
    es = ExitStack()

    def din(name, shape):
        return nc.dram_tensor(name, list(shape), F32, kind="ExternalInput").ap()

    def dout(name, shape):
        return nc.dram_tensor(name, list(shape), F32, kind="ExternalOutput").ap()

    xp = din("xp", [2, SEQ, D]); xs_ = din("xs", [2, TS, D])
    cK = din("cK", [L, 2, 128, 128]); cV = din("cV", [L, 2, 128, 128])
    sG = din("sG", [L, 2, 4, 128, 128]); sC = din("sC", [L, 2, 3, 1536])
    w_inF = din("w_inF", [L, 128, 8, 21 * 128]); w_inT = din("w_inT", [L, 128, 8, 392])
    w_outL = din("w_outL", [L, 128, 8, D]); w_gateL = din("w_gateL", [L, 128, 8, DFF])
    w_upL = din("w_upL", [L, 128, 8, DFF]); w_downL = din("w_downL", [L, 128, NFF, D])
    g1c = din("g1c", [128, L * 8]); g2c = din("g2c", [128, L * 8]); convc = din("convc", [128, L * 48])
    alog = din("alog", [4, L]); dtb = din("dtb", [4, L]); gng = din("gng", [128, L])
    lng = din("lng", [L, 256]); lnb = din("lnb", [L, 256]); ws = din("ws", [L, 4, 128, 128]); bs = din("bs", [L, 4, 128])
    qng = din("qng", [128, L]); kng = din("kng", [128, L]); sinks = din("sinks", [L, 4])
    c_ident = din("identF", [128, 128]); c_mb3 = din("mb3", [128, 384]); c_ones64 = din("ones64", [128, 128])
    c_indH = din("indH", [128, 512]); c_sel = din("sel", [128, 512]); c_scan = din("scanmask", [4, 256]); c_cv = din("cvals", [128, 8])

    yp = dout("yp", [2, SEQ, D]); ys = dout("ys", [2, TS, D])
    o_kp = dout("o_kp", [L, 2, 128, 128]); o_vp = dout("o_vp", [L, 2, 128, 128])
    o_gp = dout("o_gp", [L, 2, 4, 128, 128]); o_cp = dout("o_cp", [L, 2, 3, 1536])
    o_ks = dout("o_ks", [L, 2, 128, 128]); o_vs = dout("o_vs", [L, 2, 128, 128])
    o_gs = dout("o_gs", [L, 2, 4, 128, 128]); o_cs = dout("o_cs", [L, 2, 3, 1536])
    o_ms = dout("o_ms", [L, 2, TS, 256])

    def sb(name, shape, dt=F32):
        return es.enter_context(nc.sbuf_tensor("sb_" + name, list(shape), dt))

    banks = [es.enter_context(nc.psum_tensor("bank%d" % i, [128, 512], F32)) for i in range(8)]
    rr = [0]

    def pb():
        pool = dpool[0]
        rr[0] = (rr[0] + 1) % len(pool)
        return pool[rr[0]]

    def B(i):
        return banks[i], ("ps", i)

    rrA = [0]

    def pbA():
        i = rrA[0]
        rrA[0] = (rrA[0] + 1) % 3
        return i

    pbB = pbA
    dpool = [[3, 4, 5]]

    def interleave(gens):
        gens = list(gens)
        while gens:
            for g_ in list(gens):
                try:
                    next(g_)
                except StopIteration:
                    gens.remove(g_)

    sems = {e: es.enter_context(nc.semaphore("s_" + e)) for e in COMPUTE}
    dma_sems = {}

    def dsem(key):
        if key not in dma_sems:
            dma_sems[key] = es.enter_context(nc.semaphore("d%d" % len(dma_sems)))
        return key

    def fsz(ap):
        n = 1
        for d in ap.shape[1:]:
            n *= int(d)
        return n

    def mm(out, lhsT, rhs, start, r, w, tr=False):
        c = max(64, fsz(rhs)) * 0.00047 + 0.018
        if tr and rhs.dtype == F32:
            c *= 2.0
        P.add("pe", lambda e: e.matmul(out=out, lhsT=lhsT, rhs=rhs, start=start, stop=True, is_transpose=(True if tr else None)), reads=r, writes=w, cost=c)

    def act(out, in_, func, r, w, scale=None, bias=None):
        kw = {}
        if scale is not None:
            kw["scale"] = scale
        if bias is not None:
            kw["bias"] = bias
        P.add("act", lambda e: e.activation(out=out, in_=in_, func=func, **kw), reads=r, writes=w, cost=0.2 + fsz(out) * 0.00085)

    def tt(out, in0, in1, op, r, w, eng="dve"):
        P.add(eng, lambda e: e.tensor_tensor(out=out, in0=in0, in1=in1, op=op), reads=r, writes=w,
              cost=(0.14 + fsz(out) * 0.00125) if eng == "dve" else (0.15 + fsz(out) * 0.0021))

    def stt(out, in0, scalar, in1, op0, op1, r, w):
        P.add("dve", lambda e: e.scalar_tensor_tensor(out=out, in0=in0, scalar=scalar, in1=in1, op0=op0, op1=op1), reads=r, writes=w,
              cost=0.14 + fsz(out) * 0.00125)

    def ts(out, in0, s1, op0, r, w, s2=None, op1=None, eng="dve"):
        if op1 is None:
            P.add(eng, lambda e: e.tensor_scalar(out=out, in0=in0, scalar1=s1, scalar2=None, op0=op0), reads=r, writes=w, cost=0.13 + fsz(out) * 0.0009)
        else:
            P.add(eng, lambda e: e.tensor_scalar(out=out, in0=in0, scalar1=s1, scalar2=s2, op0=op0, op1=op1), reads=r, writes=w, cost=0.13 + fsz(out) * 0.0009)

    def cp(out, in_, r, w, eng="dve"):
        if eng == "act":
            P.add("act", lambda e: e.copy(out=out, in_=in_), reads=r, writes=w, cost=0.2 + fsz(out) * 0.00085)
        else:
            P.add(eng, lambda e: e.tensor_copy(out=out, in_=in_), reads=r, writes=w,
                  cost=(0.1 + fsz(out) * 0.0008) if eng == "dve" else (0.12 + fsz(out) * 0.0016))

    def mset(ap, val, w, eng="pool"):
        P.add(eng, lambda e: e.memset(ap, val), writes=w, cost=0.1 + fsz(ap) * 0.001)

    store_keys = set()

    def dma(eng, out, in_, r, w, key, store=False, slow=False):
        if store:
            store_keys.add(key)
        nbytes = fsz(out) * 128 * (2 if out.dtype == BF16 else 4)
        c = 2.2 + nbytes / 180e3
        if slow:
            P.add(eng, lambda e: e.dma_start(out=out, in_=in_, allow_slow_non_contiguous=True), reads=r, writes=w, dma=dsem(key), cost=c)
        else:
            P.add(eng, lambda e: e.dma_start(out=out, in_=in_), reads=r, writes=w, dma=dsem(key), cost=c)

    dummy = sb("dummyk", [128, 8])

    def barrier(keys, reads=()):
        P.add("pool", lambda e: e.memset(dummy[:], 0.0), reads=list(reads), writes=keys)

    SETUP = ("setup",)
    nset = [0]

    def setup_load(out, in_, eng="sp"):
        nset[0] += 1
        dma(eng, out, in_, [], [("setup", nset[0])], "setup")

    identF = sb("identF", [128, 128]); identB = sb("identB", [128, 128], BF16); negIB = sb("negIB", [128, 128], BF16)
    mb3 = sb("mb3", [128, 384], BF16)
    ones64 = sb("ones64", [128, 128], BF16)
    onesD = sb("onesD", [128, 128], BF16); onesDV = sb("onesDV", [128, 128], BF16); onesB = sb("onesB", [128, 128], BF16)
    indH = sb("indH", [128, 512], BF16)
    selB = sb("selB", [128, 512], BF16)
    scanm = sb("scanm", [4, 256]); cv = sb("cv", [128, 8])
    g1 = sb("g1", [128, L * 8]); g2 = sb("g2", [128, L * 8]); cw = sb("cw", [128, L * 48])
    alog_s = sb("alog_s", [4, L]); dtb_s = sb("dtb_s", [4, L]); nega = sb("nega", [4, L]); gng_s = sb("gng_s", [128, L])
    lng_bc = sb("lng_bc", [128, L, 256]); lnb_bc = sb("lnb_bc", [128, L, 256])
    qng_s = sb("qng_s", [128, L]); kng_s = sb("kng_s", [128, L]); esink = sb("esink", [128, L * 4])
    wsn = sb("wsn", [128, 128]); wsT = sb("wsT", [128, L, 2, 4, 128], BF16)
    bsbc = sb("bsbc", [128, L, 2, 2, 128])

    for (t, src) in ((mb3, c_mb3), (ones64, c_ones64), (indH, c_indH), (selB, c_sel)):
        setup_load(t[:], src, eng="pool")
    for (t, src) in ((identF, c_ident), (scanm, c_scan),
                     (cv, c_cv), (g1, g1c), (g2, g2c), (cw, convc), (alog_s, alog), (dtb_s, dtb), (gng_s, gng),
                     (qng_s, qng), (kng_s, kng)):
        setup_load(t[:], src)
    for l in range(L):
        setup_load(lng_bc[:, l, :], lng[l:l + 1, :].partition_broadcast(128))
        setup_load(lnb_bc[:, l, :], lnb[l:l + 1, :].partition_broadcast(128))
        setup_load(esink[:, l * 4:(l + 1) * 4], sinks[l:l + 1, :].partition_broadcast(128))
        for pr in range(2):
            for hh in range(2):
                h = 2 * pr + hh
                setup_load(bsbc[hh * 64:(hh + 1) * 64, l, 0, pr, :], bs[l, h:h + 1, :].partition_broadcast(64))
                for s in range(2):
                    setup_load(bsbc[hh * 64:(hh + 1) * 64, l, 1, pr, s * 64:(s + 1) * 64], bs[l, h:h + 1, 0:64].partition_broadcast(64))
    cp(identB[:], identF[:], [SETUP], ["identB"])
    ts(negIB[:], identF[:], -1.0, ALU.mult, [SETUP], ["negIB"])
    barrier([("mb3",), ("ones64",), ("indH",), ("selB",)], reads=[SETUP])
    mset(onesD[:], 1.0 / 1024, ["onesD"]); mset(onesDV[:], 1.0 / 128, ["onesDV"]); mset(onesB[:], 1.0, ["onesB"])
    act(nega[:], alog_s[:], AF.Exp, [SETUP], ["nega"])
    ts(nega[:], nega[:], -1.0, ALU.mult, ["nega"], ["nega"])
    act(esink[:], esink[:], AF.Exp, [SETUP], ["esinkx"])
    for l in range(L):
        for h in range(4):
            for var in range(2):
                key = dsem("wsn")
                if var == 0:
                    dma("sp", wsn[:], ws[l, h], [], ["wsn"], "wsn")
                    mset(wsn[0:64, 64:128], 0.0, ["wsn"])
                else:
                    mset(wsn[:], 0.0, ["wsn"])
                    dma("sp", wsn[0:64, 0:64], ws[l, h, 0:64, 0:64], [], [("wsn", 0)], "wsn")
                    dma("sp", wsn[64:128, 64:128], ws[l, h, 0:64, 0:64], [], [("wsn", 1)], "wsn")
                bi = pb(); bk, br = B(bi)
                mm(bk[:, 0:128], wsn[:], identF[:], True, ["wsn", SETUP], [br], tr=True)
                cp(wsT[:, l, var, h, :], bk[:, 0:128], [br], [("wsT", l, var, h)])

    xtok = [sb("xtok0", [128, D])] * 2
    xTs = [sb("xT%d" % i, [128, 8, 256]) for i in range(3)]
    hT = sb("hT", [128, 8, 256], BF16)
    sqb = [sb("sqb%d" % i, [128, 256], BF16) for i in range(2)]
    lnv = sb("lnv", [128, 256]); rstd = sb("rstd", [128, 256])
    raw = [sb("raw%d" % i, [128, 2, 131]) for i in range(2)]
    hist = sb("hist", [128, L, 12, 2, 3])
    cacc = [sb("cacc%d" % i, [128, 2, 128]) for i in range(2)]
    sl = sb("sl", [128, 12, 256], BF16)
    qh = sb("qh", [128, 4, 256], BF16); qg = sb("qg", [128, 4, 256], BF16); kh = sb("kh", [128, 4, 256], BF16)
    kn = sb("kn", [128, 4, 256], BF16); kdT = sb("kdT", [128, 4, 256], BF16); vbT = sb("vbT", [128, 4, 256], BF16)
    zgs = [sb("zg%d" % i, [128, 4, 256], BF16) for i in range(2)]; uT = sb("uT", [128, 2, 256])
    bvt = sb("bvt", [128, 2, 4, 128], BF16); kdt = sb("kdt", [128, 2, 4, 128], BF16)
    vnew = sb("vnew", [128, 2, 4, 128], BF16); tmpp = sb("tmpp", [128, 4, 128], BF16)
    ATP = sb("ATP", [128, 2, 4, 256], BF16)
    Qm = [[sb("Qm%d_%d" % (q, i), [128, 4, 128], BF16) for i in range(2)] for q in range(2)]
    Pm = [[sb("Pm%d_%d" % (q, i), [128, 4, 128], BF16) for i in range(2)] for q in range(2)]
    Rb = [sb("Rb%d" % q, [128, 4, 128], BF16) for q in range(2)]
    sqbW = sb("sqbW", [128, 256], BF16); lnvW = sb("lnvW", [128, 256]); rstdW = sb("rstdW", [128, 256])
    TT = sb("TT", [128, 2, 4, 128], BF16)
    E12 = sb("E12", [128, 2, 256]); E3 = sb("E3", [128, 2, 128])
    gcT = sb("gcT", [128, 2, 8])
    eglbc = sb("eglbc", [128, 4, 4])
    oT = sb("oT", [128, 4, 256]); OT = sb("OT", [128, 8, 256], BF16)
    actT = sb("actT", [128, NFF, 256], BF16); sg = [sb("sg%d" % i, [128, 256]) for i in range(2)]
    S = sb("S", [128, L, 2, 4, 128]); Sb = sb("Sb", [128, L, 2, 4, 128], BF16)
    rA = sb("rA", [4, 256]); rB_ = sb("rB", [4, 256]); rG = rA; rGC = sb("rGC", [4, 256]); rL2 = rB_
    rGCB = sb("rGCB", [4, 256]); rT0 = sb("rT0", [4, 256]); rT1 = sb("rT1", [4, 256]); rFK = sb("rFK", [4, 256]); rFQ = sb("rFQ", [4, 256])
    rhiF = sb("rhi", [128, 2, 256], BF16); rloF = sb("rlo", [128, 2, 256], BF16)
    facF = sb("fac", [128, 6, 256], BF16)
    rhi = rhiF[0:4]; rlo = rloF[0:4]; fac = facF[0:4]
    mset(rhiF[:], 0.0, ["rhi"]); mset(rloF[:], 0.0, ["rlo"]); mset(facF[:], 0.0, ["fac"])
    swx = sb("swx", [128, 3, 256])
    qAB = sb("qAB", [128, 2, 256], BF16)
    kTh = sb("kTh", [128, L, 2, 2, 128], BF16)
    kTf = sb("kTf", [128, 256])
    Vp = sb("Vp", [128, L, 2, 2, 2, 192], BF16)
    vf = sb("vf", [128, 2, 128])
    PT = [sb("PT%d" % i, [128, 4, 128], BF16) for i in range(2)]
    rec = sb("rec", [128, 4, 128])
    kout = sb("kout", [128, 128])
    tokS = sb("tokS", [128, 2, 388])
    vmg = sb("vmg", [128, 256]); xc = sb("xc", [128, 256]); sqv = vmg; st4 = sb("st4", [128, 8])
    vpad = sb("vpad", [128, 2, 4, 128], BF16); tmpb = sb("tmpb", [128, 128])
    NSLOT = 3; SLOT = 4096
    NSLAB = 28
    wscr = nc.dram_tensor("wscr", [L * NSLAB, 128, SLOT], BF16, kind="Internal").ap()
    ring = [sb("ring%d" % i, [128, SLOT], BF16) for i in range(NSLOT)]

    mset(hist[:], 0.0, ["hist"]); mset(S[:], 0.0, ["S"]); mset(Sb[:], 0.0, ["Sb"])
    mset(Vp[:], 0.0, ["Vp"]); mset(vpad[:], 0.0, ["vpad"]); mset(kTh[:], 0.0, ["kTh"])
    mset(tmpp[:], 0.0, ["tmpp"]); mset(vnew[:], 0.0, ["vnew"])

    SLAB_IDX = {}
    for g in range(1):
        SLAB_IDX[("inT", 0)] = 0
    for g in range(6):
        SLAB_IDX[("inF", g)] = 1 + g
    for g in range(2):
        SLAB_IDX[("out", g)] = 7 + g
    for g in range(11):
        SLAB_IDX[("gu", g)] = 9 + g
    for g in range(8):
        SLAB_IDX[("down", g)] = 20 + g

    def slab_list():
        ntile = SEQ // 128
        order = [(0, 0)] + [x for t in range(ntile) for x in ((t + 1, 0), (t, 1))] + [(ntile, 1)]
        d1 = lambda l: [("inT", l, 0)] + [("inF", l, g) for g in range(6)]
        d2 = lambda l: [("out", l, g) for g in range(2)] + [("gu", l, g) for g in range(11)] + [("down", l, g) for g in range(8)]
        lst = []
        prev = None
        for (tile, l) in order:
            lst += d1(l)
            if prev is not None:
                lst += d2(prev)
            prev = l
        lst += d2(prev)
        return lst

    slabs = slab_list()
    seen_slabs = set()
    issued = [0]
    used = [0]

    def issue_slab(i):
        kind, l, g = slabs[i]
        slot = i % NSLOT
        R = ring[slot]
        key = ("ring", slot)
        sid = l * NSLAB + SLAB_IDX[(kind, g)]
        first_use = sid not in seen_slabs
        seen_slabs.add(sid)
        if not first_use:
            dma("sp", R[:, :], wscr[sid], [("wscr", sid)], [("ring", slot, 0)], key)
            return
        if kind == "inF":
            n = min(4, 21 - 4 * g)
            dst = R[:, 0:8 * n * 128].rearrange("p (k c) -> p k c", k=8)
            dma("pool", dst, w_inF[l, :, :, g * 512:g * 512 + n * 128], [], [("ring", slot, 0)], key)
        elif kind == "inT":
            dst = R[:, 0:8 * 392].rearrange("p (k c) -> p k c", k=8)
            dma("pool", dst, w_inT[l], [], [("ring", slot, 0)], key)
        elif kind == "out":
            dst = R[:, 0:4096].rearrange("p (k c) -> p k c", k=8)
            dma("pool", dst, w_outL[l, :, :, g * 512:(g + 1) * 512], [], [("ring", slot, 0)], key)
        elif kind == "gu":
            dst = R[:, 0:2048].rearrange("p (k c) -> p k c", k=8)
            dma("pool", dst, w_gateL[l, :, :, g * 256:(g + 1) * 256], [], [("ring", slot, 0)], key)
            dst2 = R[:, 2048:4096].rearrange("p (k c) -> p k c", k=8)
            dma("pool", dst2, w_upL[l, :, :, g * 256:(g + 1) * 256], [], [("ring", slot, 1)], key)
        else:
            dst = R[:, 0:NFF * 128].rearrange("p (k c) -> p k c", k=NFF)
            dma("pool", dst, w_downL[l, :, :, g * 128:(g + 1) * 128], [], [("ring", slot, 0)], key)
        dma("sp", wscr[sid], R[:, :], [("ring", slot)], [("wscr", sid)], ("wscr_st", slot))

    def next_slab(kind, l, g):
        i = used[0]
        assert slabs[i] == (kind, l, g), (slabs[i], kind, l, g)
        while issued[0] < min(len(slabs), i + NSLOT):
            issue_slab(issued[0]); issued[0] += 1
        used[0] += 1
        slot = i % NSLOT
        return ring[slot], ("ring", slot)

    def rmsnorm(l, gcols, N, xT, xi):
        bi = pb(); bk, br = B(bi)
        for kc in range(8):
            if kc % 2 == 0:
                act(hT[:, kc, 0:N], xT[:, kc, 0:N], AF.Square, [("xT", xi, kc)], [("hT", kc)])
            else:
                tt(hT[:, kc, 0:N], xT[:, kc, 0:N], xT[:, kc, 0:N], ALU.mult, [("xT", xi, kc)], [("hT", kc)], eng="pool")
        for kc in range(8):
            mm(bk[:, 0:N], onesD[:], hT[:, kc, 0:N], kc == 0, [("hT", kc), "onesD"], [br])
        act(lnv[:, 0:N], bk[:, 0:N], AF.Ln, [br, SETUP], ["lnv"], bias=cv[:, 0:1])
        act(rstd[:, 0:N], lnv[:, 0:N], AF.Exp, ["lnv"], ["rstd"], scale=-0.5)
        for kc in range(8):
            stt(hT[:, kc, 0:N], xT[:, kc, 0:N], gcols[:, l * 8 + kc:l * 8 + kc + 1], rstd[:, 0:N], ALU.mult, ALU.mult,
                [("xT", xi, kc), "rstd", SETUP], [("hT", kc)])

    def load_x(mode, st, N, xT, xi):
        for q in range(2 if mode == "p" else 1):
            xt = xtok[q]
            if mode == "p":
                dma("sp", xt[:], xp[q, st * 128:(st + 1) * 128, :], [], [("xtok", 0)], ("xtok", 0))
            else:
                dma("sp", xt[:], xs_.rearrange("s t d -> (s t) d"), [], [("xtok", 0)], ("xtok", 0))
            for half in range(2):
                bi = pb(); bk, br = B(bi)
                for j in range(4):
                    kc = half * 4 + j
                    mm(bk[:, j * 128:(j + 1) * 128], xt[:, kc * 128:(kc + 1) * 128], identF[:], j == 0, [("xtok", 0), SETUP], [br], tr=True)
                dst = xT[:, half * 4:half * 4 + 4, q * 128:(q + 1) * 128]
                src = bk[:].rearrange("p (k c) -> p k c", k=4)
                if half == 0:
                    P.add("act", lambda e, dst=dst, src=src: e.copy(out=dst, in_=src), reads=[br], writes=[("xT", xi, half * 4 + j) for j in range(4)])
                else:
                    cp(dst, src, [br], [("xT", xi, half * 4 + j) for j in range(4)])

    def store_y(mode, st, N, xT, xi):
        for q in range(2 if mode == "p" else 1):
            xt = xtok[q]
            for half in range(2):
                bi = pb(); bk, br = B(bi)
                for j in range(4):
                    kc = half * 4 + j
                    mm(bk[:, j * 128:(j + 1) * 128], xT[:, kc, q * 128:(q + 1) * 128], identF[:], j == 0, [("xT", xi, kc), SETUP], [br], tr=True)
                if half == 0:
                    P.add("act", lambda e, xt=xt, bk=bk: e.copy(out=xt[:, 0:512], in_=bk[:]), reads=[br], writes=[("xtok", 0, 0)])
                else:
                    cp(xt[:, 512:1024], bk[:], [br], [("xtok", 0, 1)])
            if mode == "p":
                dma("sp", yp[q, st * 128:(st + 1) * 128, :], xt[:], [("xtok", 0)], [], ("xtok", 0), store=True)
            else:
                dma("sp", ys.rearrange("s t d -> (s t) d"), xt[:], [("xtok", 0)], [], ("xtok", 0), store=True)

    def stl(mode, st, l, xT, xi, par, preloaded=False):
        prompt = mode == "p"
        N = 256 if prompt else 128
        NT = 2 if prompt else 1
        T = 128 if prompt else 64
        last = prompt and st == SEQ // 128 - 1
        var = 0 if prompt else 1
        cur = st % 2 if prompt else 0
        hsl = 1 - cur
        nch = N // 64
        tokb = []; fq_units = []; alloc = [pb]; rows_part3 = None; factor_unit = None
        zg = zgs[par]; zgk = "zg%d" % par

        def D1_stream():
            nonlocal tokb, fq_units, rows_part3, factor_unit
            dpool[0] = [4, 5]
            if l == 0 and not preloaded:
                load_x(mode, st, N, xT, xi)
            rmsnorm(l, g1, N, xT, xi)
            yield

            R, rk = next_slab("inT", l, 0)
            Rv = R[:, 0:8 * 392].rearrange("p (k c) -> p k c", k=8)
            ba, bra = B(pb()); bb_, brb = B(pb())
            for kc in range(8):
                mm(ba[:, 0:N], Rv[:, kc, 0:128], hT[:, kc, 0:N], kc == 0, [rk, ("hT", kc)], [bra])
            for kc in range(8):
                mm(bb_[:, 0:N], Rv[:, kc, 260:388], hT[:, kc, 0:N], kc == 0, [rk, ("hT", kc)], [brb])
            act(rA[:, 0:N], ba[0:4, 0:N], AF.Exp, [bra, SETUP], ["rA"], bias=dtb_s[:, l:l + 1])
            act(rB_[:, 0:N], bb_[0:4, 0:N], AF.Exp, [brb], ["rB"], scale=-1.0)
            act(rA[:, 0:N], rA[:, 0:N], AF.Ln, ["rA", SETUP], ["rA"], bias=cv[0:4, 2:3])
            act(rL2[:, 0:N], rB_[:, 0:N], AF.Ln, ["rB", SETUP], ["rB", "rB"], bias=cv[0:4, 2:3])
            tokb = []
            yield
            for q in range(NT):
                bt, brt = B(pb())
                for kc in range(8):
                    mm(bt[:, 0:388], hT[:, kc, q * 128:(q + 1) * 128], Rv[:, kc, 4:392], kc == 0, [rk, ("hT", kc)], [brt])
                P.add("act", lambda e, q=q, bt=bt: e.copy(out=tokS[:, q, :], in_=bt[:, 0:388]), reads=[brt], writes=[("tokS", q)])
                tokb.append((tokS[:, q, :], ("tokS", q)))
            ts(rG[:, 0:N], rA[:, 0:N], nega[:, l:l + 1], ALU.mult, ["rA", "nega"], ["rA", "rA"])
            P.add("dve", lambda e: e.tensor_tensor_scan(out=rGC[:, 0:N], data0=scanm[:, 0:N], data1=rG[:, 0:N], initial=0.0,
                                                        op0=ALU.mult, op1=ALU.add), reads=["rA", SETUP], writes=["rGC"])
            tt(rGCB[:, 0:N], rGC[:, 0:N], rL2[:, 0:N], ALU.subtract, ["rGC", "rB"], ["rGCB"])
            for i, (src, nm) in enumerate(((rGC, "rGC"), (rGCB, "rGCB"))):
                cp(rhi[:, i, 0:N], src[:, 0:N], [nm], [("rhi", i)])
                tt(rlo[:, i, 0:N], src[:, 0:N], rhi[:, i, 0:N], ALU.subtract, [nm, ("rhi", i)], [("rlo", i)])
            gc3 = rGC[:, 0:N].rearrange("p (c t) -> p c t", t=64)

            def rows_part2():
                act(rFQ[:, 0:N], bq[0:4, 0:N], AF.Ln, [brq, SETUP], ["rFQ"], bias=cv[0:4, 0:1])
                act(rFK[:, 0:N], bkk[0:4, 0:N], AF.Ln, [brk, SETUP], ["rFK"], bias=cv[0:4, 0:1])
                act(rFQ[:, 0:N], rFQ[:, 0:N], AF.Exp, ["rFQ", SETUP], ["rFQ"], scale=-0.5, bias=cv[0:4, 1:2])
                act(rFK[:, 0:N], rFK[:, 0:N], AF.Exp, ["rFK"], ["rFK"], scale=-0.5)
                act(rT0[:, 0:N], rGC[:, 0:N], AF.Exp, ["rGC"], ["rT0"])
                act(rT1[:, 0:N], rGCB[:, 0:N], AF.Exp, ["rGCB"], ["rT1"])
                act(fac[:, 5, 0:N], rL2[:, 0:N], AF.Exp, ["rB"], [("fac", 5)], scale=-1.0)
                cp(fac[:, 0, 0:N], rFQ[:, 0:N], ["rFQ"], [("fac", 0)])
                tt(fac[:, 1, 0:N], rFQ[:, 0:N], rT0[:, 0:N], ALU.mult, ["rFQ", "rT0"], [("fac", 1)])
                cp(fac[:, 2, 0:N], rFK[:, 0:N], ["rFK"], [("fac", 2)])
                stt(fac[:, 3, 0:N], rT1[:, 0:N], -1.0, rFK[:, 0:N], ALU.mult, ALU.mult, ["rT1", "rFK"], [("fac", 3)])
                tt(rT0[:, 0:N].rearrange("p (c t) -> p c t", t=64), gc3[:, :, 63:64].to_broadcast([4, nch, 64]), gc3, ALU.subtract,
                   ["rGC", "rT0"], ["rT0"])
                act(rT0[:, 0:N], rT0[:, 0:N], AF.Exp, ["rT0"], ["rT0"])
                tt(fac[:, 4, 0:N], rT0[:, 0:N], rFK[:, 0:N], ALU.mult, ["rT0", "rFK"], [("fac", 4)])
                act(rT1[:, 0:nch], gc3[:, :, 63], AF.Exp, ["rGC", "rT1"], ["rT1"])
                egl_b = sb_egl
                cp(egl_b[:, 0, 0:nch], rT1[:, 0:nch], ["rT1"], ["eglb"])
                tt(egl_b[:, 1, 0:nch], rT1[:, 0:nch], egl_b[:, 0, 0:nch], ALU.subtract, ["rT1", "eglb"], ["eglb2"])

            def rows_part3():
                egl_b = sb_egl
                for h in range(4):
                    bk, br = B(pbA())
                    mm(bk[:, 0:nch], selB[:, h * 128:(h + 1) * 128], sb_eglF[:, 0, 0:nch], True, ["selB", "eglb"], [br])
                    mm(bk[:, 0:nch], selB[:, h * 128:(h + 1) * 128], sb_eglF[:, 1, 0:nch], False, ["selB", "eglb2"], [br])
                    cp(eglbc[:, h, 0:nch], bk[:, 0:nch], [br], [("eglbc", h)])
                for q in range(NT):
                    bk, br = B(pbA())
                    mm(bk[:, 0:4], rGC[:, q * 128:(q + 1) * 128], identF[0:4, 0:4], True, ["rGC", SETUP], [br], tr=True)
                    mm(bk[:, 4:8], rGCB[:, q * 128:(q + 1) * 128], identF[0:4, 0:4], False, ["rGCB", SETUP], [br], tr=True)
                    ts(gcT[:, q, 0:4], bk[:, 0:4], -1.0, ALU.mult, [br], [("gcT", q)])
                    cp(gcT[:, q, 4:8], bk[:, 4:8], [br], [("gcT", q)])

            targets = ((qh, "qh", 0, 0), (qg, "qg", 0, 1), (kh, "kh", 4, 2), (kn, "kn", 4, 3), (kdT, "kdT", 4, 4))
            fq_units = [(dst, nm, c0, fi, h) for (dst, nm, c0, fi) in targets for h in range(4)]

            def factor_unit(u):
                dst, nm, c0, fi, h = u
                bk, br = B(alloc[0]())
                mm(bk[:, 0:N], selB[:, h * 128:(h + 1) * 128], facF[:, fi, 0:N], True, ["selB", ("fac", fi)], [br])
                tt(dst[:, h, 0:N], sl[:, c0 + h, 0:N], bk[:, 0:N], ALU.mult, [("sl", c0 + h), br], [(nm, h)])

            bq = banks[3][:, 0:256]; brq = ("ps", 3); bkk = banks[3][:, 256:512]; brk = ("ps", 3)
            pending = []
            deferred = []
            for g in range(6):
                R, rk = next_slab("inF", l, g)
                n = min(4, 21 - 4 * g)
                Rv = R[:, 0:8 * n * 128].rearrange("p (k c) -> p k c", k=8)
                for j in range(n):
                    c = 4 * g + j
                    bi = pb(); bk, br = B(bi)
                    for kc in range(8):
                        mm(bk[:, 0:N], Rv[:, kc, j * 128:(j + 1) * 128], hT[:, kc, 0:N], kc == 0, [rk, ("hT", kc)], [br])
                    for f_ in pending:
                        f_()
                    pending = []
                    if c < 12:
                        rb = raw[c % 2]; ca = cacc[c % 2]; rkx = ("raw", c % 2); ckx = ("cacc", c % 2)
                        cp(rb[:, :, 0:3], hist[:, l, c, :, :], [("hist", l, c)], [rkx], eng="pool")
                        act(rb[:, :, 3:3 + T], bk[:, 0:N].rearrange("p (s t) -> p s t", s=2), AF.Copy, [br], [rkx])
                        cp(hist[:, l, c, :, :], rb[:, :, T:T + 3], [rkx], [("hist", l, c)], eng="pool")
                    for f_ in deferred:
                        f_()
                    deferred = []
                    if c < 12:
                        def epi(c=c, rb=rb, ca=ca, rkx=rkx, ckx=ckx):
                            w0 = l * 48 + c * 4
                            ts(ca[:, :, 0:T], rb[:, :, 0:T], cw[:, w0:w0 + 1], ALU.mult, [rkx, SETUP], [ckx])
                            for tp in range(1, 4):
                                stt(ca[:, :, 0:T], rb[:, :, tp:tp + T], cw[:, w0 + tp:w0 + tp + 1], ca[:, :, 0:T], ALU.mult, ALU.add,
                                    [rkx, ckx, SETUP], [ckx])
                            act(sl[:, c, 0:N].rearrange("p (s t) -> p s t", s=2), ca[:, :, 0:T], AF.Silu, [ckx], [("sl", c)])
                            if c < 8:
                                s = sqb[c % 2]
                                tt(s[:, 0:N], sl[:, c, 0:N], sl[:, c, 0:N], ALU.mult, [("sl", c)], [("sqb", c % 2)], eng="pool")
                                h = c % 4
                                tgt, tr_ = (bq, brq) if c < 4 else (bkk, brk)

                                def ssq_mm(tgt=tgt, tr_=tr_, h=h, s=s, c=c):
                                    mm(tgt[:, 0:N], indH[:, h * 128:(h + 1) * 128], s[:, 0:N], h == 0, [("sqb", c % 2), "indH"], [tr_])
                                pending.append(ssq_mm)
                        deferred.append(epi)
                    elif c < 16:
                        act(zg[:, c - 12, 0:N], bk[:, 0:N], AF.Silu, [br], [(zgk, c - 12)])
                    elif c < 18:
                        act(uT[:, c - 16, 0:N], bk[:, 0:N], AF.Gelu_apprx_tanh, [br], [("uT", c - 16)])
                    else:
                        act(swx[:, c - 18, 0:N], bk[:, 0:N], AF.Copy, [br], [("swx", c - 18)])
                    if c == 10:
                        rows_part2()
                    yield
            for f_ in pending:
                f_()
            fq_units += [(vbT, "vbT", 8, 5, h) for h in range(4)]
            yield


        def fac_stream():
            alloc[0] = pbA
            rows_part3()
            yield
            while fq_units:
                factor_unit(fq_units.pop(0))
                if len(fq_units) % 2 == 0:
                    yield

        def gdn_pre():
            for q in range(NT):
                qc = slice(q * 128, (q + 1) * 128)
                bk, br = B(pbA())
                bkb = bk[:].bitcast(BF16)
                for h in range(4):
                    mm(bkb[:, h * 128:(h + 1) * 128], kdT[:, h, qc], identB[:], h == 0, [("kdT", h), "identB"], [br], tr=True)
                cp(kdt[:, q, :, :], bkb[:, 0:512].rearrange("p (h d) -> p h d", h=4), [br], [("kdt", q)])
                bk, br = B(pbA())
                bkb = bk[:].bitcast(BF16)
                for h in range(4):
                    mm(bkb[:, h * 128:(h + 1) * 128], vbT[:, h, qc], identB[:], h == 0, [("vbT", h), "identB"], [br], tr=True)
                P.add("act", lambda e, q=q, bkb=bkb: e.copy(out=bvt[:, q, :, :], in_=bkb[:, 0:512].rearrange("p (h d) -> p h d", h=4)),
                      reads=[br], writes=[("bvt", q)])
                yield
                for hp in range(2):
                    bG, brG = B(pbA())
                    bZ = [B(pbA()) for _ in range(2)]
                    for hh in range(2):
                        h = 2 * hp + hh
                        mm(bG[:, hh * 256:hh * 256 + 128], kh[:, h, qc], qh[:, h, qc], hh == 0, [("kh", h), ("qh", h)], [brG])
                        mm(bG[:, hh * 256 + 128:hh * 256 + 256], kh[:, h, qc], kh[:, h, qc], False, [("kh", h)], [brG])
                        z, zr = bZ[hh]
                        mm(z[:, 0:384], identB[:], mb3[:], True, ["identB", "mb3"], [zr])
                        sh = selB[:, h * 128:(h + 1) * 128]
                        for (o, i) in ((0, 0), (128, 1), (256, 0)):
                            mm(z[:, o:o + 128], sh, rhiF[:, i, qc], False, ["selB", ("rhi", i)], [zr])
                            mm(z[:, o:o + 128], sh, rloF[:, i, qc], False, ["selB", ("rlo", i)], [zr])
                        act(E12[:, hh, :], z[:, 0:256], AF.Exp, [zr, ("gcT", q)], [("E12", hh)], bias=gcT[:, q, h:h + 1])
                        act(E3[:, hh, :], z[:, 256:384], AF.Exp, [zr, ("gcT", q)], [("E3", hh)], scale=-1.0, bias=gcT[:, q, 4 + h:5 + h])
                    tt(ATP[:, q, 2 * hp:2 * hp + 2, :], bG[:].rearrange("p (h c) -> p h c", h=2), E12[:, :, :], ALU.mult,
                       [brG, "E12"], [("ATP", q, hp)])
                    tt(Qm[q][0][:, 2 * hp:2 * hp + 2, :], bG[:].rearrange("p (h c) -> p h c", h=2)[:, :, 128:256], E3[:, :, :], ALU.mult,
                       [brG, "E3"], [("Qm", q, 0, hp)])
                    yield
            for m in range(6):
                for q in range(NT):
                    bR, brR = banks[6 + q], ("ps", 6 + q)
                    Qc = Qm[q][m % 2]; Qk = ("Qm", q, m % 2)
                    if m == 0:
                        Pc = ATP[:, q, :, 128:256]; Pk = ("ATP", q)
                    else:
                        Pc = Pm[q][m % 2][:, :, :]; Pk = ("Pm", q, m % 2)
                    for h in range(4):
                        if m == 0:
                            mm(bR[:, h * 128:(h + 1) * 128], identB[:], identB[:], h == 0, ["identB"], [brR])
                            mm(bR[:, h * 128:(h + 1) * 128], Qc[:, h, :], negIB[:], False, [Qk, "negIB"], [brR])
                        else:
                            mm(bR[:, h * 128:(h + 1) * 128], Qc[:, h, :], Rb[q][:, h, :], False, [Qk, ("Rb", q)], [brR])
                    if m < 5:
                        bP, brP = B(pbA()); bQ, brQ = B(pbA())
                        for h in range(4):
                            if m < 4:
                                mm(bP[:, h * 128:(h + 1) * 128], Qc[:, h, :], Pc[:, h, :], h == 0, [Qk, Pk], [brP])
                            mm(bQ[:, h * 128:(h + 1) * 128], Pc[:, h, :], Qc[:, h, :], h == 0, [Qk, Pk], [brQ])
                        nx = (m + 1) % 2
                        if m < 4:
                            P.add("act", lambda e, nx=nx, bP=bP, q=q: e.copy(out=Pm[q][nx][:, :, :], in_=bP[:].rearrange("p (h c) -> p h c", h=4)),
                                  reads=[brP], writes=[("Pm", q, nx)])
                        cp(Qm[q][nx][:, :, :], bQ[:].rearrange("p (h c) -> p h c", h=4), [brQ], [("Qm", q, nx)])
                        P.add("act", lambda e, bR=bR, q=q: e.copy(out=Rb[q][:, :, :], in_=bR[:].rearrange("p (h c) -> p h c", h=4)),
                              reads=[brR], writes=[("Rb", q)])
                    else:
                        cp(TT[:, q, :, :], bR[:].rearrange("p (h c) -> p h c", h=4), [brR], [("TT", q)])
                    yield

        def gdn_chain():
            if prompt:
                steps = [(s, s, hf) for hf in range(2) for s in range(2)]
            else:
                steps = [(s, 0, s) for s in range(2)]
            for (s, q, hf) in steps:
                rows = slice(hf * 64, (hf + 1) * 64)
                c0 = q * 128 + hf * 64
                ch = c0 // 64
                qc = slice(q * 128, (q + 1) * 128)
                Sk = ("S", l, s); Sbk = ("Sb", l, s)
                bKS, brKS = B(pbA()); bO, brO = B(pbA())
                for h in range(4):
                    mm(bKS[:, h * 128:(h + 1) * 128], kn[:, h, qc], Sb[:, l, s, h, :], h == 0, [("kn", h), Sbk], [brKS])
                for h in range(4):
                    mm(bO[:, h * 64:(h + 1) * 64], Sb[:, l, s, h, :], qg[:, h, c0:c0 + 64], h == 0, [("qg", h), Sbk], [brO])
                tt(tmpp[rows, :, :], bKS[rows, :].rearrange("p (h d) -> p h d", h=4), bvt[rows, q, :, :], ALU.add,
                   [brKS, ("bvt", q)], [("tmpp", hf)])
                yield
                bV, brV = B(pbA())
                for h in range(4):
                    mm(bV[:, h * 128:(h + 1) * 128], TT[rows, q, h, :], tmpp[rows, h, :], h == 0, [("TT", q), ("tmpp", hf)], [brV])
                P.add("act", lambda e, rows=rows, q=q, bV=bV: e.copy(out=vnew[rows, q, :, :], in_=bV[rows, :].rearrange("p (h d) -> p h d", h=4)),
                      reads=[brV], writes=[("vnew", q, hf)])
                yield
                for h in range(4):
                    mm(bO[:, h * 64:(h + 1) * 64], vnew[rows, q, h, :], ATP[rows, q, h, hf * 64:hf * 64 + 64], False,
                       [("vnew", q, hf), ("ATP", q)], [brO])
                P.add("act", lambda e, c0=c0, bO=bO: e.copy(out=oT[:, :, c0:c0 + 64], in_=bO[:, 0:256].rearrange("p (h c) -> p h c", h=4)),
                      reads=[brO], writes=[("oT", ch)])
                bD, brD = B(pbA())
                for h in range(4):
                    mm(bD[:, h * 128:(h + 1) * 128], kdt[rows, q, h, :], vnew[rows, q, h, :], h == 0, [("kdt", q), ("vnew", q, hf)], [brD])
                for h in range(4):
                    stt(S[:, l, s, h, :], S[:, l, s, h, :], eglbc[:, h, ch:ch + 1], bD[:, h * 128:(h + 1) * 128], ALU.mult, ALU.add,
                        [Sk, ("eglbc", h), brD], [Sk])
                P.add("act", lambda e, s=s: e.copy(out=Sb[:, l, s, :, :], in_=S[:, l, s, :, :]), reads=[Sk], writes=[Sbk])
                yield
            for h in range(4):
                s_ = sqbW
                act(s_[:, 0:N], oT[:, h, 0:N], AF.Square, ["oT"], ["sqbW"])
                yield
                bk, br = B(pbA())
                mm(bk[:, 0:N], onesDV[:], s_[:, 0:N], True, ["sqbW", "onesDV"], [br])
                act(lnvW[:, 0:N], bk[:, 0:N], AF.Ln, [br, SETUP], ["lnvW"], bias=cv[:, 0:1])
                act(rstdW[:, 0:N], lnvW[:, 0:N], AF.Exp, ["lnvW"], ["rstdW"], scale=-0.5)
                stt(lnvW[:, 0:N], oT[:, h, 0:N], gng_s[:, l:l + 1], rstdW[:, 0:N], ALU.mult, ALU.mult, ["oT", "rstdW", "lnvW", SETUP], ["lnvW"])
                tt(OT[:, h, 0:N], lnvW[:, 0:N], zg[:, h, 0:N], ALU.mult, ["lnvW", (zgk, h)], [("OT", h)])
                yield

        def gmlp():
            for q in range(NT):
                bt, brt = tokb[q]
                act(vmg[:], bt[:, 0:256], AF.Gelu_apprx_tanh, [brt], ["vmg"])
                if prompt:
                    P.add("act", lambda e, q=q, bt=bt: e.copy(out=Vp[:, l, q, cur, :, 64:128], in_=bt[:, 260:388].rearrange("p (g d) -> p g d", g=2)),
                          reads=[brt], writes=[("Vp", l, q, cur)])
                else:
                    for s in range(2):
                        rws = slice(s * 64, (s + 1) * 64)
                        P.add("act", lambda e, s=s, rws=rws, bt=bt: e.copy(out=Vp[rws, l, s, 1, :, 64:128], in_=bt[rws, 260:388].rearrange("p (g d) -> p g d", g=2)),
                              reads=[brt], writes=[("Vp", l, s, 1)])
                if last or not prompt:
                    cp(vf[:, q, :], bt[:, 260:388], [brt], [("vf", q)])
                v3 = vmg[:].rearrange("p (h d) -> p h d", h=4)
                P.add("dve", lambda e, v3=v3: e.tensor_reduce(out=st4[:, 0:4], in_=v3, axis=AX.X, op=ALU.add), reads=["vmg"], writes=[("st4", 0)])
                ts(st4[:, 0:4], st4[:, 0:4], -1.0 / 64, ALU.mult, [("st4", 0)], [("st4", 0)])
                x3 = xc[:].rearrange("p (h d) -> p h d", h=4)
                tt(x3, v3, st4[:, 0:4].unsqueeze(2).to_broadcast([128, 4, 64]), ALU.add, ["vmg", ("st4", 0)], ["xc"])
                tt(sqv[:], xc[:], xc[:], ALU.mult, ["xc", "vmg"], ["vmg", "vmg"])
                P.add("dve", lambda e: e.tensor_reduce(out=st4[:, 4:8], in_=sqv[:].rearrange("p (h d) -> p h d", h=4), axis=AX.X, op=ALU.add),
                      reads=["vmg"], writes=[("st4", 1)])
                yield
                act(st4[:, 4:8], st4[:, 4:8], AF.Ln, [("st4", 1), SETUP], [("st4", 1)], scale=1.0 / 64, bias=cv[:, 0:1])
                act(st4[:, 4:8], st4[:, 4:8], AF.Exp, [("st4", 1)], [("st4", 1)], scale=-0.5)
                tt(x3, x3, st4[:, 4:8].unsqueeze(2).to_broadcast([128, 4, 64]), ALU.mult, ["xc", ("st4", 1)], ["xc"])
                tt(xc[:], xc[:], lng_bc[:, l, :], ALU.mult, ["xc", SETUP], ["xc"])
                tt(xc[:], xc[:], lnb_bc[:, l, :], ALU.add, ["xc", SETUP], ["xc"])
                if not prompt:
                    for s in range(2):
                        dma("sp", o_ms[l, s], xc[s * 64:(s + 1) * 64, :], ["xc"], [], "o_ms", store=True)
                vp4 = vpad[:, q, :, :].rearrange("p (a b) c -> p a b c", b=2)
                x4 = xc[:].rearrange("p (a b d) -> p a b d", a=2, b=2)
                cp(vp4[:, :, 0, 0:64], x4[:, :, 0, :], ["xc"], [("vpad", q, 0)])
                cp(vp4[:, :, 1, 64:128], x4[:, :, 1, :], ["xc"], [("vpad", q, 1)])
                yield
                for pr in range(2):
                    bk, br = B(pbB())
                    mm(bk[:, 0:128], vpad[:, q, 2 * pr, :], wsT[:, l, var, 2 * pr, :], True, [("vpad", q), ("wsT", l, var, 2 * pr)], [br])
                    mm(bk[:, 0:128], vpad[:, q, 2 * pr + 1, :], wsT[:, l, var, 2 * pr + 1, :], False, [("vpad", q), ("wsT", l, var, 2 * pr + 1)], [br])
                    tt(tmpb[:], bk[:, 0:128], bsbc[:, l, var, pr, :], ALU.add, [br, SETUP, "tmpb"], ["tmpb"])
                    tt(OT[:, 4 + pr, q * 128:(q + 1) * 128], uT[:, pr, q * 128:(q + 1) * 128], tmpb[:], ALU.mult, [("uT", pr), "tmpb"], [("OT", 4 + pr, q)])
                yield

        def swa():
            for i in range(3):
                s_ = sqbW
                act(s_[:, 0:N], swx[:, i, 0:N], AF.Square, [("swx", i)], ["sqbW"])
                bk, br = B(pbB())
                mm(bk[:, 0:N], ones64[:], s_[:, 0:N], True, ["sqbW", "ones64"], [br])
                act(lnvW[:, 0:N], bk[:, 0:N], AF.Ln, [br, SETUP], ["lnvW"], bias=cv[:, 0:1])
                act(rstdW[:, 0:N], lnvW[:, 0:N], AF.Exp, ["lnvW"], ["rstdW"], scale=-0.5)
                if i < 2:
                    stt(qAB[:, i, 0:N], swx[:, i, 0:N], qng_s[:, l:l + 1], rstdW[:, 0:N], ALU.mult, ALU.mult, [("swx", i), "rstdW", SETUP], [("qAB", i)])
                else:
                    stt(kTf[:, 0:N], swx[:, 2, 0:N], kng_s[:, l:l + 1], rstdW[:, 0:N], ALU.mult, ALU.mult, [("swx", 2), "rstdW", SETUP], ["kTf"])
                    if prompt:
                        for s in range(2):
                            cp(kTh[:, l, s, cur, :], kTf[:, s * 128:(s + 1) * 128], ["kTf"], [("kTh", l, s, cur)])
                    else:
                        cp(kTh[:, l, 0, 1, :], kTf[:, 0:128], ["kTf"], [("kTh", l, 0, 1)])
                yield
            if not prompt:
                for s in range(2):
                    dma("sp", kout[:], cK[l, s], [], ["kout"], "kout")
                    bk, br = B(pbB())
                    mm(bk[:, 0:128], kout[:], identF[:], True, ["kout", SETUP], [br], tr=True)
                    cp(kTh[:, l, s, 0, :], bk[:, 0:128], [br], [("kTh", l, s, 0)])
                    dma("pool", Vp[:, l, s, 0, :, 64:128], cV[l, s].rearrange("t (g d) -> t g d", g=2), [], [("Vp", l, s, 0)], ("vcache", s))
                    dma("sp", o_ks[l, s, 0:64, :], cK[l, s, 64:128, :], [], [], "o_cache", store=True)
                    dma("sp", o_vs[l, s, 0:64, :], cV[l, s, 64:128, :], [], [], "o_cache", store=True)
                    yield
        def swa_att():
            for s in range(2):
                bO, brO = banks[6], ("ps", 6)
                bS, brS = banks[7], ("ps", 7)
                if prompt:
                    qcols = slice(s * 128, (s + 1) * 128); NQ = 128
                    ktiles = ([("h", hsl)] if st > 0 else []) + [("c", cur)]
                else:
                    qcols = slice(s * 64, (s + 1) * 64); NQ = 64
                    ktiles = [("h", 0), ("c", 1)]
                first = True
                for ki, (kk, slot) in enumerate(ktiles):
                    pt = PT[ki % 2]; ptk = ("PT", ki % 2)
                    if prompt or kk == "h":
                        krows = slice(0, 128); kT_ = kTh[:, l, s, slot, :]; kkey = ("kTh", l, s, slot); vkey = ("Vp", l, s, slot)
                        vs_ = s; vslot = slot
                    else:
                        krows = slice(s * 64, (s + 1) * 64); kT_ = kTh[:, l, 0, 1, :]; kkey = ("kTh", l, 0, 1); vkey = ("Vp", l, s, 1)
                        vs_ = s; vslot = 1
                    for g in range(2):
                        bk, br = B(pbB())
                        gr = slice(g * 64, (g + 1) * 64)
                        for hh in range(2):
                            h = 2 * g + hh
                            mm(bk[:, hh * NQ:(hh + 1) * NQ], kT_[gr, :], qAB[gr, hh, qcols], hh == 0, [kkey, ("qAB", hh)], [br])
                        act(pt[:, 2 * g:2 * g + 2, 0:NQ], bk[:, 0:2 * NQ].rearrange("p (h c) -> p h c", h=2), AF.Exp, [br], [(ptk[0], ptk[1], g)], scale=0.125)
                    if prompt:
                        if kk == "h":
                            mset(pt[0:64, :, 64:128], 0.0, [ptk])
                        else:
                            mset(pt[64:128, :, 0:64], 0.0, [ptk])
                    yield
                    for h in range(4):
                        mm(bS[:, h * NQ:(h + 1) * NQ], onesB[krows, :], pt[krows, h, 0:NQ], first and h == 0, [ptk, "onesB"], [brS])
                    for h in range(4):
                        g = h // 2
                        lo = 64 if h % 2 == 0 else 0
                        mm(bO[:, g * NQ:(g + 1) * NQ], Vp[krows, l, vs_, vslot, g, lo:lo + 128], pt[krows, h, 0:NQ], first and h == 0, [ptk, vkey], [brO])
                    first = False
                    yield
                for h in range(4):
                    ts(rec[:, h, 0:NQ], bS[:, h * NQ:(h + 1) * NQ], esink[:, l * 4 + h:l * 4 + h + 1], ALU.add, [brS, "esinkx"], [("rec", h)])
                act(rec[:, :, 0:NQ], rec[:, :, 0:NQ], AF.Ln, ["rec"], ["rec"])
                act(rec[:, :, 0:NQ], rec[:, :, 0:NQ], AF.Exp, ["rec"], ["rec"], scale=-1.0)
                for g in range(2):
                    for hh in range(2):
                        rws = slice(hh * 64, (hh + 1) * 64)
                        tt(OT[rws, 6 + g, qcols], bO[rws, g * NQ:(g + 1) * NQ], rec[rws, 2 * g + hh, 0:NQ], ALU.mult, [brO, "rec"], [("OT", 6 + g, s, hh)])
                yield

        def outputs():
            if last or not prompt:
                for q in range(NT):
                    bk, br = B(pbA())
                    mm(bk[:, 0:128], kTf[:, q * 128:(q + 1) * 128], identF[:], True, ["kTf", SETUP], [br], tr=True)
                    cp(kout[:], bk[:, 0:128], [br, "kout"], ["kout"])
                    if prompt:
                        dma("sp", o_kp[l, q], kout[:], ["kout"], [], "kout", store=True)
                        dma("sp", o_vp[l, q], vf[:, q, :], [("vf", q)], [], ("o_v", q), store=True)
                    else:
                        for s in range(2):
                            dma("sp", o_ks[l, s, 64:128, :], kout[s * 64:(s + 1) * 64, :], ["kout"], [], "kout", store=True)
                            dma("sp", o_vs[l, s, 64:128, :], vf[s * 64:(s + 1) * 64, 0, :], [("vf", 0)], [], ("o_v", 0), store=True)
                for s in range(2):
                    og = o_gp if prompt else o_gs
                    dma("sp", og[l, s].rearrange("h k v -> k h v"), S[:, l, s, :, :], [("S", l, s)], [], "o_g", store=True)
                    oc = o_cp if prompt else o_cs
                    for c in range(12):
                        dma("sp", oc[l, s, :, c * 128:(c + 1) * 128].rearrange("t p -> p t"), hist[:, l, c, s, :], [("hist", l, c)], [], "o_c", store=True, slow=True)

        def Mh_stream():
            yield from fac_stream()
            yield from gdn_pre()
            yield from gmlp()
            yield from swa()

        def Mt_stream():
            yield from gdn_chain()
            yield from swa_att()
            outputs()
            yield

        def D2_stream():
            dpool[0] = [3, 4, 5]
            for g in range(2):
                R, rk = next_slab("out", l, g)
                Rv = R[:, 0:4096].rearrange("p (k c) -> p k c", k=8)
                for j in range(4):
                    c = 4 * g + j
                    bk, br = B(pb())
                    for kc in range(8):
                        mm(bk[:, 0:N], Rv[:, kc, j * 128:(j + 1) * 128], OT[:, kc, 0:N], kc == 0, [rk, ("OT", kc)], [br])
                    tt(xT[:, c, 0:N], xT[:, c, 0:N], bk[:, 0:N], ALU.add, [("xT", xi, c), br], [("xT", xi, c)])
            yield
            rmsnorm(l, g2, N, xT, xi)
            yield
            for g in range(11):
                R, rk = next_slab("gu", l, g)
                Gv = R[:, 0:2048].rearrange("p (k c) -> p k c", k=8)
                Uv = R[:, 2048:4096].rearrange("p (k c) -> p k c", k=8)
                for j in range(2):
                    c = 2 * g + j
                    bGU, brGU = B(pb())
                    bG = bGU[:, 0:256]; bU = bGU[:, 256:512]
                    for kc in range(8):
                        mm(bG[:, 0:N], Gv[:, kc, j * 128:(j + 1) * 128], hT[:, kc, 0:N], kc == 0, [rk, ("hT", kc)], [brGU])
                    for kc in range(8):
                        mm(bU[:, 0:N], Uv[:, kc, j * 128:(j + 1) * 128], hT[:, kc, 0:N], False, [rk, ("hT", kc)], [brGU])
                    sgt = sg[c % 2]
                    act(sgt[:, 0:N], bG[:, 0:N], AF.Tanh, [brGU], [("sg", c % 2)], scale=0.5)
                    stt(sgt[:, 0:N], sgt[:, 0:N], 1.0, bG[:, 0:N], ALU.add, ALU.mult, [("sg", c % 2), brGU], [("sg", c % 2)])
                    tt(actT[:, c, 0:N], sgt[:, 0:N], bU[:, 0:N], ALU.mult, [("sg", c % 2), brGU], [("actT", c)])
                    yield
            for g in range(8):
                R, rk = next_slab("down", l, g)
                Rv = R[:, 0:NFF * 128].rearrange("p (k c) -> p k c", k=NFF)
                c = g
                bk, br = B(pb())
                for kc in range(NFF):
                    mm(bk[:, 0:N], Rv[:, kc, :], actT[:, kc, 0:N], kc == 0, [rk, ("actT", kc)], [br])
                stt(xT[:, c, 0:N], bk[:, 0:N], 0.5, xT[:, c, 0:N], ALU.mult, ALU.add, [("xT", xi, c), br], [("xT", xi, c)])
                yield
            if l == L - 1:
                store_y(mode, st, N, xT, xi)
            yield

        return D1_stream(), Mh_stream(), Mt_stream(), D2_stream()

    sb_eglF = sb("egl_b", [128, 2, 8], BF16)
    sb_egl = sb_eglF[0:4]
    mset(sb_eglF[:], 0.0, ["eglb", "eglb2"])

    NTILE = SEQ // 128
    order = [(0, 0)] + [x for t in range(NTILE) for x in ((t + 1, 0), (t, 1))] + [(NTILE, 1)]

    def load_sample_state(l):
        barrier([("S", l), ("hist", l), ("Sb", l)])
        for s in range(2):
            dma("sp", S[:, l, s, :, :], sG[l, s].rearrange("h k v -> k h v"), [], [("S", l, s)], ("sload", l, s))
            P.add("act", lambda e, l=l, s=s: e.copy(out=Sb[:, l, s, :, :], in_=S[:, l, s, :, :]), reads=[("S", l, s)], writes=[("Sb", l, s)])
            for c in range(12):
                dma("sp", hist[:, l, c, s, :], sC[l, s, :, c * 128:(c + 1) * 128].rearrange("t p -> p t"), [], [("hist", l, c, s)], ("hload", l), slow=True)
        barrier([("hist", l)])

    prev = None
    pre = set()
    for k, (tile, l) in enumerate(order):
        if tile == NTILE:
            if getattr(P, "fence", None) is None:
                P.fence = len(P.ops)
            if prev is not None:
                interleave([prev[0]])
            load_sample_state(l)
            D1g, Mh, Mt, D2g = stl("s", 0, l, xTs[tile % 3], tile % 3, k % 2, preloaded=(k in pre))
            interleave([D1g])
        else:
            D1g, Mh, Mt, D2g = stl("p", tile, l, xTs[tile % 3], tile % 3, k % 2, preloaded=(k in pre))
            interleave([D1g] + ([prev[0]] if prev is not None else []))
        if k + 1 < len(order) and order[k + 1][1] == 0:
            nt = order[k + 1][0]
            dpool[0] = [3, 4, 5]
            if nt == NTILE:
                load_x("s", 0, 128, xTs[nt % 3], nt % 3)
            else:
                load_x("p", nt, 256, xTs[nt % 3], nt % 3)
            pre.add(k + 1)
        interleave(([prev[1]] if prev is not None else []) + [Mh])
        prev = (Mt, D2g)
    interleave([prev[0]])
    interleave([prev[1]])
    assert used[0] == len(slabs), (used[0], len(slabs))

    P.finalize()
    out_keys = sorted(store_keys, key=str)
    with nc.Block() as block:
        @block.sync
        def _(e):
            P.run_engine("sp", e, sems, dma_sems)
            for k in out_keys:
                e.wait_ge(dma_sems[k], 16 * P.dma_count[k])

        @block.scalar
        def _(e):
            P.run_engine("act", e, sems, dma_sems)

        @block.vector
        def _(e):
            P.run_engine("dve", e, sems, dma_sems)

        @block.gpsimd
        def _(e):
            P.run_engine("pool", e, sems, dma_sems)

        @block.tensor
        def _(e):
            P.run_engine("pe", e, sems, dma_sems)
    es.close()
    return nc, P


_CACHE = {}


def kernel(**inputs):
    if "nc" not in _CACHE:
        _CACHE["nc"] = build()
    nc, P = _CACHE["nc"]
    cst = _consts()
    wl = _layout_weights(inputs)
    f = lambda k: np.asarray(inputs[k], np.float32)
    xpa = f("x_prompt"); xsa = f("x_sample")
    ck = f("cache_swa_k").reshape(L, 16, 128, 128); cvv = f("cache_swa_v").reshape(L, 16, 128, 128)
    sg_ = f("state_gdn"); sc_ = f("state_gdn_conv")
    in_maps = []
    for c in range(NCORE):
        b = slice(2 * c, 2 * c + 2)
        m = {"xp": np.ascontiguousarray(xpa[b]), "xs": np.ascontiguousarray(xsa[b]),
             "cK": np.ascontiguousarray(ck[:, b]), "cV": np.ascontiguousarray(cvv[:, b]),
             "sG": np.ascontiguousarray(sg_[:, b]), "sC": np.ascontiguousarray(sc_[:, b])}
        m.update(wl); m.update(cst)
        in_maps.append(m)
    res = run_bass_kernel_spmd(nc, in_maps, core_ids=list(range(NCORE)))
    r = res.results
    cat = lambda k, ax: np.concatenate([np.asarray(r[c][k], np.float32) for c in range(NCORE)], axis=ax)
    yp = cat("yp", 0); ys = cat("ys", 0)
    kp = cat("o_kp", 1).reshape(L, 16, 128, 2, 64); vp = cat("o_vp", 1).reshape(L, 16, 128, 2, 64)
    gp = cat("o_gp", 1); cpp = cat("o_cp", 1)
    ks = cat("o_ks", 1).reshape(L, 16, 128, 2, 64); vs = cat("o_vs", 1).reshape(L, 16, 128, 2, 64)
    gs = cat("o_gs", 1); cs = cat("o_cs", 1)
    ms = cat("o_ms", 1).reshape(L, 16, TS, 4, 64)
    return (yp, ys, kp, vp, gp, cpp, ks, vs, gs, cs, ms)
```

```python
import numpy as np
from contextlib import ExitStack
import concourse.bass as bass
import concourse.mybir as mybir
from concourse.bass_utils import run_bass_kernel_spmd

F32 = mybir.dt.float32
BF16 = mybir.dt.bfloat16
AF = mybir.ActivationFunctionType
ALU = mybir.AluOpType
AX = mybir.AxisListType
COMPUTE = ("pe", "act", "dve", "pool")
NOPRUNE = False

D = 1024; L = 2; SEQ = 2048; TS = 64; DFF = 2816; NFF = 22
NCORE = 8
BIG = 30000.0


class Op:
    __slots__ = ("eng", "fn", "reads", "writes", "dma", "lidx", "waits", "need_inc",
                 "incval", "dmaval", "clock", "gidx", "cost", "start", "finish", "deps")


class Prog:
    def __init__(self):
        self.ops = []
        self.state = {}
        self.children = {}
        self.dma_count = {}

    def add(self, eng, fn, reads=(), writes=(), dma=None, cost=None):
        op = Op()
        op.eng = eng; op.fn = fn
        op.cost = cost
        op.reads = [tuple(r) if isinstance(r, (tuple, list)) else (r,) for r in reads]
        op.writes = [tuple(w) if isinstance(w, (tuple, list)) else (w,) for w in writes]
        op.dma = dma
        op.gidx = len(self.ops)
        op.need_inc = False
        self.ops.append(op)
        return op

    def _conflicts(self, key):
        out = []
        for n in range(1, len(key) + 1):
            k = key[:n]
            if k in self.state:
                out.append(k)
        for k in self.children.get(key, ()):
            if k != key and k in self.state:
                out.append(k)
        return out

    def _touch(self, key):
        if key not in self.state:
            self.state[key] = [None, []]
            for n in range(1, len(key)):
                self.children.setdefault(key[:n], set()).add(key)

    def _analyze(self):
        for op in self.ops:
            deps = {}

            def adddep(a, kind):
                if a is None or a is op:
                    return
                old = deps.get(a.gidx)
                if old is None or (kind == "RAW" and old[1] != "RAW"):
                    deps[a.gidx] = (a, kind)

            for r in op.reads:
                self._touch(r)
                for k in self._conflicts(r):
                    adddep(self.state[k][0], "RAW")
            for w in op.writes:
                self._touch(w)
                for k in self._conflicts(w):
                    st = self.state[k]
                    adddep(st[0], "WAW")
                    for rd in st[1]:
                        adddep(rd, "WAR")
            for r in op.reads:
                self.state[r][1].append(op)
            for w in op.writes:
                self.state[w] = [op, []]
                for k in list(self.children.get(w, ())):
                    if k in self.state:
                        self.state[k] = [op, []]
            op.waits = list(deps.values())

    @staticmethod
    def _needed(a, b, kind):
        if a.dma is not None or b.dma is not None:
            return True
        if a.eng != b.eng:
            return True
        if a.eng == "pe":
            return False
        return kind == "RAW"

    def _schedule(self):
        import heapq
        ops = self.ops
        n = len(ops)
        DEF = {"pe": 0.14, "act": 0.45, "dve": 0.4, "pool": 0.5, "sp": 0.1}
        succ = [[] for _ in range(n)]
        indeg = [0] * n
        fence = getattr(self, "fence", None)
        last_pre = {}
        prev_post = {}
        for op in ops:
            op.deps = [a for (a, kind) in op.waits]
            if fence is not None:
                if op.gidx < fence:
                    last_pre[op.eng] = op
                else:
                    if op.eng in prev_post:
                        op.deps.append(prev_post[op.eng])
                    else:
                        op.deps.extend(last_pre.values())
                    prev_post[op.eng] = op
                    op.deps = list({id(a): a for a in op.deps}.values())
            indeg[op.gidx] = len(op.deps)
            for a in op.deps:
                succ[a.gidx].append(op)
        engines = sorted(set(o.eng for o in ops))
        free_at = {e: 0.0 for e in engines}
        avail = {e: [] for e in engines}
        ready_t = [0.0] * n
        for op in ops:
            if indeg[op.gidx] == 0:
                heapq.heappush(avail[op.eng], op.gidx)
        done = 0
        HOP = 0.4
        while done < n:
            best = None
            for e in engines:
                h = avail[e]
                if not h:
                    continue
                t = free_at[e]
                cand = None; cand_key = None
                for gi in (h[:256] if len(h) > 256 else h):
                    rt = ready_t[gi]
                    key = (max(rt, t), gi)
                    if cand_key is None or key < cand_key:
                        cand_key = key; cand = gi
                if best is None or cand_key < best[0]:
                    best = (cand_key, e, cand)
            (st_, gi_), e, gi = best
            avail[e].remove(gi); heapq.heapify(avail[e])
            op = ops[gi]
            c = op.cost if op.cost is not None else DEF[op.eng]
            op.start = st_
            if op.dma is not None:
                free_at[e] = st_ + 0.08
                op.finish = st_ + c
            else:
                free_at[e] = st_ + c
                op.finish = st_ + c
            done += 1
            for sck in succ[gi]:
                k = sck.gidx
                lat = op.finish + (HOP if (sck.eng != op.eng or op.dma is not None) else 0.0)
                if lat > ready_t[k]:
                    ready_t[k] = lat
                indeg[k] -= 1
                if indeg[k] == 0:
                    heapq.heappush(avail[sck.eng], k)
        order = sorted(ops, key=lambda o: (o.start, o.gidx))
        self.sim_span = max(o.finish for o in ops)
        self.ops = order

    def finalize(self, reorder=True):
        self._analyze()
        if reorder:
            self._schedule()
        for op in self.ops:
            if op.dma is not None:
                self.dma_count[op.dma] = self.dma_count.get(op.dma, 0) + 1
                op.dmaval = 16 * self.dma_count[op.dma]
        engines = sorted(set(o.eng for o in self.ops))
        cnt = {e: 0 for e in engines}
        for op in self.ops:
            cnt[op.eng] += 1
            op.lidx = cnt[op.eng]
        known = {e: {} for e in engines}
        pos = {}
        for i_, op in enumerate(self.ops):
            pos[op.gidx] = i_
        for op in self.ops:
            kn = known[op.eng]
            need = []
            for (a, kind) in sorted(op.waits, key=lambda t: -pos[t[0].gidx]):
                if not self._needed(a, op, kind):
                    continue
                if a.dma is not None:
                    src = ("dma", a.dma); val = a.dmaval
                else:
                    src = a.eng; val = a.lidx
                if kn.get(src, 0) >= val and not NOPRUNE:
                    continue
                need.append(a)
                a.need_inc = True
                kn[src] = max(kn.get(src, 0), val)
                for s, v in a.clock.items():
                    if kn.get(s, 0) < v:
                        kn[s] = v
            op.waits = need
            op.clock = dict(kn)
        incc = {e: 0 for e in engines}
        for op in self.ops:
            if op.dma is None and op.need_inc:
                incc[op.eng] += 1
                op.incval = incc[op.eng]
        self.streams = {e: [] for e in engines}
        for op in self.ops:
            self.streams[op.eng].append(op)
        self.stats = {e: (cnt[e], incc[e]) for e in engines}

    def run_engine(self, e, engobj, sems, dma_sems):
        for op in self.streams.get(e, []):
            w = {}
            for a in op.waits:
                if a.dma is not None:
                    s = dma_sems[a.dma]; v = a.dmaval
                else:
                    s = sems[a.eng]; v = a.incval
                k = id(s)
                if k not in w or w[k][1] < v:
                    w[k] = (s, v)
            for (s, v) in w.values():
                engobj.wait_ge(s, v)
            ins = op.fn(engobj)
            if op.dma is not None:
                ins.then_inc(dma_sems[op.dma], 16)
            elif op.need_inc:
                ins.then_inc(sems[op.eng], 1)


def _consts():
    c = {}
    c["identF"] = np.eye(128, dtype=np.float32)
    half = (np.arange(128) // 64)
    same = half[:, None] == half[None, :]
    p = np.arange(128)[:, None]; f = np.arange(128)[None, :]
    mbI = np.where(same & (f >= p), 0.0, -BIG)
    mbS = np.where(same & (f > p), 0.0, -BIG)
    mbS2 = np.where(same & (p > f), 0.0, BIG)
    c["mb3"] = np.concatenate([mbI, mbS, mbS2], axis=1).astype(np.float32)
    c["ones64"] = np.where(same, 1.0 / 64, 0.0).astype(np.float32)
    ind = np.zeros((128, 4, 128), np.float32)
    for h in range(4):
        ind[:, h, h] = 1.0
    c["indH"] = ind.reshape(128, 512)
    sel = np.zeros((128, 4, 128), np.float32)
    for h in range(4):
        sel[h, h, :] = 1.0
    c["sel"] = sel.reshape(128, 512)
    m = np.ones((4, 256), np.float32); m[:, ::64] = 0.0
    c["scanmask"] = m
    cv = np.zeros((128, 8), np.float32)
    cv[:, 0] = 1e-6
    cv[:, 1] = np.log(128.0 ** -0.5)
    cv[:, 2] = 1.0
    c["cvals"] = cv
    return c


IN_OFF = dict(qkv=0, z=1536, b=2048, a=2052, u=2056, vm=2312, sq=2568, sk=2824, sv=2952)


def _layout_weights(inp):
    w_in = np.asarray(inp["w_in"], np.float32)
    cols = list(range(0, 2048))
    cols += list(range(IN_OFF["u"], IN_OFF["u"] + 256))
    sq = IN_OFF["sq"]
    cols += list(range(sq, sq + 64)) + list(range(sq + 128, sq + 192))
    cols += list(range(sq + 64, sq + 128)) + list(range(sq + 192, sq + 256))
    cols += list(range(IN_OFF["sk"], IN_OFF["sk"] + 128))
    wF = w_in[:, :, cols].reshape(L, 8, 128, 21 * 128).transpose(0, 2, 1, 3)
    colsT = list(range(IN_OFF["a"], IN_OFF["a"] + 4)) + list(range(IN_OFF["vm"], IN_OFF["vm"] + 256)) \
        + list(range(IN_OFF["b"], IN_OFF["b"] + 4)) + list(range(IN_OFF["sv"], IN_OFF["sv"] + 128))
    wT = w_in[:, :, colsT].reshape(L, 8, 128, 392).transpose(0, 2, 1, 3)
    out = {}
    out["w_inF"] = np.ascontiguousarray(wF)
    out["w_inT"] = np.ascontiguousarray(wT)
    out["w_outL"] = np.ascontiguousarray(np.asarray(inp["w_out"], np.float32).reshape(L, 8, 128, D).transpose(0, 2, 1, 3))
    out["w_gateL"] = np.ascontiguousarray(np.asarray(inp["ffn_w_gate"], np.float32).reshape(L, 8, 128, DFF).transpose(0, 2, 1, 3))
    out["w_upL"] = np.ascontiguousarray(np.asarray(inp["ffn_w_up"], np.float32).reshape(L, 8, 128, DFF).transpose(0, 2, 1, 3))
    out["w_downL"] = np.ascontiguousarray(np.asarray(inp["ffn_w_down"], np.float32).reshape(L, NFF, 128, D).transpose(0, 2, 1, 3))
    f = lambda k: np.asarray(inp[k], np.float32)
    out["g1c"] = np.ascontiguousarray(f("norm1_g").reshape(L, 8, 128).transpose(2, 0, 1).reshape(128, L * 8))
    out["g2c"] = np.ascontiguousarray(f("norm2_g").reshape(L, 8, 128).transpose(2, 0, 1).reshape(128, L * 8))
    out["convc"] = np.ascontiguousarray(f("gdn_conv_w").reshape(L, 4, 12, 128).transpose(3, 0, 2, 1).reshape(128, L * 48))
    out["alog"] = np.ascontiguousarray(f("gdn_a_log").T)
    out["dtb"] = np.ascontiguousarray(f("gdn_dt_bias").T)
    out["gng"] = np.ascontiguousarray(f("gdn_norm_g").T)
    out["lng"] = np.ascontiguousarray(f("mlp_ln_g").reshape(L, 256))
    out["lnb"] = np.ascontiguousarray(f("mlp_ln_b").reshape(L, 256))
    out["ws"] = np.ascontiguousarray(f("mlp_ws"))
    out["bs"] = np.ascontiguousarray(f("mlp_bs"))
    out["qng"] = np.ascontiguousarray(np.tile(f("swa_q_norm_g"), (1, 2)).T)
    out["kng"] = np.ascontiguousarray(np.tile(f("swa_k_norm_g"), (1, 2)).T)
    out["sinks"] = np.ascontiguousarray(f("swa_sinks"))
    return out


def build():
    nc = bass.Bass("TRN2", target_bir_lowering=False)
    P = Prog()
    es = ExitStack()

    def din(name, shape):
        return nc.dram_tensor(name, list(shape), F32, kind="ExternalInput").ap()

    def dout(name, shape):
        return nc.dram_tensor(name, list(shape), F32, kind="ExternalOutput").ap()

    xp = din("xp", [2, SEQ, D]); xs_ = din("xs", [2, TS, D])
    cK = din("cK", [L, 2, 128, 128]); cV = din("cV", [L, 2, 128, 128])
    sG = din("sG", [L, 2, 4, 128, 128]); sC = din("sC", [L, 2, 3, 1536])
    w_inF = din("w_inF", [L, 128, 8, 21 * 128]); w_inT = din("w_inT", [L, 128, 8, 392])
    w_outL = din("w_outL", [L, 128, 8, D]); w_gateL = din("w_gateL", [L, 128, 8, DFF])
    w_upL = din("w_upL", [L, 128, 8, DFF]); w_downL = din("w_downL", [L, 128, NFF, D])
    g1c = din("g1c", [128, L * 8]); g2c = din("g2c", [128, L * 8]); convc = din("convc", [128, L * 48])
    alog = din("alog", [4, L]); dtb = din("dtb", [4, L]); gng = din("gng", [128, L])
    lng = din("lng", [L, 256]); lnb = din("lnb", [L, 256]); ws = din("ws", [L, 4, 128, 128]); bs = din("bs", [L, 4, 128])
    qng = din("qng", [128, L]); kng = din("kng", [128, L]); sinks = din("sinks", [L, 4])
    c_ident = din("identF", [128, 128]); c_mb3 = din("mb3", [128, 384]); c_ones64 = din("ones64", [128, 128])
    c_indH = din("indH", [128, 512]); c_sel = din("sel", [128, 512]); c_scan = din("scanmask", [4, 256]); c_cv = din("cvals", [128, 8])

    yp = dout("yp", [2, SEQ, D]); ys = dout("ys", [2, TS, D])
    o_kp = dout("o_kp", [L, 2, 128, 128]); o_vp = dout("o_vp", [L, 2, 128, 128])
    o_gp = dout("o_gp", [L, 2, 4, 128, 128]); o_cp = dout("o_cp", [L, 2, 3, 1536])
    o_ks = dout("o_ks", [L, 2, 128, 128]); o_vs = dout("o_vs", [L, 2, 128, 128])
    o_gs = dout("o_gs", [L, 2, 4, 128, 128]); o_cs = dout("o_cs", [L, 2, 3, 1536])
    o_ms = dout("o_ms", [L, 2, TS, 256])

    def sb(name, shape, dt=F32):
        return es.enter_context(nc.sbuf_tensor("sb_" + name, list(shape), dt))

    banks = [es.enter_context(nc.psum_tensor("bank%d" % i, [128, 512], F32)) for i in range(8)]
    rr = [0]

    def pb():
        pool = dpool[0]
        rr[0] = (rr[0] + 1) % len(pool)
        return pool[rr[0]]

    def B(i):
        return banks[i], ("ps", i)

    rrA = [0]

    def pbA():
        i = rrA[0]
        rrA[0] = (rrA[0] + 1) % 3
        return i

    pbB = pbA
    dpool = [[3, 4, 5]]

    def interleave(gens):
        gens = list(gens)
        while gens:
            for g_ in list(gens):
                try:
                    next(g_)
                except StopIteration:
                    gens.remove(g_)

    sems = {e: es.enter_context(nc.semaphore("s_" + e)) for e in COMPUTE}
    dma_sems = {}

    def dsem(key):
        if key not in dma_sems:
            dma_sems[key] = es.enter_context(nc.semaphore("d%d" % len(dma_sems)))
        return key

    def fsz(ap):
        n = 1
        for d in ap.shape[1:]:
            n *= int(d)
        return n

    def mm(out, lhsT, rhs, start, r, w, tr=False):
        c = max(64, fsz(rhs)) * 0.00047 + 0.018
        if tr and rhs.dtype == F32:
            c *= 2.0
        P.add("pe", lambda e: e.matmul(out=out, lhsT=lhsT, rhs=rhs, start=start, stop=True, is_transpose=(True if tr else None)), reads=r, writes=w, cost=c)

    def act(out, in_, func, r, w, scale=None, bias=None):
        kw = {}
        if scale is not None:
            kw["scale"] = scale
        if bias is not None:
            kw["bias"] = bias
        P.add("act", lambda e: e.activation(out=out, in_=in_, func=func, **kw), reads=r, writes=w, cost=0.2 + fsz(out) * 0.00085)

    def tt(out, in0, in1, op, r, w, eng="dve"):
        P.add(eng, lambda e: e.tensor_tensor(out=out, in0=in0, in1=in1, op=op), reads=r, writes=w,
              cost=(0.14 + fsz(out) * 0.00125) if eng == "dve" else (0.15 + fsz(out) * 0.0021))

    def stt(out, in0, scalar, in1, op0, op1, r, w):
        P.add("dve", lambda e: e.scalar_tensor_tensor(out=out, in0=in0, scalar=scalar, in1=in1, op0=op0, op1=op1), reads=r, writes=w,
              cost=0.14 + fsz(out) * 0.00125)

    def ts(out, in0, s1, op0, r, w, s2=None, op1=None, eng="dve"):
        if op1 is None:
            P.add(eng, lambda e: e.tensor_scalar(out=out, in0=in0, scalar1=s1, scalar2=None, op0=op0), reads=r, writes=w, cost=0.13 + fsz(out) * 0.0009)
        else:
            P.add(eng, lambda e: e.tensor_scalar(out=out, in0=in0, scalar1=s1, scalar2=s2, op0=op0, op1=op1), reads=r, writes=w, cost=0.13 + fsz(out) * 0.0009)

    def cp(out, in_, r, w, eng="dve"):
        if eng == "act":
            P.add("act", lambda e: e.copy(out=out, in_=in_), reads=r, writes=w, cost=0.2 + fsz(out) * 0.00085)
        else:
            P.add(eng, lambda e: e.tensor_copy(out=out, in_=in_), reads=r, writes=w,
                  cost=(0.1 + fsz(out) * 0.0008) if eng == "dve" else (0.12 + fsz(out) * 0.0016))

    def mset(ap, val, w, eng="pool"):
        P.add(eng, lambda e: e.memset(ap, val), writes=w, cost=0.1 + fsz(ap) * 0.001)

    store_keys = set()

    def dma(eng, out, in_, r, w, key, store=False, slow=False):
        if store:
            store_keys.add(key)
        nbytes = fsz(out) * 128 * (2 if out.dtype == BF16 else 4)
        c = 2.2 + nbytes / 180e3
        if slow:
            P.add(eng, lambda e: e.dma_start(out=out, in_=in_, allow_slow_non_contiguous=True), reads=r, writes=w, dma=dsem(key), cost=c)
        else:
            P.add(eng, lambda e: e.dma_start(out=out, in_=in_), reads=r, writes=w, dma=dsem(key), cost=c)

    dummy = sb("dummyk", [128, 8])

    def barrier(keys, reads=()):
        P.add("pool", lambda e: e.memset(dummy[:], 0.0), reads=list(reads), writes=keys)

    SETUP = ("setup",)
    nset = [0]

    def setup_load(out, in_, eng="sp"):
        nset[0] += 1
        dma(eng, out, in_, [], [("setup", nset[0])], "setup")

    identF = sb("identF", [128, 128]); identB = sb("identB", [128, 128], BF16); negIB = sb("negIB", [128, 128], BF16)
    mb3 = sb("mb3", [128, 384], BF16)
    ones64 = sb("ones64", [128, 128], BF16)
    onesD = sb("onesD", [128, 128], BF16); onesDV = sb("onesDV", [128, 128], BF16); onesB = sb("onesB", [128, 128], BF16)
    indH = sb("indH", [128, 512], BF16)
    selB = sb("selB", [128, 512], BF16)
    scanm = sb("scanm", [4, 256]); cv = sb("cv", [128, 8])
    g1 = sb("g1", [128, L * 8]); g2 = sb("g2", [128, L * 8]); cw = sb("cw", [128, L * 48])
    alog_s = sb("alog_s", [4, L]); dtb_s = sb("dtb_s", [4, L]); nega = sb("nega", [4, L]); gng_s = sb("gng_s", [128, L])
    lng_bc = sb("lng_bc", [128, L, 256]); lnb_bc = sb("lnb_bc", [128, L, 256])
    qng_s = sb("qng_s", [128, L]); kng_s = sb("kng_s", [128, L]); esink = sb("esink", [128, L * 4])
    wsn = sb("wsn", [128, 128]); wsT = sb("wsT", [128, L, 2, 4, 128], BF16)
    bsbc = sb("bsbc", [128, L, 2, 2, 128])

    for (t, src) in ((mb3, c_mb3), (ones64, c_ones64), (indH, c_indH), (selB, c_sel)):
        setup_load(t[:], src, eng="pool")
    for (t, src) in ((identF, c_ident), (scanm, c_scan),
                     (cv, c_cv), (g1, g1c), (g2, g2c), (cw, convc), (alog_s, alog), (dtb_s, dtb), (gng_s, gng),
                     (qng_s, qng), (kng_s, kng)):
        setup_load(t[:], src)
    for l in range(L):
        setup_load(lng_bc[:, l, :], lng[l:l + 1, :].partition_broadcast(128))
        setup_load(lnb_bc[:, l, :], lnb[l:l + 1, :].partition_broadcast(128))
        setup_load(esink[:, l * 4:(l + 1) * 4], sinks[l:l + 1, :].partition_broadcast(128))
        for pr in range(2):
            for hh in range(2):
                h = 2 * pr + hh
                setup_load(bsbc[hh * 64:(hh + 1) * 64, l, 0, pr, :], bs[l, h:h + 1, :].partition_broadcast(64))
                for s in range(2):
                    setup_load(bsbc[hh * 64:(hh + 1) * 64, l, 1, pr, s * 64:(s + 1) * 64], bs[l, h:h + 1, 0:64].partition_broadcast(64))
    cp(identB[:], identF[:], [SETUP], ["identB"])
    ts(negIB[:], identF[:], -1.0, ALU.mult, [SETUP], ["negIB"])
    barrier([("mb3",), ("ones64",), ("indH",), ("selB",)], reads=[SETUP])
    mset(onesD[:], 1.0 / 1024, ["onesD"]); mset(onesDV[:], 1.0 / 128, ["onesDV"]); mset(onesB[:], 1.0, ["onesB"])
    act(nega[:], alog_s[:], AF.Exp, [SETUP], ["nega"])
    ts(nega[:], nega[:], -1.0, ALU.mult, ["nega"], ["nega"])
    act(esink[:], esink[:], AF.Exp, [SETUP], ["esinkx"])
    for l in range(L):
        for h in range(4):
            for var in range(2):
                key = dsem("wsn")
                if var == 0:
                    dma("sp", wsn[:], ws[l, h], [], ["wsn"], "wsn")
                    mset(wsn[0:64, 64:128], 0.0, ["wsn"])
                else:
                    mset(wsn[:], 0.0, ["wsn"])
                    dma("sp", wsn[0:64, 0:64], ws[l, h, 0:64, 0:64], [], [("wsn", 0)], "wsn")
                    dma("sp", wsn[64:128, 64:128], ws[l, h, 0:64, 0:64], [], [("wsn", 1)], "wsn")
                bi = pb(); bk, br = B(bi)
                mm(bk[:, 0:128], wsn[:], identF[:], True, ["wsn", SETUP], [br], tr=True)
                cp(wsT[:, l, var, h, :], bk[:, 0:128], [br], [("wsT", l, var, h)])

    xtok = [sb("xtok0", [128, D])] * 2
    xTs = [sb("xT%d" % i, [128, 8, 256]) for i in range(3)]
    hT = sb("hT", [128, 8, 256], BF16)
    sqb = [sb("sqb%d" % i, [128, 256], BF16) for i in range(2)]
    lnv = sb("lnv", [128, 256]); rstd = sb("rstd", [128, 256])
    raw = [sb("raw%d" % i, [128, 2, 131]) for i in range(2)]
    hist = sb("hist", [128, L, 12, 2, 3])
    cacc = [sb("cacc%d" % i, [128, 2, 128]) for i in range(2)]
    sl = sb("sl", [128, 12, 256], BF16)
    qh = sb("qh", [128, 4, 256], BF16); qg = sb("qg", [128, 4, 256], BF16); kh = sb("kh", [128, 4, 256], BF16)
    kn = sb("kn", [128, 4, 256], BF16); kdT = sb("kdT", [128, 4, 256], BF16); vbT = sb("vbT", [128, 4, 256], BF16)
    zgs = [sb("zg%d" % i, [128, 4, 256], BF16) for i in range(2)]; uT = sb("uT", [128, 2, 256])
    bvt = sb("bvt", [128, 2, 4, 128], BF16); kdt = sb("kdt", [128, 2, 4, 128], BF16)
    vnew = sb("vnew", [128, 2, 4, 128], BF16); tmpp = sb("tmpp", [128, 4, 128], BF16)
    ATP = sb("ATP", [128, 2, 4, 256], BF16)
    Qm = [[sb("Qm%d_%d" % (q, i), [128, 4, 128], BF16) for i in range(2)] for q in range(2)]
    Pm = [[sb("Pm%d_%d" % (q, i), [128, 4, 128], BF16) for i in range(2)] for q in range(2)]
    Rb = [sb("Rb%d" % q, [128, 4, 128], BF16) for q in range(2)]
    sqbW = sb("sqbW", [128, 256], BF16); lnvW = sb("lnvW", [128, 256]); rstdW = sb("rstdW", [128, 256])
    TT = sb("TT", [128, 2, 4, 128], BF16)
    E12 = sb("E12", [128, 2, 256]); E3 = sb("E3", [128, 2, 128])
    gcT = sb("gcT", [128, 2, 8])
    eglbc = sb("eglbc", [128, 4, 4])
    oT = sb("oT", [128, 4, 256]); OT = sb("OT", [128, 8, 256], BF16)
    actT = sb("actT", [128, NFF, 256], BF16); sg = [sb("sg%d" % i, [128, 256]) for i in range(2)]
    S = sb("S", [128, L, 2, 4, 128]); Sb = sb("Sb", [128, L, 2, 4, 128], BF16)
    rA = sb("rA", [4, 256]); rB_ = sb("rB", [4, 256]); rG = rA; rGC = sb("rGC", [4, 256]); rL2 = rB_
    rGCB = sb("rGCB", [4, 256]); rT0 = sb("rT0", [4, 256]); rT1 = sb("rT1", [4, 256]); rFK = sb("rFK", [4, 256]); rFQ = sb("rFQ", [4, 256])
    rhiF = sb("rhi", [128, 2, 256], BF16); rloF = sb("rlo", [128, 2, 256], BF16)
    facF = sb("fac", [128, 6, 256], BF16)
    rhi = rhiF[0:4]; rlo = rloF[0:4]; fac = facF[0:4]
    mset(rhiF[:], 0.0, ["rhi"]); mset(rloF[:], 0.0, ["rlo"]); mset(facF[:], 0.0, ["fac"])
    swx = sb("swx", [128, 3, 256])
    qAB = sb("qAB", [128, 2, 256], BF16)
    kTh = sb("kTh", [128, L, 2, 2, 128], BF16)
    kTf = sb("kTf", [128, 256])
    Vp = sb("Vp", [128, L, 2, 2, 2, 192], BF16)
    vf = sb("vf", [128, 2, 128])
    PT = [sb("PT%d" % i, [128, 4, 128], BF16) for i in range(2)]
    rec = sb("rec", [128, 4, 128])
    kout = sb("kout", [128, 128])
    tokS = sb("tokS", [128, 2, 388])
    vmg = sb("vmg", [128, 256]); xc = sb("xc", [128, 256]); sqv = vmg; st4 = sb("st4", [128, 8])
    vpad = sb("vpad", [128, 2, 4, 128], BF16); tmpb = sb("tmpb", [128, 128])
    NSLOT = 3; SLOT = 4096
    NSLAB = 28
    wscr = nc.dram_tensor("wscr", [L * NSLAB, 128, SLOT], BF16, kind="Internal").ap()
    ring = [sb("ring%d" % i, [128, SLOT], BF16) for i in range(NSLOT)]

    mset(hist[:], 0.0, ["hist"]); mset(S[:], 0.0, ["S"]); mset(Sb[:], 0.0, ["Sb"])
    mset(Vp[:], 0.0, ["Vp"]); mset(vpad[:], 0.0, ["vpad"]); mset(kTh[:], 0.0, ["kTh"])
    mset(tmpp[:], 0.0, ["tmpp"]); mset(vnew[:], 0.0, ["vnew"])

    SLAB_IDX = {}
    for g in range(1):
        SLAB_IDX[("inT", 0)] = 0
    for g in range(6):
        SLAB_IDX[("inF", g)] = 1 + g
    for g in range(2):
        SLAB_IDX[("out", g)] = 7 + g
    for g in range(11):
        SLAB_IDX[("gu", g)] = 9 + g
    for g in range(8):
        SLAB_IDX[("down", g)] = 20 + g

    def slab_list():
        ntile = SEQ // 128
        order = [(0, 0)] + [x for t in range(ntile) for x in ((t + 1, 0), (t, 1))] + [(ntile, 1)]
        d1 = lambda l: [("inT", l, 0)] + [("inF", l, g) for g in range(6)]
        d2 = lambda l: [("out", l, g) for g in range(2)] + [("gu", l, g) for g in range(11)] + [("down", l, g) for g in range(8)]
        lst = []
        prev = None
        for (tile, l) in order:
            lst += d1(l)
            if prev is not None:
                lst += d2(prev)
            prev = l
        lst += d2(prev)
        return lst

    slabs = slab_list()
    seen_slabs = set()
    issued = [0]
    used = [0]

    def issue_slab(i):
        kind, l, g = slabs[i]
        slot = i % NSLOT
        R = ring[slot]
        key = ("ring", slot)
        sid = l * NSLAB + SLAB_IDX[(kind, g)]
        first_use = sid not in seen_slabs
        seen_slabs.add(sid)
        if not first_use:
            dma("sp", R[:, :], wscr[sid], [("wscr", sid)], [("ring", slot, 0)], key)
            return
        if kind == "inF":
            n = min(4, 21 - 4 * g)
            dst = R[:, 0:8 * n * 128].rearrange("p (k c) -> p k c", k=8)
            dma("pool", dst, w_inF[l, :, :, g * 512:g * 512 + n * 128], [], [("ring", slot, 0)], key)
        elif kind == "inT":
            dst = R[:, 0:8 * 392].rearrange("p (k c) -> p k c", k=8)
            dma("pool", dst, w_inT[l], [], [("ring", slot, 0)], key)
        elif kind == "out":
            dst = R[:, 0:4096].rearrange("p (k c) -> p k c", k=8)
            dma("pool", dst, w_outL[l, :, :, g * 512:(g + 1) * 512], [], [("ring", slot, 0)], key)
        elif kind == "gu":
            dst = R[:, 0:2048].rearrange("p (k c) -> p k c", k=8)
            dma("pool", dst, w_gateL[l, :, :, g * 256:(g + 1) * 256], [], [("ring", slot, 0)], key)
            dst2 = R[:, 2048:4096].rearrange("p (k c) -> p k c", k=8)
            dma("pool", dst2, w_upL[l, :, :, g * 256:(g + 1) * 256], [], [("ring", slot, 1)], key)
        else:
            dst = R[:, 0:NFF * 128].rearrange("p (k c) -> p k c", k=NFF)
            dma("pool", dst, w_downL[l, :, :, g * 128:(g + 1) * 128], [], [("ring", slot, 0)], key)
        dma("sp", wscr[sid], R[:, :], [("ring", slot)], [("wscr", sid)], ("wscr_st", slot))

    def next_slab(kind, l, g):
        i = used[0]
        assert slabs[i] == (kind, l, g), (slabs[i], kind, l, g)
        while issued[0] < min(len(slabs), i + NSLOT):
            issue_slab(issued[0]); issued[0] += 1
        used[0] += 1
        slot = i % NSLOT
        return ring[slot], ("ring", slot)

    def rmsnorm(l, gcols, N, xT, xi):
        bi = pb(); bk, br = B(bi)
        for kc in range(8):
            if kc % 2 == 0:
                act(hT[:, kc, 0:N], xT[:, kc, 0:N], AF.Square, [("xT", xi, kc)], [("hT", kc)])
            else:
                tt(hT[:, kc, 0:N], xT[:, kc, 0:N], xT[:, kc, 0:N], ALU.mult, [("xT", xi, kc)], [("hT", kc)], eng="pool")
        for kc in range(8):
            mm(bk[:, 0:N], onesD[:], hT[:, kc, 0:N], kc == 0, [("hT", kc), "onesD"], [br])
        act(lnv[:, 0:N], bk[:, 0:N], AF.Ln, [br, SETUP], ["lnv"], bias=cv[:, 0:1])
        act(rstd[:, 0:N], lnv[:, 0:N], AF.Exp, ["lnv"], ["rstd"], scale=-0.5)
        for kc in range(8):
            stt(hT[:, kc, 0:N], xT[:, kc, 0:N], gcols[:, l * 8 + kc:l * 8 + kc + 1], rstd[:, 0:N], ALU.mult, ALU.mult,
                [("xT", xi, kc), "rstd", SETUP], [("hT", kc)])

    def load_x(mode, st, N, xT, xi):
        for q in range(2 if mode == "p" else 1):
            xt = xtok[q]
            if mode == "p":
                dma("sp", xt[:], xp[q, st * 128:(st + 1) * 128, :], [], [("xtok", 0)], ("xtok", 0))
            else:
                dma("sp", xt[:], xs_.rearrange("s t d -> (s t) d"), [], [("xtok", 0)], ("xtok", 0))
            for half in range(2):
                bi = pb(); bk, br = B(bi)
                for j in range(4):
                    kc = half * 4 + j
                    mm(bk[:, j * 128:(j + 1) * 128], xt[:, kc * 128:(kc + 1) * 128], identF[:], j == 0, [("xtok", 0), SETUP], [br], tr=True)
                dst = xT[:, half * 4:half * 4 + 4, q * 128:(q + 1) * 128]
                src = bk[:].rearrange("p (k c) -> p k c", k=4)
                if half == 0:
                    P.add("act", lambda e, dst=dst, src=src: e.copy(out=dst, in_=src), reads=[br], writes=[("xT", xi, half * 4 + j) for j in range(4)])
                else:
                    cp(dst, src, [br], [("xT", xi, half * 4 + j) for j in range(4)])

    def store_y(mode, st, N, xT, xi):
        for q in range(2 if mode == "p" else 1):
            xt = xtok[q]
            for half in range(2):
                bi = pb(); bk, br = B(bi)
                for j in range(4):
                    kc = half * 4 + j
                    mm(bk[:, j * 128:(j + 1) * 128], xT[:, kc, q * 128:(q + 1) * 128], identF[:], j == 0, [("xT", xi, kc), SETUP], [br], tr=True)
                if half == 0:
                    P.add("act", lambda e, xt=xt, bk=bk: e.copy(out=xt[:, 0:512], in_=bk[:]), reads=[br], writes=[("xtok", 0, 0)])
                else:
                    cp(xt[:, 512:1024], bk[:], [br], [("xtok", 0, 1)])
            if mode == "p":
                dma("sp", yp[q, st * 128:(st + 1) * 128, :], xt[:], [("xtok", 0)], [], ("xtok", 0), store=True)
            else:
                dma("sp", ys.rearrange("s t d -> (s t) d"), xt[:], [("xtok", 0)], [], ("xtok", 0), store=True)

    def stl(mode, st, l, xT, xi, par, preloaded=False):
        prompt = mode == "p"
        N = 256 if prompt else 128
        NT = 2 if prompt else 1
        T = 128 if prompt else 64
        last = prompt and st == SEQ // 128 - 1
        var = 0 if prompt else 1
        cur = st % 2 if prompt else 0
        hsl = 1 - cur
        nch = N // 64
        tokb = []; fq_units = []; alloc = [pb]; rows_part3 = None; factor_unit = None
        zg = zgs[par]; zgk = "zg%d" % par

        def D1_stream():
            nonlocal tokb, fq_units, rows_part3, factor_unit
            dpool[0] = [4, 5]
            if l == 0 and not preloaded:
                load_x(mode, st, N, xT, xi)
            rmsnorm(l, g1, N, xT, xi)
            yield

            R, rk = next_slab("inT", l, 0)
            Rv = R[:, 0:8 * 392].rearrange("p (k c) -> p k c", k=8)
            ba, bra = B(pb()); bb_, brb = B(pb())
            for kc in range(8):
                mm(ba[:, 0:N], Rv[:, kc, 0:128], hT[:, kc, 0:N], kc == 0, [rk, ("hT", kc)], [bra])
            for kc in range(8):
                mm(bb_[:, 0:N], Rv[:, kc, 260:388], hT[:, kc, 0:N], kc == 0, [rk, ("hT", kc)], [brb])
            act(rA[:, 0:N], ba[0:4, 0:N], AF.Exp, [bra, SETUP], ["rA"], bias=dtb_s[:, l:l + 1])
            act(rB_[:, 0:N], bb_[0:4, 0:N], AF.Exp, [brb], ["rB"], scale=-1.0)
            act(rA[:, 0:N], rA[:, 0:N], AF.Ln, ["rA", SETUP], ["rA"], bias=cv[0:4, 2:3])
            act(rL2[:, 0:N], rB_[:, 0:N], AF.Ln, ["rB", SETUP], ["rB", "rB"], bias=cv[0:4, 2:3])
            tokb = []
            yield
            for q in range(NT):
                bt, brt = B(pb())
                for kc in range(8):
                    mm(bt[:, 0:388], hT[:, kc, q * 128:(q + 1) * 128], Rv[:, kc, 4:392], kc == 0, [rk, ("hT", kc)], [brt])
                P.add("act", lambda e, q=q, bt=bt: e.copy(out=tokS[:, q, :], in_=bt[:, 0:388]), reads=[brt], writes=[("tokS", q)])
                tokb.append((tokS[:, q, :], ("tokS", q)))
            ts(rG[:, 0:N], rA[:, 0:N], nega[:, l:l + 1], ALU.mult, ["rA", "nega"], ["rA", "rA"])
            P.add("dve", lambda e: e.tensor_tensor_scan(out=rGC[:, 0:N], data0=scanm[:, 0:N], data1=rG[:, 0:N], initial=0.0,
                                                        op0=ALU.mult, op1=ALU.add), reads=["rA", SETUP], writes=["rGC"])
            tt(rGCB[:, 0:N], rGC[:, 0:N], rL2[:, 0:N], ALU.subtract, ["rGC", "rB"], ["rGCB"])
            for i, (src, nm) in enumerate(((rGC, "rGC"), (rGCB, "rGCB"))):
                cp(rhi[:, i, 0:N], src[:, 0:N], [nm], [("rhi", i)])
                tt(rlo[:, i, 0:N], src[:, 0:N], rhi[:, i, 0:N], ALU.subtract, [nm, ("rhi", i)], [("rlo", i)])
            gc3 = rGC[:, 0:N].rearrange("p (c t) -> p c t", t=64)

            def rows_part2():
                act(rFQ[:, 0:N], bq[0:4, 0:N], AF.Ln, [brq, SETUP], ["rFQ"], bias=cv[0:4, 0:1])
                act(rFK[:, 0:N], bkk[0:4, 0:N], AF.Ln, [brk, SETUP], ["rFK"], bias=cv[0:4, 0:1])
                act(rFQ[:, 0:N], rFQ[:, 0:N], AF.Exp, ["rFQ", SETUP], ["rFQ"], scale=-0.5, bias=cv[0:4, 1:2])
                act(rFK[:, 0:N], rFK[:, 0:N], AF.Exp, ["rFK"], ["rFK"], scale=-0.5)
                act(rT0[:, 0:N], rGC[:, 0:N], AF.Exp, ["rGC"], ["rT0"])
                act(rT1[:, 0:N], rGCB[:, 0:N], AF.Exp, ["rGCB"], ["rT1"])
                act(fac[:, 5, 0:N], rL2[:, 0:N], AF.Exp, ["rB"], [("fac", 5)], scale=-1.0)
                cp(fac[:, 0, 0:N], rFQ[:, 0:N], ["rFQ"], [("fac", 0)])
                tt(fac[:, 1, 0:N], rFQ[:, 0:N], rT0[:, 0:N], ALU.mult, ["rFQ", "rT0"], [("fac", 1)])
                cp(fac[:, 2, 0:N], rFK[:, 0:N], ["rFK"], [("fac", 2)])
                stt(fac[:, 3, 0:N], rT1[:, 0:N], -1.0, rFK[:, 0:N], ALU.mult, ALU.mult, ["rT1", "rFK"], [("fac", 3)])
                tt(rT0[:, 0:N].rearrange("p (c t) -> p c t", t=64), gc3[:, :, 63:64].to_broadcast([4, nch, 64]), gc3, ALU.subtract,
                   ["rGC", "rT0"], ["rT0"])
                act(rT0[:, 0:N], rT0[:, 0:N], AF.Exp, ["rT0"], ["rT0"])
                tt(fac[:, 4, 0:N], rT0[:, 0:N], rFK[:, 0:N], ALU.mult, ["rT0", "rFK"], [("fac", 4)])
                act(rT1[:, 0:nch], gc3[:, :, 63], AF.Exp, ["rGC", "rT1"], ["rT1"])
                egl_b = sb_egl
                cp(egl_b[:, 0, 0:nch], rT1[:, 0:nch], ["rT1"], ["eglb"])
                tt(egl_b[:, 1, 0:nch], rT1[:, 0:nch], egl_b[:, 0, 0:nch], ALU.subtract, ["rT1", "eglb"], ["eglb2"])

            def rows_part3():
                egl_b = sb_egl
                for h in range(4):
                    bk, br = B(pbA())
                    mm(bk[:, 0:nch], selB[:, h * 128:(h + 1) * 128], sb_eglF[:, 0, 0:nch], True, ["selB", "eglb"], [br])
                    mm(bk[:, 0:nch], selB[:, h * 128:(h + 1) * 128], sb_eglF[:, 1, 0:nch], False, ["selB", "eglb2"], [br])
                    cp(eglbc[:, h, 0:nch], bk[:, 0:nch], [br], [("eglbc", h)])
                for q in range(NT):
                    bk, br = B(pbA())
                    mm(bk[:, 0:4], rGC[:, q * 128:(q + 1) * 128], identF[0:4, 0:4], True, ["rGC", SETUP], [br], tr=True)
                    mm(bk[:, 4:8], rGCB[:, q * 128:(q + 1) * 128], identF[0:4, 0:4], False, ["rGCB", SETUP], [br], tr=True)
                    ts(gcT[:, q, 0:4], bk[:, 0:4], -1.0, ALU.mult, [br], [("gcT", q)])
                    cp(gcT[:, q, 4:8], bk[:, 4:8], [br], [("gcT", q)])

            targets = ((qh, "qh", 0, 0), (qg, "qg", 0, 1), (kh, "kh", 4, 2), (kn, "kn", 4, 3), (kdT, "kdT", 4, 4))
            fq_units = [(dst, nm, c0, fi, h) for (dst, nm, c0, fi) in targets for h in range(4)]

            def factor_unit(u):
                dst, nm, c0, fi, h = u
                bk, br = B(alloc[0]())
                mm(bk[:, 0:N], selB[:, h * 128:(h + 1) * 128], facF[:, fi, 0:N], True, ["selB", ("fac", fi)], [br])
                tt(dst[:, h, 0:N], sl[:, c0 + h, 0:N], bk[:, 0:N], ALU.mult, [("sl", c0 + h), br], [(nm, h)])

            bq = banks[3][:, 0:256]; brq = ("ps", 3); bkk = banks[3][:, 256:512]; brk = ("ps", 3)
            pending = []
            deferred = []
            for g in range(6):
                R, rk = next_slab("inF", l, g)
                n = min(4, 21 - 4 * g)
                Rv = R[:, 0:8 * n * 128].rearrange("p (k c) -> p k c", k=8)
                for j in range(n):
                    c = 4 * g + j
                    bi = pb(); bk, br = B(bi)
                    for kc in range(8):
                        mm(bk[:, 0:N], Rv[:, kc, j * 128:(j + 1) * 128], hT[:, kc, 0:N], kc == 0, [rk, ("hT", kc)], [br])
                    for f_ in pending:
                        f_()
                    pending = []
                    if c < 12:
                        rb = raw[c % 2]; ca = cacc[c % 2]; rkx = ("raw", c % 2); ckx = ("cacc", c % 2)
                        cp(rb[:, :, 0:3], hist[:, l, c, :, :], [("hist", l, c)], [rkx], eng="pool")
                        act(rb[:, :, 3:3 + T], bk[:, 0:N].rearrange("p (s t) -> p s t", s=2), AF.Copy, [br], [rkx])
                        cp(hist[:, l, c, :, :], rb[:, :, T:T + 3], [rkx], [("hist", l, c)], eng="pool")
                    for f_ in deferred:
                        f_()
                    deferred = []
                    if c < 12:
                        def epi(c=c, rb=rb, ca=ca, rkx=rkx, ckx=ckx):
                            w0 = l * 48 + c * 4
                            ts(ca[:, :, 0:T], rb[:, :, 0:T], cw[:, w0:w0 + 1], ALU.mult, [rkx, SETUP], [ckx])
                            for tp in range(1, 4):
                                stt(ca[:, :, 0:T], rb[:, :, tp:tp + T], cw[:, w0 + tp:w0 + tp + 1], ca[:, :, 0:T], ALU.mult, ALU.add,
                                    [rkx, ckx, SETUP], [ckx])
                            act(sl[:, c, 0:N].rearrange("p (s t) -> p s t", s=2), ca[:, :, 0:T], AF.Silu, [ckx], [("sl", c)])
                            if c < 8:
                                s = sqb[c % 2]
                                tt(s[:, 0:N], sl[:, c, 0:N], sl[:, c, 0:N], ALU.mult, [("sl", c)], [("sqb", c % 2)], eng="pool")
                                h = c % 4
                                tgt, tr_ = (bq, brq) if c < 4 else (bkk, brk)

                                def ssq_mm(tgt=tgt, tr_=tr_, h=h, s=s, c=c):
                                    mm(tgt[:, 0:N], indH[:, h * 128:(h + 1) * 128], s[:, 0:N], h == 0, [("sqb", c % 2), "indH"], [tr_])
                                pending.append(ssq_mm)
                        deferred.append(epi)
                    elif c < 16:
                        act(zg[:, c - 12, 0:N], bk[:, 0:N], AF.Silu, [br], [(zgk, c - 12)])
                    elif c < 18:
                        act(uT[:, c - 16, 0:N], bk[:, 0:N], AF.Gelu_apprx_tanh, [br], [("uT", c - 16)])
                    else:
                        act(swx[:, c - 18, 0:N], bk[:, 0:N], AF.Copy, [br], [("swx", c - 18)])
                    if c == 10:
                        rows_part2()
                    yield
            for f_ in pending:
                f_()
            fq_units += [(vbT, "vbT", 8, 5, h) for h in range(4)]
            yield


        def fac_stream():
            alloc[0] = pbA
            rows_part3()
            yield
            while fq_units:
                factor_unit(fq_units.pop(0))
                if len(fq_units) % 2 == 0:
                    yield

        def gdn_pre():
            for q in range(NT):
                qc = slice(q * 128, (q + 1) * 128)
                bk, br = B(pbA())
                bkb = bk[:].bitcast(BF16)
                for h in range(4):
                    mm(bkb[:, h * 128:(h + 1) * 128], kdT[:, h, qc], identB[:], h == 0, [("kdT", h), "identB"], [br], tr=True)
                cp(kdt[:, q, :, :], bkb[:, 0:512].rearrange("p (h d) -> p h d", h=4), [br], [("kdt", q)])
                bk, br = B(pbA())
                bkb = bk[:].bitcast(BF16)
                for h in range(4):
                    mm(bkb[:, h * 128:(h + 1) * 128], vbT[:, h, qc], identB[:], h == 0, [("vbT", h), "identB"], [br], tr=True)
                P.add("act", lambda e, q=q, bkb=bkb: e.copy(out=bvt[:, q, :, :], in_=bkb[:, 0:512].rearrange("p (h d) -> p h d", h=4)),
                      reads=[br], writes=[("bvt", q)])
                yield
                for hp in range(2):
                    bG, brG = B(pbA())
                    bZ = [B(pbA()) for _ in range(2)]
                    for hh in range(2):
                        h = 2 * hp + hh
                        mm(bG[:, hh * 256:hh * 256 + 128], kh[:, h, qc], qh[:, h, qc], hh == 0, [("kh", h), ("qh", h)], [brG])
                        mm(bG[:, hh * 256 + 128:hh * 256 + 256], kh[:, h, qc], kh[:, h, qc], False, [("kh", h)], [brG])
                        z, zr = bZ[hh]
                        mm(z[:, 0:384], identB[:], mb3[:], True, ["identB", "mb3"], [zr])
                        sh = selB[:, h * 128:(h + 1) * 128]
                        for (o, i) in ((0, 0), (128, 1), (256, 0)):
                            mm(z[:, o:o + 128], sh, rhiF[:, i, qc], False, ["selB", ("rhi", i)], [zr])
                            mm(z[:, o:o + 128], sh, rloF[:, i, qc], False, ["selB", ("rlo", i)], [zr])
                        act(E12[:, hh, :], z[:, 0:256], AF.Exp, [zr, ("gcT", q)], [("E12", hh)], bias=gcT[:, q, h:h + 1])
                        act(E3[:, hh, :], z[:, 256:384], AF.Exp, [zr, ("gcT", q)], [("E3", hh)], scale=-1.0, bias=gcT[:, q, 4 + h:5 + h])
                    tt(ATP[:, q, 2 * hp:2 * hp + 2, :], bG[:].rearrange("p (h c) -> p h c", h=2), E12[:, :, :], ALU.mult,
                       [brG, "E12"], [("ATP", q, hp)])
                    tt(Qm[q][0][:, 2 * hp:2 * hp + 2, :], bG[:].rearrange("p (h c) -> p h c", h=2)[:, :, 128:256], E3[:, :, :], ALU.mult,
                       [brG, "E3"], [("Qm", q, 0, hp)])
                    yield
            for m in range(6):
                for q in range(NT):
                    bR, brR = banks[6 + q], ("ps", 6 + q)
                    Qc = Qm[q][m % 2]; Qk = ("Qm", q, m % 2)
                    if m == 0:
                        Pc = ATP[:, q, :, 128:256]; Pk = ("ATP", q)
                    else:
                        Pc = Pm[q][m % 2][:, :, :]; Pk = ("Pm", q, m % 2)
                    for h in range(4):
                        if m == 0:
                            mm(bR[:, h * 128:(h + 1) * 128], identB[:], identB[:], h == 0, ["identB"], [brR])
                            mm(bR[:, h * 128:(h + 1) * 128], Qc[:, h, :], negIB[:], False, [Qk, "negIB"], [brR])
                        else:
                            mm(bR[:, h * 128:(h + 1) * 128], Qc[:, h, :], Rb[q][:, h, :], False, [Qk, ("Rb", q)], [brR])
                    if m < 5:
                        bP, brP = B(pbA()); bQ, brQ = B(pbA())
                        for h in range(4):
                            if m < 4:
                                mm(bP[:, h * 128:(h + 1) * 128], Qc[:, h, :], Pc[:, h, :], h == 0, [Qk, Pk], [brP])
                            mm(bQ[:, h * 128:(h + 1) * 128], Pc[:, h, :], Qc[:, h, :], h == 0, [Qk, Pk], [brQ])
                        nx = (m + 1) % 2
                        if m < 4:
                            P.add("act", lambda e, nx=nx, bP=bP, q=q: e.copy(out=Pm[q][nx][:, :, :], in_=bP[:].rearrange("p (h c) -> p h c", h=4)),
                                  reads=[brP], writes=[("Pm", q, nx)])
                        cp(Qm[q][nx][:, :, :], bQ[:].rearrange("p (h c) -> p h c", h=4), [brQ], [("Qm", q, nx)])
                        P.add("act", lambda e, bR=bR, q=q: e.copy(out=Rb[q][:, :, :], in_=bR[:].rearrange("p (h c) -> p h c", h=4)),
                              reads=[brR], writes=[("Rb", q)])
                    else:
                        cp(TT[:, q, :, :], bR[:].rearrange("p (h c) -> p h c", h=4), [brR], [("TT", q)])
                    yield

        def gdn_chain():
            if prompt:
                steps = [(s, s, hf) for hf in range(2) for s in range(2)]
            else:
                steps = [(s, 0, s) for s in range(2)]
            for (s, q, hf) in steps:
                rows = slice(hf * 64, (hf + 1) * 64)
                c0 = q * 128 + hf * 64
                ch = c0 // 64
                qc = slice(q * 128, (q + 1) * 128)
                Sk = ("S", l, s); Sbk = ("Sb", l, s)
                bKS, brKS = B(pbA()); bO, brO = B(pbA())
                for h in range(4):
                    mm(bKS[:, h * 128:(h + 1) * 128], kn[:, h, qc], Sb[:, l, s, h, :], h == 0, [("kn", h), Sbk], [brKS])
                for h in range(4):
                    mm(bO[:, h * 64:(h + 1) * 64], Sb[:, l, s, h, :], qg[:, h, c0:c0 + 64], h == 0, [("qg", h), Sbk], [brO])
                tt(tmpp[rows, :, :], bKS[rows, :].rearrange("p (h d) -> p h d", h=4), bvt[rows, q, :, :], ALU.add,
                   [brKS, ("bvt", q)], [("tmpp", hf)])
                yield
                bV, brV = B(pbA())
                for h in range(4):
                    mm(bV[:, h * 128:(h + 1) * 128], TT[rows, q, h, :], tmpp[rows, h, :], h == 0, [("TT", q), ("tmpp", hf)], [brV])
                P.add("act", lambda e, rows=rows, q=q, bV=bV: e.copy(out=vnew[rows, q, :, :], in_=bV[rows, :].rearrange("p (h d) -> p h d", h=4)),
                      reads=[brV], writes=[("vnew", q, hf)])
                yield
                for h in range(4):
                    mm(bO[:, h * 64:(h + 1) * 64], vnew[rows, q, h, :], ATP[rows, q, h, hf * 64:hf * 64 + 64], False,
                       [("vnew", q, hf), ("ATP", q)], [brO])
                P.add("act", lambda e, c0=c0, bO=bO: e.copy(out=oT[:, :, c0:c0 + 64], in_=bO[:, 0:256].rearrange("p (h c) -> p h c", h=4)),
                      reads=[brO], writes=[("oT", ch)])
                bD, brD = B(pbA())
                for h in range(4):
                    mm(bD[:, h * 128:(h + 1) * 128], kdt[rows, q, h, :], vnew[rows, q, h, :], h == 0, [("kdt", q), ("vnew", q, hf)], [brD])
                for h in range(4):
                    stt(S[:, l, s, h, :], S[:, l, s, h, :], eglbc[:, h, ch:ch + 1], bD[:, h * 128:(h + 1) * 128], ALU.mult, ALU.add,
                        [Sk, ("eglbc", h), brD], [Sk])
                P.add("act", lambda e, s=s: e.copy(out=Sb[:, l, s, :, :], in_=S[:, l, s, :, :]), reads=[Sk], writes=[Sbk])
                yield
            for h in range(4):
                s_ = sqbW
                act(s_[:, 0:N], oT[:, h, 0:N], AF.Square, ["oT"], ["sqbW"])
                yield
                bk, br = B(pbA())
                mm(bk[:, 0:N], onesDV[:], s_[:, 0:N], True, ["sqbW", "onesDV"], [br])
                act(lnvW[:, 0:N], bk[:, 0:N], AF.Ln, [br, SETUP], ["lnvW"], bias=cv[:, 0:1])
                act(rstdW[:, 0:N], lnvW[:, 0:N], AF.Exp, ["lnvW"], ["rstdW"], scale=-0.5)
                stt(lnvW[:, 0:N], oT[:, h, 0:N], gng_s[:, l:l + 1], rstdW[:, 0:N], ALU.mult, ALU.mult, ["oT", "rstdW", "lnvW", SETUP], ["lnvW"])
                tt(OT[:, h, 0:N], lnvW[:, 0:N], zg[:, h, 0:N], ALU.mult, ["lnvW", (zgk, h)], [("OT", h)])
                yield

        def gmlp():
            for q in range(NT):
                bt, brt = tokb[q]
                act(vmg[:], bt[:, 0:256], AF.Gelu_apprx_tanh, [brt], ["vmg"])
                if prompt:
                    P.add("act", lambda e, q=q, bt=bt: e.copy(out=Vp[:, l, q, cur, :, 64:128], in_=bt[:, 260:388].rearrange("p (g d) -> p g d", g=2)),
                          reads=[brt], writes=[("Vp", l, q, cur)])
                else:
                    for s in range(2):
                        rws = slice(s * 64, (s + 1) * 64)
                        P.add("act", lambda e, s=s, rws=rws, bt=bt: e.copy(out=Vp[rws, l, s, 1, :, 64:128], in_=bt[rws, 260:388].rearrange("p (g d) -> p g d", g=2)),
                              reads=[brt], writes=[("Vp", l, s, 1)])
                if last or not prompt:
                    cp(vf[:, q, :], bt[:, 260:388], [brt], [("vf", q)])
                v3 = vmg[:].rearrange("p (h d) -> p h d", h=4)
                P.add("dve", lambda e, v3=v3: e.tensor_reduce(out=st4[:, 0:4], in_=v3, axis=AX.X, op=ALU.add), reads=["vmg"], writes=[("st4", 0)])
                ts(st4[:, 0:4], st4[:, 0:4], -1.0 / 64, ALU.mult, [("st4", 0)], [("st4", 0)])
                x3 = xc[:].rearrange("p (h d) -> p h d", h=4)
                tt(x3, v3, st4[:, 0:4].unsqueeze(2).to_broadcast([128, 4, 64]), ALU.add, ["vmg", ("st4", 0)], ["xc"])
                tt(sqv[:], xc[:], xc[:], ALU.mult, ["xc", "vmg"], ["vmg", "vmg"])
                P.add("dve", lambda e: e.tensor_reduce(out=st4[:, 4:8], in_=sqv[:].rearrange("p (h d) -> p h d", h=4), axis=AX.X, op=ALU.add),
                      reads=["vmg"], writes=[("st4", 1)])
                yield
                act(st4[:, 4:8], st4[:, 4:8], AF.Ln, [("st4", 1), SETUP], [("st4", 1)], scale=1.0 / 64, bias=cv[:, 0:1])
                act(st4[:, 4:8], st4[:, 4:8], AF.Exp, [("st4", 1)], [("st4", 1)], scale=-0.5)
                tt(x3, x3, st4[:, 4:8].unsqueeze(2).to_broadcast([128, 4, 64]), ALU.mult, ["xc", ("st4", 1)], ["xc"])
                tt(xc[:], xc[:], lng_bc[:, l, :], ALU.mult, ["xc", SETUP], ["xc"])
                tt(xc[:], xc[:], lnb_bc[:, l, :], ALU.add, ["xc", SETUP], ["xc"])
                if not prompt:
                    for s in range(2):
                        dma("sp", o_ms[l, s], xc[s * 64:(s + 1) * 64, :], ["xc"], [], "o_ms", store=True)
                vp4 = vpad[:, q, :, :].rearrange("p (a b) c -> p a b c", b=2)
                x4 = xc[:].rearrange("p (a b d) -> p a b d", a=2, b=2)
                cp(vp4[:, :, 0, 0:64], x4[:, :, 0, :], ["xc"], [("vpad", q, 0)])
                cp(vp4[:, :, 1, 64:128], x4[:, :, 1, :], ["xc"], [("vpad", q, 1)])
                yield
                for pr in range(2):
                    bk, br = B(pbB())
                    mm(bk[:, 0:128], vpad[:, q, 2 * pr, :], wsT[:, l, var, 2 * pr, :], True, [("vpad", q), ("wsT", l, var, 2 * pr)], [br])
                    mm(bk[:, 0:128], vpad[:, q, 2 * pr + 1, :], wsT[:, l, var, 2 * pr + 1, :], False, [("vpad", q), ("wsT", l, var, 2 * pr + 1)], [br])
                    tt(tmpb[:], bk[:, 0:128], bsbc[:, l, var, pr, :], ALU.add, [br, SETUP, "tmpb"], ["tmpb"])
                    tt(OT[:, 4 + pr, q * 128:(q + 1) * 128], uT[:, pr, q * 128:(q + 1) * 128], tmpb[:], ALU.mult, [("uT", pr), "tmpb"], [("OT", 4 + pr, q)])
                yield

        def swa():
            for i in range(3):
                s_ = sqbW
                act(s_[:, 0:N], swx[:, i, 0:N], AF.Square, [("swx", i)], ["sqbW"])
                bk, br = B(pbB())
                mm(bk[:, 0:N], ones64[:], s_[:, 0:N], True, ["sqbW", "ones64"], [br])
                act(lnvW[:, 0:N], bk[:, 0:N], AF.Ln, [br, SETUP], ["lnvW"], bias=cv[:, 0:1])
                act(rstdW[:, 0:N], lnvW[:, 0:N], AF.Exp, ["lnvW"], ["rstdW"], scale=-0.5)
                if i < 2:
                    stt(qAB[:, i, 0:N], swx[:, i, 0:N], qng_s[:, l:l + 1], rstdW[:, 0:N], ALU.mult, ALU.mult, [("swx", i), "rstdW", SETUP], [("qAB", i)])
                else:
                    stt(kTf[:, 0:N], swx[:, 2, 0:N], kng_s[:, l:l + 1], rstdW[:, 0:N], ALU.mult, ALU.mult, [("swx", 2), "rstdW", SETUP], ["kTf"])
                    if prompt:
                        for s in range(2):
                            cp(kTh[:, l, s, cur, :], kTf[:, s * 128:(s + 1) * 128], ["kTf"], [("kTh", l, s, cur)])
                    else:
                        cp(kTh[:, l, 0, 1, :], kTf[:, 0:128], ["kTf"], [("kTh", l, 0, 1)])
                yield
            if not prompt:
                for s in range(2):
                    dma("sp", kout[:], cK[l, s], [], ["kout"], "kout")
                    bk, br = B(pbB())
                    mm(bk[:, 0:128], kout[:], identF[:], True, ["kout", SETUP], [br], tr=True)
                    cp(kTh[:, l, s, 0, :], bk[:, 0:128], [br], [("kTh", l, s, 0)])
                    dma("pool", Vp[:, l, s, 0, :, 64:128], cV[l, s].rearrange("t (g d) -> t g d", g=2), [], [("Vp", l, s, 0)], ("vcache", s))
                    dma("sp", o_ks[l, s, 0:64, :], cK[l, s, 64:128, :], [], [], "o_cache", store=True)
                    dma("sp", o_vs[l, s, 0:64, :], cV[l, s, 64:128, :], [], [], "o_cache", store=True)
                    yield
        def swa_att():
            for s in range(2):
                bO, brO = banks[6], ("ps", 6)
                bS, brS = banks[7], ("ps", 7)
                if prompt:
                    qcols = slice(s * 128, (s + 1) * 128); NQ = 128
                    ktiles = ([("h", hsl)] if st > 0 else []) + [("c", cur)]
                else:
                    qcols = slice(s * 64, (s + 1) * 64); NQ = 64
                    ktiles = [("h", 0), ("c", 1)]
                first = True
                for ki, (kk, slot) in enumerate(ktiles):
                    pt = PT[ki % 2]; ptk = ("PT", ki % 2)
                    if prompt or kk == "h":
                        krows = slice(0, 128); kT_ = kTh[:, l, s, slot, :]; kkey = ("kTh", l, s, slot); vkey = ("Vp", l, s, slot)
                        vs_ = s; vslot = slot
                    else:
                        krows = slice(s * 64, (s + 1) * 64); kT_ = kTh[:, l, 0, 1, :]; kkey = ("kTh", l, 0, 1); vkey = ("Vp", l, s, 1)
                        vs_ = s; vslot = 1
                    for g in range(2):
                        bk, br = B(pbB())
                        gr = slice(g * 64, (g + 1) * 64)
                        for hh in range(2):
                            h = 2 * g + hh
                            mm(bk[:, hh * NQ:(hh + 1) * NQ], kT_[gr, :], qAB[gr, hh, qcols], hh == 0, [kkey, ("qAB", hh)], [br])
                        act(pt[:, 2 * g:2 * g + 2, 0:NQ], bk[:, 0:2 * NQ].rearrange("p (h c) -> p h c", h=2), AF.Exp, [br], [(ptk[0], ptk[1], g)], scale=0.125)
                    if prompt:
                        if kk == "h":
                            mset(pt[0:64, :, 64:128], 0.0, [ptk])
                        else:
                            mset(pt[64:128, :, 0:64], 0.0, [ptk])
                    yield
                    for h in range(4):
                        mm(bS[:, h * NQ:(h + 1) * NQ], onesB[krows, :], pt[krows, h, 0:NQ], first and h == 0, [ptk, "onesB"], [brS])
                    for h in range(4):
                        g = h // 2
                        lo = 64 if h % 2 == 0 else 0
                        mm(bO[:, g * NQ:(g + 1) * NQ], Vp[krows, l, vs_, vslot, g, lo:lo + 128], pt[krows, h, 0:NQ], first and h == 0, [ptk, vkey], [brO])
                    first = False
                    yield
                for h in range(4):
                    ts(rec[:, h, 0:NQ], bS[:, h * NQ:(h + 1) * NQ], esink[:, l * 4 + h:l * 4 + h + 1], ALU.add, [brS, "esinkx"], [("rec", h)])
                act(rec[:, :, 0:NQ], rec[:, :, 0:NQ], AF.Ln, ["rec"], ["rec"])
                act(rec[:, :, 0:NQ], rec[:, :, 0:NQ], AF.Exp, ["rec"], ["rec"], scale=-1.0)
                for g in range(2):
                    for hh in range(2):
                        rws = slice(hh * 64, (hh + 1) * 64)
                        tt(OT[rws, 6 + g, qcols], bO[rws, g * NQ:(g + 1) * NQ], rec[rws, 2 * g + hh, 0:NQ], ALU.mult, [brO, "rec"], [("OT", 6 + g, s, hh)])
                yield

        def outputs():
            if last or not prompt:
                for q in range(NT):
                    bk, br = B(pbA())
                    mm(bk[:, 0:128], kTf[:, q * 128:(q + 1) * 128], identF[:], True, ["kTf", SETUP], [br], tr=True)
                    cp(kout[:], bk[:, 0:128], [br, "kout"], ["kout"])
                    if prompt:
                        dma("sp", o_kp[l, q], kout[:], ["kout"], [], "kout", store=True)
                        dma("sp", o_vp[l, q], vf[:, q, :], [("vf", q)], [], ("o_v", q), store=True)
                    else:
                        for s in range(2):
                            dma("sp", o_ks[l, s, 64:128, :], kout[s * 64:(s + 1) * 64, :], ["kout"], [], "kout", store=True)
                            dma("sp", o_vs[l, s, 64:128, :], vf[s * 64:(s + 1) * 64, 0, :], [("vf", 0)], [], ("o_v", 0), store=True)
                for s in range(2):
                    og = o_gp if prompt else o_gs
                    dma("sp", og[l, s].rearrange("h k v -> k h v"), S[:, l, s, :, :], [("S", l, s)], [], "o_g", store=True)
                    oc = o_cp if prompt else o_cs
                    for c in range(12):
                        dma("sp", oc[l, s, :, c * 128:(c + 1) * 128].rearrange("t p -> p t"), hist[:, l, c, s, :], [("hist", l, c)], [], "o_c", store=True, slow=True)

        def Mh_stream():
            yield from fac_stream()
            yield from gdn_pre()
            yield from gmlp()
            yield from swa()

        def Mt_stream():
            yield from gdn_chain()
            yield from swa_att()
            outputs()
            yield

        def D2_stream():
            dpool[0] = [3, 4, 5]
            for g in range(2):
                R, rk = next_slab("out", l, g)
                Rv = R[:, 0:4096].rearrange("p (k c) -> p k c", k=8)
                for j in range(4):
                    c = 4 * g + j
                    bk, br = B(pb())
                    for kc in range(8):
                        mm(bk[:, 0:N], Rv[:, kc, j * 128:(j + 1) * 128], OT[:, kc, 0:N], kc == 0, [rk, ("OT", kc)], [br])
                    tt(xT[:, c, 0:N], xT[:, c, 0:N], bk[:, 0:N], ALU.add, [("xT", xi, c), br], [("xT", xi, c)])
            yield
            rmsnorm(l, g2, N, xT, xi)
            yield
            for g in range(11):
                R, rk = next_slab("gu", l, g)
                Gv = R[:, 0:2048].rearrange("p (k c) -> p k c", k=8)
                Uv = R[:, 2048:4096].rearrange("p (k c) -> p k c", k=8)
                for j in range(2):
                    c = 2 * g + j
                    bGU, brGU = B(pb())
                    bG = bGU[:, 0:256]; bU = bGU[:, 256:512]
                    for kc in range(8):
                        mm(bG[:, 0:N], Gv[:, kc, j * 128:(j + 1) * 128], hT[:, kc, 0:N], kc == 0, [rk, ("hT", kc)], [brGU])
                    for kc in range(8):
                        mm(bU[:, 0:N], Uv[:, kc, j * 128:(j + 1) * 128], hT[:, kc, 0:N], False, [rk, ("hT", kc)], [brGU])
                    sgt = sg[c % 2]
                    act(sgt[:, 0:N], bG[:, 0:N], AF.Tanh, [brGU], [("sg", c % 2)], scale=0.5)
                    stt(sgt[:, 0:N], sgt[:, 0:N], 1.0, bG[:, 0:N], ALU.add, ALU.mult, [("sg", c % 2), brGU], [("sg", c % 2)])
                    tt(actT[:, c, 0:N], sgt[:, 0:N], bU[:, 0:N], ALU.mult, [("sg", c % 2), brGU], [("actT", c)])
                    yield
            for g in range(8):
                R, rk = next_slab("down", l, g)
                Rv = R[:, 0:NFF * 128].rearrange("p (k c) -> p k c", k=NFF)
                c = g
                bk, br = B(pb())
                for kc in range(NFF):
                    mm(bk[:, 0:N], Rv[:, kc, :], actT[:, kc, 0:N], kc == 0, [rk, ("actT", kc)], [br])
                stt(xT[:, c, 0:N], bk[:, 0:N], 0.5, xT[:, c, 0:N], ALU.mult, ALU.add, [("xT", xi, c), br], [("xT", xi, c)])
                yield
            if l == L - 1:
                store_y(mode, st, N, xT, xi)
            yield

        return D1_stream(), Mh_stream(), Mt_stream(), D2_stream()

    sb_eglF = sb("egl_b", [128, 2, 8], BF16)
    sb_egl = sb_eglF[0:4]
    mset(sb_eglF[:], 0.0, ["eglb", "eglb2"])

    NTILE = SEQ // 128
    order = [(0, 0)] + [x for t in range(NTILE) for x in ((t + 1, 0), (t, 1))] + [(NTILE, 1)]

    def load_sample_state(l):
        barrier([("S", l), ("hist", l), ("Sb", l)])
        for s in range(2):
            dma("sp", S[:, l, s, :, :], sG[l, s].rearrange("h k v -> k h v"), [], [("S", l, s)], ("sload", l, s))
            P.add("act", lambda e, l=l, s=s: e.copy(out=Sb[:, l, s, :, :], in_=S[:, l, s, :, :]), reads=[("S", l, s)], writes=[("Sb", l, s)])
            for c in range(12):
                dma("sp", hist[:, l, c, s, :], sC[l, s, :, c * 128:(c + 1) * 128].rearrange("t p -> p t"), [], [("hist", l, c, s)], ("hload", l), slow=True)
        barrier([("hist", l)])

    prev = None
    pre = set()
    for k, (tile, l) in enumerate(order):
        if tile == NTILE:
            if prev is not None:
                interleave([prev[0]])
            load_sample_state(l)
            D1g, Mh, Mt, D2g = stl("s", 0, l, xTs[tile % 3], tile % 3, k % 2, preloaded=(k in pre))
            interleave([D1g])
        else:
            D1g, Mh, Mt, D2g = stl("p", tile, l, xTs[tile % 3], tile % 3, k % 2, preloaded=(k in pre))
            interleave([D1g] + ([prev[0]] if prev is not None else []))
        if k + 1 < len(order) and order[k + 1][1] == 0:
            nt = order[k + 1][0]
            dpool[0] = [3, 4, 5]
            if nt == NTILE:
                load_x("s", 0, 128, xTs[nt % 3], nt % 3)
            else:
                load_x("p", nt, 256, xTs[nt % 3], nt % 3)
            pre.add(k + 1)
        interleave(([prev[1]] if prev is not None else []) + [Mh])
        prev = (Mt, D2g)
    interleave([prev[0]])
    interleave([prev[1]])
    assert used[0] == len(slabs), (used[0], len(slabs))

    P.finalize()
    out_keys = sorted(store_keys, key=str)
    with nc.Block() as block:
        @block.sync
        def _(e):
            P.run_engine("sp", e, sems, dma_sems)
            for k in out_keys:
                e.wait_ge(dma_sems[k], 16 * P.dma_count[k])

        @block.scalar
        def _(e):
            P.run_engine("act", e, sems, dma_sems)

        @block.vector
        def _(e):
            P.run_engine("dve", e, sems, dma_sems)

        @block.gpsimd
        def _(e):
            P.run_engine("pool", e, sems, dma_sems)

        @block.tensor
        def _(e):
            P.run_engine("pe", e, sems, dma_sems)
    es.close()
    return nc, P


_CACHE = {}


def kernel(**inputs):
    if "nc" not in _CACHE:
        _CACHE["nc"] = build()
    nc, P = _CACHE["nc"]
    cst = _consts()
    wl = _layout_weights(inputs)
    f = lambda k: np.asarray(inputs[k], np.float32)
    xpa = f("x_prompt"); xsa = f("x_sample")
    ck = f("cache_swa_k").reshape(L, 16, 128, 128); cvv = f("cache_swa_v").reshape(L, 16, 128, 128)
    sg_ = f("state_gdn"); sc_ = f("state_gdn_conv")
    in_maps = []
    for c in range(NCORE):
        b = slice(2 * c, 2 * c + 2)
        m = {"xp": np.ascontiguousarray(xpa[b]), "xs": np.ascontiguousarray(xsa[b]),
             "cK": np.ascontiguousarray(ck[:, b]), "cV": np.ascontiguousarray(cvv[:, b]),
             "sG": np.ascontiguousarray(sg_[:, b]), "sC": np.ascontiguousarray(sc_[:, b])}
        m.update(wl); m.update(cst)
        in_maps.append(m)
    res = run_bass_kernel_spmd(nc, in_maps, core_ids=list(range(NCORE)))
    r = res.results
    cat = lambda k, ax: np.concatenate([np.asarray(r[c][k], np.float32) for c in range(NCORE)], axis=ax)
    yp = cat("yp", 0); ys = cat("ys", 0)
    kp = cat("o_kp", 1).reshape(L, 16, 128, 2, 64); vp = cat("o_vp", 1).reshape(L, 16, 128, 2, 64)
    gp = cat("o_gp", 1); cpp = cat("o_cp", 1)
    ks = cat("o_ks", 1).reshape(L, 16, 128, 2, 64); vs = cat("o_vs", 1).reshape(L, 16, 128, 2, 64)
    gs = cat("o_gs", 1); cs = cat("o_cs", 1)
    ms = cat("o_ms", 1).reshape(L, 16, TS, 4, 64)
    return (yp, ys, kp, vp, gp, cpp, ks, vs, gs, cs, ms)
```

```python
import numpy as np
from contextlib import ExitStack
import concourse.bass as bass
import concourse.mybir as mybir
from concourse.bass_utils import run_bass_kernel_spmd

F32 = mybir.dt.float32
BF16 = mybir.dt.bfloat16
AF = mybir.ActivationFunctionType
ALU = mybir.AluOpType
AX = mybir.AxisListType
COMPUTE = ("pe", "act", "dve", "pool")
NOPRUNE = False

D = 1024; L = 2; SEQ = 2048; TS = 64; DFF = 2816; NFF = 22
NCORE = 8
BIG = 30000.0


class Op:
    __slots__ = ("eng", "fn", "reads", "writes", "dma", "lidx", "waits", "need_inc",
                 "incval", "dmaval", "clock", "gidx", "cost", "start", "finish", "deps", "emb")


class Prog:
    def __init__(self):
        self.ops = []
        self.state = {}
        self.children = {}
        self.dma_count = {}

    def add(self, eng, fn, reads=(), writes=(), dma=None, cost=None, emb=False):
        op = Op()
        op.eng = eng; op.fn = fn
        op.cost = cost
        op.emb = emb
        op.reads = [tuple(r) if isinstance(r, (tuple, list)) else (r,) for r in reads]
        op.writes = [tuple(w) if isinstance(w, (tuple, list)) else (w,) for w in writes]
        op.dma = dma
        op.gidx = len(self.ops)
        op.need_inc = False
        self.ops.append(op)
        return op

    def _conflicts(self, key):
        out = []
        for n in range(1, len(key) + 1):
            k = key[:n]
            if k in self.state:
                out.append(k)
        for k in self.children.get(key, ()):
            if k != key and k in self.state:
                out.append(k)
        return out

    def _touch(self, key):
        if key not in self.state:
            self.state[key] = [None, []]
            for n in range(1, len(key)):
                self.children.setdefault(key[:n], set()).add(key)

    def _analyze(self):
        for op in self.ops:
            deps = {}

            def adddep(a, kind):
                if a is None or a is op:
                    return
                old = deps.get(a.gidx)
                if old is None or (kind == "RAW" and old[1] != "RAW"):
                    deps[a.gidx] = (a, kind)

            for r in op.reads:
                self._touch(r)
                for k in self._conflicts(r):
                    adddep(self.state[k][0], "RAW")
            for w in op.writes:
                self._touch(w)
                for k in self._conflicts(w):
                    st = self.state[k]
                    adddep(st[0], "WAW")
                    for rd in st[1]:
                        adddep(rd, "WAR")
            for r in op.reads:
                self.state[r][1].append(op)
            for w in op.writes:
                self.state[w] = [op, []]
                for k in list(self.children.get(w, ())):
                    if k in self.state:
                        self.state[k] = [op, []]
            op.waits = list(deps.values())

    @staticmethod
    def _needed(a, b, kind):
        if a.dma is not None or b.dma is not None:
            return True
        if a.eng != b.eng:
            return True
        if a.eng == "pe":
            return False
        return kind == "RAW"

    def _schedule(self):
        import heapq
        ops = self.ops
        n = len(ops)
        DEF = {"pe": 0.14, "act": 0.45, "dve": 0.4, "pool": 0.5, "sp": 0.1}
        succ = [[] for _ in range(n)]
        indeg = [0] * n
        fence = getattr(self, "fence", None)
        last_pre = {}
        prev_post = {}
        for op in ops:
            op.deps = [a for (a, kind) in op.waits]
            if fence is not None:
                if op.gidx < fence:
                    last_pre[op.eng] = op
                else:
                    if op.eng in prev_post:
                        op.deps.append(prev_post[op.eng])
                    else:
                        op.deps.extend(last_pre.values())
                    prev_post[op.eng] = op
                    op.deps = list({id(a): a for a in op.deps}.values())
            indeg[op.gidx] = len(op.deps)
            for a in op.deps:
                succ[a.gidx].append(op)
        engines = sorted(set(o.eng for o in ops))
        free_at = {e: 0.0 for e in engines}
        avail = {e: [] for e in engines}
        ready_t = [0.0] * n
        for op in ops:
            if indeg[op.gidx] == 0:
                heapq.heappush(avail[op.eng], op.gidx)
        done = 0
        HOP = 0.4
        while done < n:
            best = None
            for e in engines:
                h = avail[e]
                if not h:
                    continue
                t = free_at[e]
                cand = None; cand_key = None
                for gi in (h[:256] if len(h) > 256 else h):
                    rt = ready_t[gi]
                    key = (max(rt, t), gi)
                    if cand_key is None or key < cand_key:
                        cand_key = key; cand = gi
                if best is None or cand_key < best[0]:
                    best = (cand_key, e, cand)
            (st_, gi_), e, gi = best
            avail[e].remove(gi); heapq.heapify(avail[e])
            op = ops[gi]
            c = op.cost if op.cost is not None else DEF[op.eng]
            op.start = st_
            if op.dma is not None:
                free_at[e] = st_ + 0.08
                op.finish = st_ + c
            else:
                free_at[e] = st_ + c
                op.finish = st_ + c
            done += 1
            for sck in succ[gi]:
                k = sck.gidx
                lat = op.finish + (HOP if (sck.eng != op.eng or op.dma is not None) else 0.0)
                if lat > ready_t[k]:
                    ready_t[k] = lat
                indeg[k] -= 1
                if indeg[k] == 0:
                    heapq.heappush(avail[sck.eng], k)
        order = sorted(ops, key=lambda o: (o.start, o.gidx))
        self.sim_span = max(o.finish for o in ops)
        self.ops = order

    def finalize(self, reorder=True):
        self._analyze()
        if reorder:
            self._schedule()
        for op in self.ops:
            if op.dma is not None:
                self.dma_count[op.dma] = self.dma_count.get(op.dma, 0) + 1
                op.dmaval = 16 * self.dma_count[op.dma]
        engines = sorted(set(o.eng for o in self.ops))
        cnt = {e: 0 for e in engines}
        for op in self.ops:
            cnt[op.eng] += 1
            op.lidx = cnt[op.eng]
        known = {e: {} for e in engines}
        pos = {}
        for i_, op in enumerate(self.ops):
            pos[op.gidx] = i_
        for op in self.ops:
            kn = known[op.eng]
            need = []
            for (a, kind) in sorted(op.waits, key=lambda t: -pos[t[0].gidx]):
                if not self._needed(a, op, kind):
                    continue
                if a.dma is not None:
                    src = ("dma", a.dma); val = a.dmaval
                else:
                    src = a.eng; val = a.lidx
                if kn.get(src, 0) >= val and not NOPRUNE:
                    continue
                need.append(a)
                a.need_inc = True
                kn[src] = max(kn.get(src, 0), val)
                for s, v in a.clock.items():
                    if kn.get(s, 0) < v:
                        kn[s] = v
            op.waits = need
            op.clock = dict(kn)
        incc = {e: 0 for e in engines}
        for op in self.ops:
            if op.dma is None and op.need_inc:
                incc[op.eng] += 1
                op.incval = incc[op.eng]
        self.streams = {e: [] for e in engines}
        for op in self.ops:
            self.streams[op.eng].append(op)
        self.stats = {e: (cnt[e], incc[e]) for e in engines}

    def run_engine(self, e, engobj, sems, dma_sems):
        for op in self.streams.get(e, []):
            w = {}
            for a in op.waits:
                if a.dma is not None:
                    s = dma_sems[a.dma]; v = a.dmaval
                else:
                    s = sems[a.eng]; v = a.incval
                k = id(s)
                if k not in w or w[k][1] < v:
                    w[k] = (s, v)
            ws = list(w.values())
            emb = None
            if ws and op.dma is None and op.emb and e in ("act", "dve"):
                emb = ws.pop()
            for (s, v) in ws:
                engobj.wait_ge(s, v)
            ins = op.fn(engobj)
            if emb is not None:
                ins._wait_ge(emb[0], emb[1])
            if op.dma is not None:
                ins.then_inc(dma_sems[op.dma], 16)
            elif op.need_inc:
                ins.then_inc(sems[op.eng], 1)


def _consts():
    c = {}
    c["identF"] = np.eye(128, dtype=np.float32)
    half = (np.arange(128) // 64)
    same = half[:, None] == half[None, :]
    p = np.arange(128)[:, None]; f = np.arange(128)[None, :]
    mbI = np.where(same & (f >= p), 0.0, -BIG)
    mbS = np.where(same & (f > p), 0.0, -BIG)
    mbS2 = np.where(same & (p > f), 0.0, BIG)
    c["mb3"] = np.concatenate([mbI, mbS, mbS2], axis=1).astype(np.float32)
    c["ones64"] = np.where(same, 1.0 / 64, 0.0).astype(np.float32)
    ind = np.zeros((128, 4, 128), np.float32)
    for h in range(4):
        ind[:, h, h] = 1.0
    c["indH"] = ind.reshape(128, 512)
    sel = np.zeros((128, 4, 128), np.float32)
    for h in range(4):
        sel[h, h, :] = 1.0
    c["sel"] = sel.reshape(128, 512)
    m = np.ones((4, 256), np.float32); m[:, ::64] = 0.0
    c["scanmask"] = m
    cv = np.zeros((128, 8), np.float32)
    cv[:, 0] = 1e-6
    cv[:, 1] = np.log(128.0 ** -0.5)
    cv[:, 2] = 1.0
    c["cvals"] = cv
    return c


IN_OFF = dict(qkv=0, z=1536, b=2048, a=2052, u=2056, vm=2312, sq=2568, sk=2824, sv=2952)


def _layout_weights(inp):
    w_in = np.asarray(inp["w_in"], np.float32)
    cols = list(range(0, 2048))
    cols += list(range(IN_OFF["u"], IN_OFF["u"] + 256))
    sq = IN_OFF["sq"]
    cols += list(range(sq, sq + 64)) + list(range(sq + 128, sq + 192))
    cols += list(range(sq + 64, sq + 128)) + list(range(sq + 192, sq + 256))
    cols += list(range(IN_OFF["sk"], IN_OFF["sk"] + 128))
    wF = w_in[:, :, cols].reshape(L, 8, 128, 21 * 128).transpose(0, 2, 1, 3)
    colsT = list(range(IN_OFF["a"], IN_OFF["a"] + 4)) + list(range(IN_OFF["vm"], IN_OFF["vm"] + 256)) \
        + list(range(IN_OFF["b"], IN_OFF["b"] + 4)) + list(range(IN_OFF["sv"], IN_OFF["sv"] + 128))
    wT = w_in[:, :, colsT].reshape(L, 8, 128, 392).transpose(0, 2, 1, 3)
    out = {}
    out["w_inF"] = np.ascontiguousarray(wF)
    out["w_inT"] = np.ascontiguousarray(wT)
    out["w_outL"] = np.ascontiguousarray(np.asarray(inp["w_out"], np.float32).reshape(L, 8, 128, D).transpose(0, 2, 1, 3))
    out["w_gateL"] = np.ascontiguousarray(np.asarray(inp["ffn_w_gate"], np.float32).reshape(L, 8, 128, DFF).transpose(0, 2, 1, 3))
    out["w_upL"] = np.ascontiguousarray(np.asarray(inp["ffn_w_up"], np.float32).reshape(L, 8, 128, DFF).transpose(0, 2, 1, 3))
    out["w_downL"] = np.ascontiguousarray(np.asarray(inp["ffn_w_down"], np.float32).reshape(L, NFF, 128, D).transpose(0, 2, 1, 3))
    f = lambda k: np.asarray(inp[k], np.float32)
    out["g1c"] = np.ascontiguousarray(f("norm1_g").reshape(L, 8, 128).transpose(2, 0, 1).reshape(128, L * 8))
    out["g2c"] = np.ascontiguousarray(f("norm2_g").reshape(L, 8, 128).transpose(2, 0, 1).reshape(128, L * 8))
    out["convc"] = np.ascontiguousarray(f("gdn_conv_w").reshape(L, 4, 12, 128).transpose(3, 0, 2, 1).reshape(128, L * 48))
    out["alog"] = np.ascontiguousarray(f("gdn_a_log").T)
    out["dtb"] = np.ascontiguousarray(f("gdn_dt_bias").T)
    out["gng"] = np.ascontiguousarray(f("gdn_norm_g").T)
    out["lng"] = np.ascontiguousarray(f("mlp_ln_g").reshape(L, 256))
    out["lnb"] = np.ascontiguousarray(f("mlp_ln_b").reshape(L, 256))
    out["ws"] = np.ascontiguousarray(f("mlp_ws"))
    out["bs"] = np.ascontiguousarray(f("mlp_bs"))
    out["qng"] = np.ascontiguousarray(np.tile(f("swa_q_norm_g"), (1, 2)).T)
    out["kng"] = np.ascontiguousarray(np.tile(f("swa_k_norm_g"), (1, 2)).T)
    out["sinks"] = np.ascontiguousarray(f("swa_sinks"))
    return out


def build():
    nc = bass.Bass("TRN2", target_bir_lowering=False)
    P = Prog()
    es = ExitStack()

    def din(name, shape):
        return nc.dram_tensor(name, list(shape), F32, kind="ExternalInput").ap()

    def dout(name, shape):
        return nc.dram_tensor(name, list(shape), F32, kind="ExternalOutput").ap()

    xp = din("xp", [2, SEQ, D]); xs_ = din("xs", [2, TS, D])
    cK = din("cK", [L, 2, 128, 128]); cV = din("cV", [L, 2, 128, 128])
    sG = din("sG", [L, 2, 4, 128, 128]); sC = din("sC", [L, 2, 3, 1536])
    w_inF = din("w_inF", [L, 128, 8, 21 * 128]); w_inT = din("w_inT", [L, 128, 8, 392])
    w_outL = din("w_outL", [L, 128, 8, D]); w_gateL = din("w_gateL", [L, 128, 8, DFF])
    w_upL = din("w_upL", [L, 128, 8, DFF]); w_downL = din("w_downL", [L, 128, NFF, D])
    g1c = din("g1c", [128, L * 8]); g2c = din("g2c", [128, L * 8]); convc = din("convc", [128, L * 48])
    alog = din("alog", [4, L]); dtb = din("dtb", [4, L]); gng = din("gng", [128, L])
    lng = din("lng", [L, 256]); lnb = din("lnb", [L, 256]); ws = din("ws", [L, 4, 128, 128]); bs = din("bs", [L, 4, 128])
    qng = din("qng", [128, L]); kng = din("kng", [128, L]); sinks = din("sinks", [L, 4])
    c_ident = din("identF", [128, 128]); c_mb3 = din("mb3", [128, 384]); c_ones64 = din("ones64", [128, 128])
    c_indH = din("indH", [128, 512]); c_sel = din("sel", [128, 512]); c_scan = din("scanmask", [4, 256]); c_cv = din("cvals", [128, 8])

    yp = dout("yp", [2, SEQ, D]); ys = dout("ys", [2, TS, D])
    o_kp = dout("o_kp", [L, 2, 128, 128]); o_vp = dout("o_vp", [L, 2, 128, 128])
    o_gp = dout("o_gp", [L, 2, 4, 128, 128]); o_cp = dout("o_cp", [L, 2, 3, 1536])
    o_ks = dout("o_ks", [L, 2, 128, 128]); o_vs = dout("o_vs", [L, 2, 128, 128])
    o_gs = dout("o_gs", [L, 2, 4, 128, 128]); o_cs = dout("o_cs", [L, 2, 3, 1536])
    o_ms = dout("o_ms", [L, 2, TS, 256])

    def sb(name, shape, dt=F32):
        return es.enter_context(nc.sbuf_tensor("sb_" + name, list(shape), dt))

    banks = [es.enter_context(nc.psum_tensor("bank%d" % i, [128, 512], F32)) for i in range(8)]
    rr = [0]

    def pb():
        pool = dpool[0]
        rr[0] = (rr[0] + 1) % len(pool)
        return pool[rr[0]]

    def B(i):
        return banks[i], ("ps", i)

    rrA = [0]

    def pbA():
        i = rrA[0]
        rrA[0] = (rrA[0] + 1) % 3
        return i

    pbB = pbA
    dpool = [[3, 4, 5]]

    def interleave(gens):
        gens = list(gens)
        while gens:
            for g_ in list(gens):
                try:
                    next(g_)
                except StopIteration:
                    gens.remove(g_)

    sems = {e: es.enter_context(nc.semaphore("s_" + e)) for e in COMPUTE}
    dma_sems = {}

    def dsem(key):
        if key not in dma_sems:
            dma_sems[key] = es.enter_context(nc.semaphore("d%d" % len(dma_sems)))
        return key

    def fsz(ap):
        n = 1
        for d in ap.shape[1:]:
            n *= int(d)
        return n

    def mm(out, lhsT, rhs, start, r, w, tr=False):
        c = max(64, fsz(rhs)) * 0.00047 + 0.018
        if tr and rhs.dtype == F32:
            c *= 2.0
        P.add("pe", lambda e: e.matmul(out=out, lhsT=lhsT, rhs=rhs, start=start, stop=True, is_transpose=(True if tr else None)), reads=r, writes=w, cost=c)

    def act(out, in_, func, r, w, scale=None, bias=None):
        kw = {}
        if scale is not None:
            kw["scale"] = scale
        if bias is not None:
            kw["bias"] = bias
        P.add("act", lambda e: e.activation(out=out, in_=in_, func=func, **kw), reads=r, writes=w, cost=0.2 + fsz(out) * 0.00085, emb=True)

    def tt(out, in0, in1, op, r, w, eng="dve"):
        P.add(eng, lambda e: e.tensor_tensor(out=out, in0=in0, in1=in1, op=op), reads=r, writes=w,
              cost=(0.14 + fsz(out) * 0.00125) if eng == "dve" else (0.15 + fsz(out) * 0.0021), emb=True)

    def stt(out, in0, scalar, in1, op0, op1, r, w):
        P.add("dve", lambda e: e.scalar_tensor_tensor(out=out, in0=in0, scalar=scalar, in1=in1, op0=op0, op1=op1), reads=r, writes=w,
              cost=0.14 + fsz(out) * 0.00125, emb=True)

    def ts(out, in0, s1, op0, r, w, s2=None, op1=None, eng="dve"):
        if op1 is None:
            P.add(eng, lambda e: e.tensor_scalar(out=out, in0=in0, scalar1=s1, scalar2=None, op0=op0), reads=r, writes=w, cost=0.13 + fsz(out) * 0.0009, emb=True)
        else:
            P.add(eng, lambda e: e.tensor_scalar(out=out, in0=in0, scalar1=s1, scalar2=s2, op0=op0, op1=op1), reads=r, writes=w, cost=0.13 + fsz(out) * 0.0009, emb=True)

    def cp(out, in_, r, w, eng="dve"):
        if eng == "act":
            P.add("act", lambda e: e.copy(out=out, in_=in_), reads=r, writes=w, cost=0.2 + fsz(out) * 0.00085)
        else:
            P.add(eng, lambda e: e.tensor_copy(out=out, in_=in_), reads=r, writes=w,
                  cost=(0.1 + fsz(out) * 0.0008) if eng == "dve" else (0.12 + fsz(out) * 0.0016))

    def mset(ap, val, w, eng="pool"):
        P.add(eng, lambda e: e.memset(ap, val), writes=w, cost=0.1 + fsz(ap) * 0.001)

    store_keys = set()

    def dma(eng, out, in_, r, w, key, store=False, slow=False):
        if store:
            store_keys.add(key)
        nbytes = fsz(out) * 128 * (2 if out.dtype == BF16 else 4)
        c = 2.2 + nbytes / 180e3
        if slow:
            P.add(eng, lambda e: e.dma_start(out=out, in_=in_, allow_slow_non_contiguous=True), reads=r, writes=w, dma=dsem(key), cost=c)
        else:
            P.add(eng, lambda e: e.dma_start(out=out, in_=in_), reads=r, writes=w, dma=dsem(key), cost=c)

    dummy = sb("dummyk", [128, 8])

    def barrier(keys, reads=()):
        P.add("pool", lambda e: e.memset(dummy[:], 0.0), reads=list(reads), writes=keys)

    SETUP = ("setup",)
    nset = [0]

    def setup_load(out, in_, eng="sp"):
        nset[0] += 1
        dma(eng, out, in_, [], [("setup", nset[0])], "setup")

    identF = sb("identF", [128, 128]); identB = sb("identB", [128, 128], BF16); negIB = sb("negIB", [128, 128], BF16)
    mb3 = sb("mb3", [128, 384], BF16)
    ones64 = sb("ones64", [128, 128], BF16)
    onesD = sb("onesD", [128, 128], BF16); onesDV = sb("onesDV", [128, 128], BF16); onesB = sb("onesB", [128, 128], BF16)
    indH = sb("indH", [128, 512], BF16)
    selB = sb("selB", [128, 512], BF16)
    scanm = sb("scanm", [4, 256]); cv = sb("cv", [128, 8])
    g1 = sb("g1", [128, L * 8]); g2 = sb("g2", [128, L * 8]); cw = sb("cw", [128, L * 48])
    alog_s = sb("alog_s", [4, L]); dtb_s = sb("dtb_s", [4, L]); nega = sb("nega", [4, L]); gng_s = sb("gng_s", [128, L])
    lng_bc = sb("lng_bc", [128, L, 256]); lnb_bc = sb("lnb_bc", [128, L, 256])
    qng_s = sb("qng_s", [128, L]); kng_s = sb("kng_s", [128, L]); esink = sb("esink", [128, L * 4])
    wsn = sb("wsn", [128, 128]); wsT = sb("wsT", [128, L, 2, 4, 128], BF16)
    bsbc = sb("bsbc", [128, L, 2, 2, 128])

    for (t, src) in ((mb3, c_mb3), (ones64, c_ones64), (indH, c_indH), (selB, c_sel)):
        setup_load(t[:], src, eng="pool")
    for (t, src) in ((identF, c_ident), (scanm, c_scan),
                     (cv, c_cv), (g1, g1c), (g2, g2c), (cw, convc), (alog_s, alog), (dtb_s, dtb), (gng_s, gng),
                     (qng_s, qng), (kng_s, kng)):
        setup_load(t[:], src)
    for l in range(L):
        setup_load(lng_bc[:, l, :], lng[l:l + 1, :].partition_broadcast(128))
        setup_load(lnb_bc[:, l, :], lnb[l:l + 1, :].partition_broadcast(128))
        setup_load(esink[:, l * 4:(l + 1) * 4], sinks[l:l + 1, :].partition_broadcast(128))
        for pr in range(2):
            for hh in range(2):
                h = 2 * pr + hh
                setup_load(bsbc[hh * 64:(hh + 1) * 64, l, 0, pr, :], bs[l, h:h + 1, :].partition_broadcast(64))
                for s in range(2):
                    setup_load(bsbc[hh * 64:(hh + 1) * 64, l, 1, pr, s * 64:(s + 1) * 64], bs[l, h:h + 1, 0:64].partition_broadcast(64))
    cp(identB[:], identF[:], [SETUP], ["identB"])
    ts(negIB[:], identF[:], -1.0, ALU.mult, [SETUP], ["negIB"])
    barrier([("mb3",), ("ones64",), ("indH",), ("selB",)], reads=[SETUP])
    mset(onesD[:], 1.0 / 1024, ["onesD"]); mset(onesDV[:], 1.0 / 128, ["onesDV"]); mset(onesB[:], 1.0, ["onesB"])
    act(nega[:], alog_s[:], AF.Exp, [SETUP], ["nega"])
    ts(nega[:], nega[:], -1.0, ALU.mult, ["nega"], ["nega"])
    act(esink[:], esink[:], AF.Exp, [SETUP], ["esinkx"])
    for l in range(L):
        for h in range(4):
            for var in range(2):
                key = dsem("wsn")
                if var == 0:
                    dma("sp", wsn[:], ws[l, h], [], ["wsn"], "wsn")
                    mset(wsn[0:64, 64:128], 0.0, ["wsn"])
                else:
                    mset(wsn[:], 0.0, ["wsn"])
                    dma("sp", wsn[0:64, 0:64], ws[l, h, 0:64, 0:64], [], [("wsn", 0)], "wsn")
                    dma("sp", wsn[64:128, 64:128], ws[l, h, 0:64, 0:64], [], [("wsn", 1)], "wsn")
                bi = pb(); bk, br = B(bi)
                mm(bk[:, 0:128], wsn[:], identF[:], True, ["wsn", SETUP], [br], tr=True)
                cp(wsT[:, l, var, h, :], bk[:, 0:128], [br], [("wsT", l, var, h)])

    xtok = [sb("xtok0", [128, D])] * 2
    xTs = [sb("xT%d" % i, [128, 8, 256]) for i in range(3)]
    hT = sb("hT", [128, 8, 256], BF16)
    sqb = [sb("sqb%d" % i, [128, 256], BF16) for i in range(2)]
    lnv = sb("lnv", [128, 256]); rstd = sb("rstd", [128, 256])
    raw = [sb("raw%d" % i, [128, 2, 131]) for i in range(2)]
    hist = sb("hist", [128, L, 12, 2, 3])
    cacc = [sb("cacc%d" % i, [128, 2, 128]) for i in range(2)]
    sl = sb("sl", [128, 12, 256], BF16)
    qh = sb("qh", [128, 4, 256], BF16); qg = sb("qg", [128, 4, 256], BF16); kh = sb("kh", [128, 4, 256], BF16)
    kn = sb("kn", [128, 4, 256], BF16); kdT = sb("kdT", [128, 4, 256], BF16); vbT = sb("vbT", [128, 4, 256], BF16)
    zgs = [sb("zg%d" % i, [128, 4, 256], BF16) for i in range(2)]; uT = sb("uT", [128, 2, 256])
    bvt = sb("bvt", [128, 2, 4, 128], BF16); kdt = sb("kdt", [128, 2, 4, 128], BF16)
    vnew = sb("vnew", [128, 2, 4, 128], BF16); tmpp = sb("tmpp", [128, 4, 128], BF16)
    ATP = sb("ATP", [128, 2, 4, 256], BF16)
    Qm = [[sb("Qm%d_%d" % (q, i), [128, 4, 128], BF16) for i in range(2)] for q in range(2)]
    Pm = [[sb("Pm%d_%d" % (q, i), [128, 4, 128], BF16) for i in range(2)] for q in range(2)]
    Rb = [sb("Rb%d" % q, [128, 4, 128], BF16) for q in range(2)]
    sqbW = sb("sqbW", [128, 256], BF16); lnvW = sb("lnvW", [128, 256]); rstdW = sb("rstdW", [128, 256])
    TT = sb("TT", [128, 2, 4, 128], BF16)
    E12 = sb("E12", [128, 2, 256]); E3 = sb("E3", [128, 2, 128])
    gcT = sb("gcT", [128, 2, 8])
    eglbc = sb("eglbc", [128, 4, 4])
    oT = sb("oT", [128, 4, 256]); OT = sb("OT", [128, 8, 256], BF16)
    actT = sb("actT", [128, NFF, 256], BF16); sg = [sb("sg%d" % i, [128, 256]) for i in range(2)]
    S = sb("S", [128, L, 2, 4, 128]); Sb = sb("Sb", [128, L, 2, 4, 128], BF16)
    rA = sb("rA", [4, 256]); rB_ = sb("rB", [4, 256]); rG = rA; rGC = sb("rGC", [4, 256]); rL2 = rB_
    rGCB = sb("rGCB", [4, 256]); rT0 = sb("rT0", [4, 256]); rT1 = sb("rT1", [4, 256]); rFK = sb("rFK", [4, 256]); rFQ = sb("rFQ", [4, 256])
    rhiF = sb("rhi", [128, 2, 256], BF16); rloF = sb("rlo", [128, 2, 256], BF16)
    facF = sb("fac", [128, 6, 256], BF16)
    rhi = rhiF[0:4]; rlo = rloF[0:4]; fac = facF[0:4]
    mset(rhiF[:], 0.0, ["rhi"]); mset(rloF[:], 0.0, ["rlo"]); mset(facF[:], 0.0, ["fac"])
    swx = sb("swx", [128, 3, 256])
    qAB = sb("qAB", [128, 2, 256], BF16)
    kTh = sb("kTh", [128, L, 2, 2, 128], BF16)
    kTf = sb("kTf", [128, 256])
    Vp = sb("Vp", [128, L, 2, 2, 2, 192], BF16)
    vf = sb("vf", [128, 2, 128])
    PT = [sb("PT%d" % i, [128, 4, 128], BF16) for i in range(2)]
    rec = sb("rec", [128, 4, 128])
    kout = sb("kout", [128, 128])
    tokS = sb("tokS", [128, 2, 388])
    vmg = sb("vmg", [128, 256]); xc = sb("xc", [128, 256]); sqv = vmg; st4 = sb("st4", [128, 8])
    vpad = sb("vpad", [128, 2, 4, 128], BF16); tmpb = sb("tmpb", [128, 128])
    NSLOT = 3; SLOT = 4096
    NSLAB = 28
    wscr = nc.dram_tensor("wscr", [L * NSLAB, 128, SLOT], BF16, kind="Internal").ap()
    ring = [sb("ring%d" % i, [128, SLOT], BF16) for i in range(NSLOT)]

    mset(hist[:], 0.0, ["hist"]); mset(S[:], 0.0, ["S"]); mset(Sb[:], 0.0, ["Sb"])
    mset(Vp[:], 0.0, ["Vp"]); mset(vpad[:], 0.0, ["vpad"]); mset(kTh[:], 0.0, ["kTh"])
    mset(tmpp[:], 0.0, ["tmpp"]); mset(vnew[:], 0.0, ["vnew"])

    SLAB_IDX = {}
    for g in range(1):
        SLAB_IDX[("inT", 0)] = 0
    for g in range(6):
        SLAB_IDX[("inF", g)] = 1 + g
    for g in range(2):
        SLAB_IDX[("out", g)] = 7 + g
    for g in range(11):
        SLAB_IDX[("gu", g)] = 9 + g
    for g in range(8):
        SLAB_IDX[("down", g)] = 20 + g

    def slab_list():
        ntile = SEQ // 128
        order = [(0, 0)] + [x for t in range(ntile) for x in ((t + 1, 0), (t, 1))] + [(ntile, 1)]
        d1 = lambda l: [("inT", l, 0)] + [("inF", l, g) for g in range(6)]
        d2 = lambda l: [("out", l, g) for g in range(2)] + [("gu", l, g) for g in range(11)] + [("down", l, g) for g in range(8)]
        lst = []
        prev = None
        for (tile, l) in order:
            lst += d1(l)
            if prev is not None:
                lst += d2(prev)
            prev = l
        lst += d2(prev)
        return lst

    slabs = slab_list()
    seen_slabs = set()
    issued = [0]
    used = [0]

    def issue_slab(i):
        kind, l, g = slabs[i]
        slot = i % NSLOT
        R = ring[slot]
        key = ("ring", slot)
        sid = l * NSLAB + SLAB_IDX[(kind, g)]
        first_use = sid not in seen_slabs
        seen_slabs.add(sid)
        if not first_use:
            dma("sp", R[:, :], wscr[sid], [("wscr", sid)], [("ring", slot, 0)], key)
            return
        if kind == "inF":
            n = min(4, 21 - 4 * g)
            dst = R[:, 0:8 * n * 128].rearrange("p (k c) -> p k c", k=8)
            dma("pool", dst, w_inF[l, :, :, g * 512:g * 512 + n * 128], [], [("ring", slot, 0)], key)
        elif kind == "inT":
            dst = R[:, 0:8 * 392].rearrange("p (k c) -> p k c", k=8)
            dma("pool", dst, w_inT[l], [], [("ring", slot, 0)], key)
        elif kind == "out":
            dst = R[:, 0:4096].rearrange("p (k c) -> p k c", k=8)
            dma("pool", dst, w_outL[l, :, :, g * 512:(g + 1) * 512], [], [("ring", slot, 0)], key)
        elif kind == "gu":
            dst = R[:, 0:2048].rearrange("p (k c) -> p k c", k=8)
            dma("pool", dst, w_gateL[l, :, :, g * 256:(g + 1) * 256], [], [("ring", slot, 0)], key)
            dst2 = R[:, 2048:4096].rearrange("p (k c) -> p k c", k=8)
            dma("pool", dst2, w_upL[l, :, :, g * 256:(g + 1) * 256], [], [("ring", slot, 1)], key)
        else:
            dst = R[:, 0:NFF * 128].rearrange("p (k c) -> p k c", k=NFF)
            dma("pool", dst, w_downL[l, :, :, g * 128:(g + 1) * 128], [], [("ring", slot, 0)], key)
        dma("sp", wscr[sid], R[:, :], [("ring", slot)], [("wscr", sid)], ("wscr_st", slot))

    def next_slab(kind, l, g):
        i = used[0]
        assert slabs[i] == (kind, l, g), (slabs[i], kind, l, g)
        while issued[0] < min(len(slabs), i + NSLOT):
            issue_slab(issued[0]); issued[0] += 1
        used[0] += 1
        slot = i % NSLOT
        return ring[slot], ("ring", slot)

    def rmsnorm(l, gcols, N, xT, xi):
        bi = pb(); bk, br = B(bi)
        for kc in range(8):
            if kc % 2 == 0:
                act(hT[:, kc, 0:N], xT[:, kc, 0:N], AF.Square, [("xT", xi, kc)], [("hT", kc)])
            else:
                tt(hT[:, kc, 0:N], xT[:, kc, 0:N], xT[:, kc, 0:N], ALU.mult, [("xT", xi, kc)], [("hT", kc)], eng="pool")
        for kc in range(8):
            mm(bk[:, 0:N], onesD[:], hT[:, kc, 0:N], kc == 0, [("hT", kc), "onesD"], [br])
        act(lnv[:, 0:N], bk[:, 0:N], AF.Ln, [br, SETUP], ["lnv"], bias=cv[:, 0:1])
        act(rstd[:, 0:N], lnv[:, 0:N], AF.Exp, ["lnv"], ["rstd"], scale=-0.5)
        for kc in range(8):
            stt(hT[:, kc, 0:N], xT[:, kc, 0:N], gcols[:, l * 8 + kc:l * 8 + kc + 1], rstd[:, 0:N], ALU.mult, ALU.mult,
                [("xT", xi, kc), "rstd", SETUP], [("hT", kc)])

    def load_x(mode, st, N, xT, xi):
        for q in range(2 if mode == "p" else 1):
            xt = xtok[q]
            if mode == "p":
                dma("sp", xt[:], xp[q, st * 128:(st + 1) * 128, :], [], [("xtok", 0)], ("xtok", 0))
            else:
                dma("sp", xt[:], xs_.rearrange("s t d -> (s t) d"), [], [("xtok", 0)], ("xtok", 0))
            for half in range(2):
                bi = pb(); bk, br = B(bi)
                for j in range(4):
                    kc = half * 4 + j
                    mm(bk[:, j * 128:(j + 1) * 128], xt[:, kc * 128:(kc + 1) * 128], identF[:], j == 0, [("xtok", 0), SETUP], [br], tr=True)
                dst = xT[:, half * 4:half * 4 + 4, q * 128:(q + 1) * 128]
                src = bk[:].rearrange("p (k c) -> p k c", k=4)
                if half == 0:
                    P.add("act", lambda e, dst=dst, src=src: e.copy(out=dst, in_=src), reads=[br], writes=[("xT", xi, half * 4 + j) for j in range(4)])
                else:
                    cp(dst, src, [br], [("xT", xi, half * 4 + j) for j in range(4)])

    def store_y(mode, st, N, xT, xi):
        for q in range(2 if mode == "p" else 1):
            xt = xtok[q]
            for half in range(2):
                bi = pb(); bk, br = B(bi)
                for j in range(4):
                    kc = half * 4 + j
                    mm(bk[:, j * 128:(j + 1) * 128], xT[:, kc, q * 128:(q + 1) * 128], identF[:], j == 0, [("xT", xi, kc), SETUP], [br], tr=True)
                if half == 0:
                    P.add("act", lambda e, xt=xt, bk=bk: e.copy(out=xt[:, 0:512], in_=bk[:]), reads=[br], writes=[("xtok", 0, 0)])
                else:
                    cp(xt[:, 512:1024], bk[:], [br], [("xtok", 0, 1)])
            if mode == "p":
                dma("sp", yp[q, st * 128:(st + 1) * 128, :], xt[:], [("xtok", 0)], [], ("xtok", 0), store=True)
            else:
                dma("sp", ys.rearrange("s t d -> (s t) d"), xt[:], [("xtok", 0)], [], ("xtok", 0), store=True)

    def stl(mode, st, l, xT, xi, par, preloaded=False):
        prompt = mode == "p"
        N = 256 if prompt else 128
        NT = 2 if prompt else 1
        T = 128 if prompt else 64
        last = prompt and st == SEQ // 128 - 1
        var = 0 if prompt else 1
        cur = st % 2 if prompt else 0
        hsl = 1 - cur
        nch = N // 64
        tokb = []; fq_units = []; alloc = [pb]; rows_part3 = None; factor_unit = None
        zg = zgs[par]; zgk = "zg%d" % par

        def D1_stream():
            nonlocal tokb, fq_units, rows_part3, factor_unit
            dpool[0] = [4, 5]
            if l == 0 and not preloaded:
                load_x(mode, st, N, xT, xi)
            rmsnorm(l, g1, N, xT, xi)
            yield

            R, rk = next_slab("inT", l, 0)
            Rv = R[:, 0:8 * 392].rearrange("p (k c) -> p k c", k=8)
            ba, bra = B(pb()); bb_, brb = B(pb())
            for kc in range(8):
                mm(ba[:, 0:N], Rv[:, kc, 0:128], hT[:, kc, 0:N], kc == 0, [rk, ("hT", kc)], [bra])
            for kc in range(8):
                mm(bb_[:, 0:N], Rv[:, kc, 260:388], hT[:, kc, 0:N], kc == 0, [rk, ("hT", kc)], [brb])
            act(rA[:, 0:N], ba[0:4, 0:N], AF.Exp, [bra, SETUP], ["rA"], bias=dtb_s[:, l:l + 1])
            act(rB_[:, 0:N], bb_[0:4, 0:N], AF.Exp, [brb], ["rB"], scale=-1.0)
            act(rA[:, 0:N], rA[:, 0:N], AF.Ln, ["rA", SETUP], ["rA"], bias=cv[0:4, 2:3])
            act(rL2[:, 0:N], rB_[:, 0:N], AF.Ln, ["rB", SETUP], ["rB", "rB"], bias=cv[0:4, 2:3])
            tokb = []
            yield
            for q in range(NT):
                bt, brt = B(pb())
                for kc in range(8):
                    mm(bt[:, 0:388], hT[:, kc, q * 128:(q + 1) * 128], Rv[:, kc, 4:392], kc == 0, [rk, ("hT", kc)], [brt])
                P.add("act", lambda e, q=q, bt=bt: e.copy(out=tokS[:, q, :], in_=bt[:, 0:388]), reads=[brt], writes=[("tokS", q)])
                tokb.append((tokS[:, q, :], ("tokS", q)))
            ts(rG[:, 0:N], rA[:, 0:N], nega[:, l:l + 1], ALU.mult, ["rA", "nega"], ["rA", "rA"])
            P.add("dve", lambda e: e.tensor_tensor_scan(out=rGC[:, 0:N], data0=scanm[:, 0:N], data1=rG[:, 0:N], initial=0.0,
                                                        op0=ALU.mult, op1=ALU.add), reads=["rA", SETUP], writes=["rGC"])
            tt(rGCB[:, 0:N], rGC[:, 0:N], rL2[:, 0:N], ALU.subtract, ["rGC", "rB"], ["rGCB"])
            for i, (src, nm) in enumerate(((rGC, "rGC"), (rGCB, "rGCB"))):
                cp(rhi[:, i, 0:N], src[:, 0:N], [nm], [("rhi", i)])
                tt(rlo[:, i, 0:N], src[:, 0:N], rhi[:, i, 0:N], ALU.subtract, [nm, ("rhi", i)], [("rlo", i)])
            gc3 = rGC[:, 0:N].rearrange("p (c t) -> p c t", t=64)

            def rows_part2():
                act(rFQ[:, 0:N], bq[0:4, 0:N], AF.Ln, [brq, SETUP], ["rFQ"], bias=cv[0:4, 0:1])
                act(rFK[:, 0:N], bkk[0:4, 0:N], AF.Ln, [brk, SETUP], ["rFK"], bias=cv[0:4, 0:1])
                act(rFQ[:, 0:N], rFQ[:, 0:N], AF.Exp, ["rFQ", SETUP], ["rFQ"], scale=-0.5, bias=cv[0:4, 1:2])
                act(rFK[:, 0:N], rFK[:, 0:N], AF.Exp, ["rFK"], ["rFK"], scale=-0.5)
                act(rT0[:, 0:N], rGC[:, 0:N], AF.Exp, ["rGC"], ["rT0"])
                act(rT1[:, 0:N], rGCB[:, 0:N], AF.Exp, ["rGCB"], ["rT1"])
                act(fac[:, 5, 0:N], rL2[:, 0:N], AF.Exp, ["rB"], [("fac", 5)], scale=-1.0)
                cp(fac[:, 0, 0:N], rFQ[:, 0:N], ["rFQ"], [("fac", 0)])
                tt(fac[:, 1, 0:N], rFQ[:, 0:N], rT0[:, 0:N], ALU.mult, ["rFQ", "rT0"], [("fac", 1)])
                cp(fac[:, 2, 0:N], rFK[:, 0:N], ["rFK"], [("fac", 2)])
                stt(fac[:, 3, 0:N], rT1[:, 0:N], -1.0, rFK[:, 0:N], ALU.mult, ALU.mult, ["rT1", "rFK"], [("fac", 3)])
                tt(rT0[:, 0:N].rearrange("p (c t) -> p c t", t=64), gc3[:, :, 63:64].to_broadcast([4, nch, 64]), gc3, ALU.subtract,
                   ["rGC", "rT0"], ["rT0"])
                act(rT0[:, 0:N], rT0[:, 0:N], AF.Exp, ["rT0"], ["rT0"])
                tt(fac[:, 4, 0:N], rT0[:, 0:N], rFK[:, 0:N], ALU.mult, ["rT0", "rFK"], [("fac", 4)])
                act(rT1[:, 0:nch], gc3[:, :, 63], AF.Exp, ["rGC", "rT1"], ["rT1"])
                egl_b = sb_egl
                cp(egl_b[:, 0, 0:nch], rT1[:, 0:nch], ["rT1"], ["eglb"])
                tt(egl_b[:, 1, 0:nch], rT1[:, 0:nch], egl_b[:, 0, 0:nch], ALU.subtract, ["rT1", "eglb"], ["eglb2"])

            def rows_part3():
                egl_b = sb_egl
                for h in range(4):
                    bk, br = B(pbA())
                    mm(bk[:, 0:nch], selB[:, h * 128:(h + 1) * 128], sb_eglF[:, 0, 0:nch], True, ["selB", "eglb"], [br])
                    mm(bk[:, 0:nch], selB[:, h * 128:(h + 1) * 128], sb_eglF[:, 1, 0:nch], False, ["selB", "eglb2"], [br])
                    cp(eglbc[:, h, 0:nch], bk[:, 0:nch], [br], [("eglbc", h)])
                for q in range(NT):
                    bk, br = B(pbA())
                    mm(bk[:, 0:4], rGC[:, q * 128:(q + 1) * 128], identF[0:4, 0:4], True, ["rGC", SETUP], [br], tr=True)
                    mm(bk[:, 4:8], rGCB[:, q * 128:(q + 1) * 128], identF[0:4, 0:4], False, ["rGCB", SETUP], [br], tr=True)
                    ts(gcT[:, q, 0:4], bk[:, 0:4], -1.0, ALU.mult, [br], [("gcT", q)])
                    cp(gcT[:, q, 4:8], bk[:, 4:8], [br], [("gcT", q)])

            targets = ((qh, "qh", 0, 0), (qg, "qg", 0, 1), (kh, "kh", 4, 2), (kn, "kn", 4, 3), (kdT, "kdT", 4, 4))
            fq_units = [(dst, nm, c0, fi, h) for (dst, nm, c0, fi) in targets for h in range(4)]

            def factor_unit(u):
                dst, nm, c0, fi, h = u
                bk, br = B(alloc[0]())
                mm(bk[:, 0:N], selB[:, h * 128:(h + 1) * 128], facF[:, fi, 0:N], True, ["selB", ("fac", fi)], [br])
                tt(dst[:, h, 0:N], sl[:, c0 + h, 0:N], bk[:, 0:N], ALU.mult, [("sl", c0 + h), br], [(nm, h)])

            bq = banks[3][:, 0:256]; brq = ("ps", 3); bkk = banks[3][:, 256:512]; brk = ("ps", 3)
            pending = []
            deferred = []
            for g in range(6):
                R, rk = next_slab("inF", l, g)
                n = min(4, 21 - 4 * g)
                Rv = R[:, 0:8 * n * 128].rearrange("p (k c) -> p k c", k=8)
                for j in range(n):
                    c = 4 * g + j
                    bi = pb(); bk, br = B(bi)
                    for kc in range(8):
                        mm(bk[:, 0:N], Rv[:, kc, j * 128:(j + 1) * 128], hT[:, kc, 0:N], kc == 0, [rk, ("hT", kc)], [br])
                    for f_ in pending:
                        f_()
                    pending = []
                    if c < 12:
                        rb = raw[c % 2]; ca = cacc[c % 2]; rkx = ("raw", c % 2); ckx = ("cacc", c % 2)
                        cp(rb[:, :, 0:3], hist[:, l, c, :, :], [("hist", l, c)], [rkx], eng="pool")
                        act(rb[:, :, 3:3 + T], bk[:, 0:N].rearrange("p (s t) -> p s t", s=2), AF.Copy, [br], [rkx])
                        cp(hist[:, l, c, :, :], rb[:, :, T:T + 3], [rkx], [("hist", l, c)], eng="pool")
                    for f_ in deferred:
                        f_()
                    deferred = []
                    if c < 12:
                        def epi(c=c, rb=rb, ca=ca, rkx=rkx, ckx=ckx):
                            w0 = l * 48 + c * 4
                            ts(ca[:, :, 0:T], rb[:, :, 0:T], cw[:, w0:w0 + 1], ALU.mult, [rkx, SETUP], [ckx])
                            for tp in range(1, 4):
                                stt(ca[:, :, 0:T], rb[:, :, tp:tp + T], cw[:, w0 + tp:w0 + tp + 1], ca[:, :, 0:T], ALU.mult, ALU.add,
                                    [rkx, ckx, SETUP], [ckx])
                            act(sl[:, c, 0:N].rearrange("p (s t) -> p s t", s=2), ca[:, :, 0:T], AF.Silu, [ckx], [("sl", c)])
                            if c < 8:
                                s = sqb[c % 2]
                                tt(s[:, 0:N], sl[:, c, 0:N], sl[:, c, 0:N], ALU.mult, [("sl", c)], [("sqb", c % 2)], eng="pool")
                                h = c % 4
                                tgt, tr_ = (bq, brq) if c < 4 else (bkk, brk)

                                def ssq_mm(tgt=tgt, tr_=tr_, h=h, s=s, c=c):
                                    mm(tgt[:, 0:N], indH[:, h * 128:(h + 1) * 128], s[:, 0:N], h == 0, [("sqb", c % 2), "indH"], [tr_])
                                pending.append(ssq_mm)
                        deferred.append(epi)
                    elif c < 16:
                        act(zg[:, c - 12, 0:N], bk[:, 0:N], AF.Silu, [br], [(zgk, c - 12)])
                    elif c < 18:
                        act(uT[:, c - 16, 0:N], bk[:, 0:N], AF.Gelu_apprx_tanh, [br], [("uT", c - 16)])
                    else:
                        act(swx[:, c - 18, 0:N], bk[:, 0:N], AF.Copy, [br], [("swx", c - 18)])
                    if c == 10:
                        rows_part2()
                    yield
            for f_ in pending:
                f_()
            fq_units += [(vbT, "vbT", 8, 5, h) for h in range(4)]
            yield


        def fac_stream():
            alloc[0] = pbA
            rows_part3()
            yield
            while fq_units:
                factor_unit(fq_units.pop(0))
                if len(fq_units) % 2 == 0:
                    yield

        def gdn_pre():
            for q in range(NT):
                qc = slice(q * 128, (q + 1) * 128)
                bk, br = B(pbA())
                bkb = bk[:].bitcast(BF16)
                for h in range(4):
                    mm(bkb[:, h * 128:(h + 1) * 128], kdT[:, h, qc], identB[:], h == 0, [("kdT", h), "identB"], [br], tr=True)
                cp(kdt[:, q, :, :], bkb[:, 0:512].rearrange("p (h d) -> p h d", h=4), [br], [("kdt", q)])
                bk, br = B(pbA())
                bkb = bk[:].bitcast(BF16)
                for h in range(4):
                    mm(bkb[:, h * 128:(h + 1) * 128], vbT[:, h, qc], identB[:], h == 0, [("vbT", h), "identB"], [br], tr=True)
                P.add("act", lambda e, q=q, bkb=bkb: e.copy(out=bvt[:, q, :, :], in_=bkb[:, 0:512].rearrange("p (h d) -> p h d", h=4)),
                      reads=[br], writes=[("bvt", q)])
                yield
                for hp in range(2):
                    bG, brG = B(pbA())
                    bZ = [B(pbA()) for _ in range(2)]
                    for hh in range(2):
                        h = 2 * hp + hh
                        mm(bG[:, hh * 256:hh * 256 + 128], kh[:, h, qc], qh[:, h, qc], hh == 0, [("kh", h), ("qh", h)], [brG])
                        mm(bG[:, hh * 256 + 128:hh * 256 + 256], kh[:, h, qc], kh[:, h, qc], False, [("kh", h)], [brG])
                        z, zr = bZ[hh]
                        mm(z[:, 0:384], identB[:], mb3[:], True, ["identB", "mb3"], [zr])
                        sh = selB[:, h * 128:(h + 1) * 128]
                        for (o, i) in ((0, 0), (128, 1), (256, 0)):
                            mm(z[:, o:o + 128], sh, rhiF[:, i, qc], False, ["selB", ("rhi", i)], [zr])
                            mm(z[:, o:o + 128], sh, rloF[:, i, qc], False, ["selB", ("rlo", i)], [zr])
                        act(E12[:, hh, :], z[:, 0:256], AF.Exp, [zr, ("gcT", q)], [("E12", hh)], bias=gcT[:, q, h:h + 1])
                        act(E3[:, hh, :], z[:, 256:384], AF.Exp, [zr, ("gcT", q)], [("E3", hh)], scale=-1.0, bias=gcT[:, q, 4 + h:5 + h])
                    tt(ATP[:, q, 2 * hp:2 * hp + 2, :], bG[:].rearrange("p (h c) -> p h c", h=2), E12[:, :, :], ALU.mult,
                       [brG, "E12"], [("ATP", q, hp)])
                    tt(Qm[q][0][:, 2 * hp:2 * hp + 2, :], bG[:].rearrange("p (h c) -> p h c", h=2)[:, :, 128:256], E3[:, :, :], ALU.mult,
                       [brG, "E3"], [("Qm", q, 0, hp)])
                    yield
            for m in range(6):
                for q in range(NT):
                    bR, brR = banks[6 + q], ("ps", 6 + q)
                    Qc = Qm[q][m % 2]; Qk = ("Qm", q, m % 2)
                    if m == 0:
                        Pc = ATP[:, q, :, 128:256]; Pk = ("ATP", q)
                    else:
                        Pc = Pm[q][m % 2][:, :, :]; Pk = ("Pm", q, m % 2)
                    for h in range(4):
                        if m == 0:
                            mm(bR[:, h * 128:(h + 1) * 128], identB[:], identB[:], h == 0, ["identB"], [brR])
                            mm(bR[:, h * 128:(h + 1) * 128], Qc[:, h, :], negIB[:], False, [Qk, "negIB"], [brR])
                        else:
                            mm(bR[:, h * 128:(h + 1) * 128], Qc[:, h, :], Rb[q][:, h, :], False, [Qk, ("Rb", q)], [brR])
                    if m < 5:
                        bP, brP = B(pbA()); bQ, brQ = B(pbA())
                        for h in range(4):
                            if m < 4:
                                mm(bP[:, h * 128:(h + 1) * 128], Qc[:, h, :], Pc[:, h, :], h == 0, [Qk, Pk], [brP])
                            mm(bQ[:, h * 128:(h + 1) * 128], Pc[:, h, :], Qc[:, h, :], h == 0, [Qk, Pk], [brQ])
                        nx = (m + 1) % 2
                        if m < 4:
                            P.add("act", lambda e, nx=nx, bP=bP, q=q: e.copy(out=Pm[q][nx][:, :, :], in_=bP[:].rearrange("p (h c) -> p h c", h=4)),
                                  reads=[brP], writes=[("Pm", q, nx)])
                        cp(Qm[q][nx][:, :, :], bQ[:].rearrange("p (h c) -> p h c", h=4), [brQ], [("Qm", q, nx)])
                        P.add("act", lambda e, bR=bR, q=q: e.copy(out=Rb[q][:, :, :], in_=bR[:].rearrange("p (h c) -> p h c", h=4)),
                              reads=[brR], writes=[("Rb", q)])
                    else:
                        cp(TT[:, q, :, :], bR[:].rearrange("p (h c) -> p h c", h=4), [brR], [("TT", q)])
                    yield

        def gdn_chain():
            if prompt:
                steps = [(s, s, hf) for hf in range(2) for s in range(2)]
            else:
                steps = [(s, 0, s) for s in range(2)]
            for (s, q, hf) in steps:
                rows = slice(hf * 64, (hf + 1) * 64)
                c0 = q * 128 + hf * 64
                ch = c0 // 64
                qc = slice(q * 128, (q + 1) * 128)
                Sk = ("S", l, s); Sbk = ("Sb", l, s)
                bKS, brKS = B(pbA()); bO, brO = B(pbA())
                for h in range(4):
                    mm(bKS[:, h * 128:(h + 1) * 128], kn[:, h, qc], Sb[:, l, s, h, :], h == 0, [("kn", h), Sbk], [brKS])
                for h in range(4):
                    mm(bO[:, h * 64:(h + 1) * 64], Sb[:, l, s, h, :], qg[:, h, c0:c0 + 64], h == 0, [("qg", h), Sbk], [brO])
                tt(tmpp[rows, :, :], bKS[rows, :].rearrange("p (h d) -> p h d", h=4), bvt[rows, q, :, :], ALU.add,
                   [brKS, ("bvt", q)], [("tmpp", hf)])
                yield
                bV, brV = B(pbA())
                for h in range(4):
                    mm(bV[:, h * 128:(h + 1) * 128], TT[rows, q, h, :], tmpp[rows, h, :], h == 0, [("TT", q), ("tmpp", hf)], [brV])
                P.add("act", lambda e, rows=rows, q=q, bV=bV: e.copy(out=vnew[rows, q, :, :], in_=bV[rows, :].rearrange("p (h d) -> p h d", h=4)),
                      reads=[brV], writes=[("vnew", q, hf)])
                yield
                for h in range(4):
                    mm(bO[:, h * 64:(h + 1) * 64], vnew[rows, q, h, :], ATP[rows, q, h, hf * 64:hf * 64 + 64], False,
                       [("vnew", q, hf), ("ATP", q)], [brO])
                P.add("act", lambda e, c0=c0, bO=bO: e.copy(out=oT[:, :, c0:c0 + 64], in_=bO[:, 0:256].rearrange("p (h c) -> p h c", h=4)),
                      reads=[brO], writes=[("oT", ch)])
                bD, brD = B(pbA())
                for h in range(4):
                    mm(bD[:, h * 128:(h + 1) * 128], kdt[rows, q, h, :], vnew[rows, q, h, :], h == 0, [("kdt", q), ("vnew", q, hf)], [brD])
                for h in range(4):
                    stt(S[:, l, s, h, :], S[:, l, s, h, :], eglbc[:, h, ch:ch + 1], bD[:, h * 128:(h + 1) * 128], ALU.mult, ALU.add,
                        [Sk, ("eglbc", h), brD], [Sk])
                P.add("act", lambda e, s=s: e.copy(out=Sb[:, l, s, :, :], in_=S[:, l, s, :, :]), reads=[Sk], writes=[Sbk])
                yield
            for h in range(4):
                s_ = sqbW
                act(s_[:, 0:N], oT[:, h, 0:N], AF.Square, ["oT"], ["sqbW"])
                yield
                bk, br = B(pbA())
                mm(bk[:, 0:N], onesDV[:], s_[:, 0:N], True, ["sqbW", "onesDV"], [br])
                act(lnvW[:, 0:N], bk[:, 0:N], AF.Ln, [br, SETUP], ["lnvW"], bias=cv[:, 0:1])
                act(rstdW[:, 0:N], lnvW[:, 0:N], AF.Exp, ["lnvW"], ["rstdW"], scale=-0.5)
                stt(lnvW[:, 0:N], oT[:, h, 0:N], gng_s[:, l:l + 1], rstdW[:, 0:N], ALU.mult, ALU.mult, ["oT", "rstdW", "lnvW", SETUP], ["lnvW"])
                tt(OT[:, h, 0:N], lnvW[:, 0:N], zg[:, h, 0:N], ALU.mult, ["lnvW", (zgk, h)], [("OT", h)])
                yield

        def gmlp():
            for q in range(NT):
                bt, brt = tokb[q]
                act(vmg[:], bt[:, 0:256], AF.Gelu_apprx_tanh, [brt], ["vmg"])
                if prompt:
                    P.add("act", lambda e, q=q, bt=bt: e.copy(out=Vp[:, l, q, cur, :, 64:128], in_=bt[:, 260:388].rearrange("p (g d) -> p g d", g=2)),
                          reads=[brt], writes=[("Vp", l, q, cur)])
                else:
                    for s in range(2):
                        rws = slice(s * 64, (s + 1) * 64)
                        P.add("act", lambda e, s=s, rws=rws, bt=bt: e.copy(out=Vp[rws, l, s, 1, :, 64:128], in_=bt[rws, 260:388].rearrange("p (g d) -> p g d", g=2)),
                              reads=[brt], writes=[("Vp", l, s, 1)])
                if last or not prompt:
                    cp(vf[:, q, :], bt[:, 260:388], [brt], [("vf", q)])
                v3 = vmg[:].rearrange("p (h d) -> p h d", h=4)
                P.add("dve", lambda e, v3=v3: e.tensor_reduce(out=st4[:, 0:4], in_=v3, axis=AX.X, op=ALU.add), reads=["vmg"], writes=[("st4", 0)])
                ts(st4[:, 0:4], st4[:, 0:4], -1.0 / 64, ALU.mult, [("st4", 0)], [("st4", 0)])
                x3 = xc[:].rearrange("p (h d) -> p h d", h=4)
                tt(x3, v3, st4[:, 0:4].unsqueeze(2).to_broadcast([128, 4, 64]), ALU.add, ["vmg", ("st4", 0)], ["xc"])
                tt(sqv[:], xc[:], xc[:], ALU.mult, ["xc", "vmg"], ["vmg", "vmg"])
                P.add("dve", lambda e: e.tensor_reduce(out=st4[:, 4:8], in_=sqv[:].rearrange("p (h d) -> p h d", h=4), axis=AX.X, op=ALU.add),
                      reads=["vmg"], writes=[("st4", 1)])
                yield
                act(st4[:, 4:8], st4[:, 4:8], AF.Ln, [("st4", 1), SETUP], [("st4", 1)], scale=1.0 / 64, bias=cv[:, 0:1])
                act(st4[:, 4:8], st4[:, 4:8], AF.Exp, [("st4", 1)], [("st4", 1)], scale=-0.5)
                tt(x3, x3, st4[:, 4:8].unsqueeze(2).to_broadcast([128, 4, 64]), ALU.mult, ["xc", ("st4", 1)], ["xc"])
                tt(xc[:], xc[:], lng_bc[:, l, :], ALU.mult, ["xc", SETUP], ["xc"])
                tt(xc[:], xc[:], lnb_bc[:, l, :], ALU.add, ["xc", SETUP], ["xc"])
                if not prompt:
                    for s in range(2):
                        dma("sp", o_ms[l, s], xc[s * 64:(s + 1) * 64, :], ["xc"], [], "o_ms", store=True)
                vp4 = vpad[:, q, :, :].rearrange("p (a b) c -> p a b c", b=2)
                x4 = xc[:].rearrange("p (a b d) -> p a b d", a=2, b=2)
                cp(vp4[:, :, 0, 0:64], x4[:, :, 0, :], ["xc"], [("vpad", q, 0)])
                cp(vp4[:, :, 1, 64:128], x4[:, :, 1, :], ["xc"], [("vpad", q, 1)])
                yield
                for pr in range(2):
                    bk, br = B(pbB())
                    mm(bk[:, 0:128], vpad[:, q, 2 * pr, :], wsT[:, l, var, 2 * pr, :], True, [("vpad", q), ("wsT", l, var, 2 * pr)], [br])
                    mm(bk[:, 0:128], vpad[:, q, 2 * pr + 1, :], wsT[:, l, var, 2 * pr + 1, :], False, [("vpad", q), ("wsT", l, var, 2 * pr + 1)], [br])
                    tt(tmpb[:], bk[:, 0:128], bsbc[:, l, var, pr, :], ALU.add, [br, SETUP, "tmpb"], ["tmpb"])
                    tt(OT[:, 4 + pr, q * 128:(q + 1) * 128], uT[:, pr, q * 128:(q + 1) * 128], tmpb[:], ALU.mult, [("uT", pr), "tmpb"], [("OT", 4 + pr, q)])
                yield

        def swa():
            for i in range(3):
                s_ = sqbW
                act(s_[:, 0:N], swx[:, i, 0:N], AF.Square, [("swx", i)], ["sqbW"])
                bk, br = B(pbB())
                mm(bk[:, 0:N], ones64[:], s_[:, 0:N], True, ["sqbW", "ones64"], [br])
                act(lnvW[:, 0:N], bk[:, 0:N], AF.Ln, [br, SETUP], ["lnvW"], bias=cv[:, 0:1])
                act(rstdW[:, 0:N], lnvW[:, 0:N], AF.Exp, ["lnvW"], ["rstdW"], scale=-0.5)
                if i < 2:
                    stt(qAB[:, i, 0:N], swx[:, i, 0:N], qng_s[:, l:l + 1], rstdW[:, 0:N], ALU.mult, ALU.mult, [("swx", i), "rstdW", SETUP], [("qAB", i)])
                else:
                    stt(kTf[:, 0:N], swx[:, 2, 0:N], kng_s[:, l:l + 1], rstdW[:, 0:N], ALU.mult, ALU.mult, [("swx", 2), "rstdW", SETUP], ["kTf"])
                    if prompt:
                        for s in range(2):
                            cp(kTh[:, l, s, cur, :], kTf[:, s * 128:(s + 1) * 128], ["kTf"], [("kTh", l, s, cur)])
                    else:
                        cp(kTh[:, l, 0, 1, :], kTf[:, 0:128], ["kTf"], [("kTh", l, 0, 1)])
                yield
            if not prompt:
                for s in range(2):
                    dma("sp", kout[:], cK[l, s], [], ["kout"], "kout")
                    bk, br = B(pbB())
                    mm(bk[:, 0:128], kout[:], identF[:], True, ["kout", SETUP], [br], tr=True)
                    cp(kTh[:, l, s, 0, :], bk[:, 0:128], [br], [("kTh", l, s, 0)])
                    dma("pool", Vp[:, l, s, 0, :, 64:128], cV[l, s].rearrange("t (g d) -> t g d", g=2), [], [("Vp", l, s, 0)], ("vcache", s))
                    dma("sp", o_ks[l, s, 0:64, :], cK[l, s, 64:128, :], [], [], "o_cache", store=True)
                    dma("sp", o_vs[l, s, 0:64, :], cV[l, s, 64:128, :], [], [], "o_cache", store=True)
                    yield
        def swa_att():
            for s in range(2):
                bO, brO = banks[6], ("ps", 6)
                bS, brS = banks[7], ("ps", 7)
                if prompt:
                    qcols = slice(s * 128, (s + 1) * 128); NQ = 128
                    ktiles = ([("h", hsl)] if st > 0 else []) + [("c", cur)]
                else:
                    qcols = slice(s * 64, (s + 1) * 64); NQ = 64
                    ktiles = [("h", 0), ("c", 1)]
                first = True
                for ki, (kk, slot) in enumerate(ktiles):
                    pt = PT[ki % 2]; ptk = ("PT", ki % 2)
                    if prompt or kk == "h":
                        krows = slice(0, 128); kT_ = kTh[:, l, s, slot, :]; kkey = ("kTh", l, s, slot); vkey = ("Vp", l, s, slot)
                        vs_ = s; vslot = slot
                    else:
                        krows = slice(s * 64, (s + 1) * 64); kT_ = kTh[:, l, 0, 1, :]; kkey = ("kTh", l, 0, 1); vkey = ("Vp", l, s, 1)
                        vs_ = s; vslot = 1
                    for g in range(2):
                        bk, br = B(pbB())
                        gr = slice(g * 64, (g + 1) * 64)
                        for hh in range(2):
                            h = 2 * g + hh
                            mm(bk[:, hh * NQ:(hh + 1) * NQ], kT_[gr, :], qAB[gr, hh, qcols], hh == 0, [kkey, ("qAB", hh)], [br])
                        act(pt[:, 2 * g:2 * g + 2, 0:NQ], bk[:, 0:2 * NQ].rearrange("p (h c) -> p h c", h=2), AF.Exp, [br], [(ptk[0], ptk[1], g)], scale=0.125)
                    if prompt:
                        if kk == "h":
                            mset(pt[0:64, :, 64:128], 0.0, [ptk])
                        else:
                            mset(pt[64:128, :, 0:64], 0.0, [ptk])
                    yield
                    for h in range(4):
                        mm(bS[:, h * NQ:(h + 1) * NQ], onesB[krows, :], pt[krows, h, 0:NQ], first and h == 0, [ptk, "onesB"], [brS])
                    for h in range(4):
                        g = h // 2
                        lo = 64 if h % 2 == 0 else 0
                        mm(bO[:, g * NQ:(g + 1) * NQ], Vp[krows, l, vs_, vslot, g, lo:lo + 128], pt[krows, h, 0:NQ], first and h == 0, [ptk, vkey], [brO])
                    first = False
                    yield
                for h in range(4):
                    ts(rec[:, h, 0:NQ], bS[:, h * NQ:(h + 1) * NQ], esink[:, l * 4 + h:l * 4 + h + 1], ALU.add, [brS, "esinkx"], [("rec", h)])
                act(rec[:, :, 0:NQ], rec[:, :, 0:NQ], AF.Ln, ["rec"], ["rec"])
                act(rec[:, :, 0:NQ], rec[:, :, 0:NQ], AF.Exp, ["rec"], ["rec"], scale=-1.0)
                for g in range(2):
                    for hh in range(2):
                        rws = slice(hh * 64, (hh + 1) * 64)
                        tt(OT[rws, 6 + g, qcols], bO[rws, g * NQ:(g + 1) * NQ], rec[rws, 2 * g + hh, 0:NQ], ALU.mult, [brO, "rec"], [("OT", 6 + g, s, hh)])
                yield

        def outputs():
            if last or not prompt:
                for q in range(NT):
                    bk, br = B(pbA())
                    mm(bk[:, 0:128], kTf[:, q * 128:(q + 1) * 128], identF[:], True, ["kTf", SETUP], [br], tr=True)
                    cp(kout[:], bk[:, 0:128], [br, "kout"], ["kout"])
                    if prompt:
                        dma("sp", o_kp[l, q], kout[:], ["kout"], [], "kout", store=True)
                        dma("sp", o_vp[l, q], vf[:, q, :], [("vf", q)], [], ("o_v", q), store=True)
                    else:
                        for s in range(2):
                            dma("sp", o_ks[l, s, 64:128, :], kout[s * 64:(s + 1) * 64, :], ["kout"], [], "kout", store=True)
                            dma("sp", o_vs[l, s, 64:128, :], vf[s * 64:(s + 1) * 64, 0, :], [("vf", 0)], [], ("o_v", 0), store=True)
                for s in range(2):
                    og = o_gp if prompt else o_gs
                    dma("sp", og[l, s].rearrange("h k v -> k h v"), S[:, l, s, :, :], [("S", l, s)], [], "o_g", store=True)
                    oc = o_cp if prompt else o_cs
                    for c in range(12):
                        dma("sp", oc[l, s, :, c * 128:(c + 1) * 128].rearrange("t p -> p t"), hist[:, l, c, s, :], [("hist", l, c)], [], "o_c", store=True, slow=True)

        def Mh_stream():
            yield from fac_stream()
            yield from gdn_pre()
            yield from gmlp()
            yield from swa()

        def Mt_stream():
            yield from gdn_chain()
            yield from swa_att()
            outputs()
            yield

        def D2_stream():
            dpool[0] = [3, 4, 5]
            for g in range(2):
                R, rk = next_slab("out", l, g)
                Rv = R[:, 0:4096].rearrange("p (k c) -> p k c", k=8)
                for j in range(4):
                    c = 4 * g + j
                    bk, br = B(pb())
                    for kc in range(8):
                        mm(bk[:, 0:N], Rv[:, kc, j * 128:(j + 1) * 128], OT[:, kc, 0:N], kc == 0, [rk, ("OT", kc)], [br])
                    tt(xT[:, c, 0:N], xT[:, c, 0:N], bk[:, 0:N], ALU.add, [("xT", xi, c), br], [("xT", xi, c)])
            yield
            rmsnorm(l, g2, N, xT, xi)
            yield
            for g in range(11):
                R, rk = next_slab("gu", l, g)
                Gv = R[:, 0:2048].rearrange("p (k c) -> p k c", k=8)
                Uv = R[:, 2048:4096].rearrange("p (k c) -> p k c", k=8)
                for j in range(2):
                    c = 2 * g + j
                    bGU, brGU = B(pb())
                    bG = bGU[:, 0:256]; bU = bGU[:, 256:512]
                    for kc in range(8):
                        mm(bG[:, 0:N], Gv[:, kc, j * 128:(j + 1) * 128], hT[:, kc, 0:N], kc == 0, [rk, ("hT", kc)], [brGU])
                    for kc in range(8):
                        mm(bU[:, 0:N], Uv[:, kc, j * 128:(j + 1) * 128], hT[:, kc, 0:N], False, [rk, ("hT", kc)], [brGU])
                    sgt = sg[c % 2]
                    act(sgt[:, 0:N], bG[:, 0:N], AF.Tanh, [brGU], [("sg", c % 2)], scale=0.5)
                    stt(sgt[:, 0:N], sgt[:, 0:N], 1.0, bG[:, 0:N], ALU.add, ALU.mult, [("sg", c % 2), brGU], [("sg", c % 2)])
                    tt(actT[:, c, 0:N], sgt[:, 0:N], bU[:, 0:N], ALU.mult, [("sg", c % 2), brGU], [("actT", c)])
                    yield
            for g in range(8):
                R, rk = next_slab("down", l, g)
                Rv = R[:, 0:NFF * 128].rearrange("p (k c) -> p k c", k=NFF)
                c = g
                bk, br = B(pb())
                for kc in range(NFF):
                    mm(bk[:, 0:N], Rv[:, kc, :], actT[:, kc, 0:N], kc == 0, [rk, ("actT", kc)], [br])
                stt(xT[:, c, 0:N], bk[:, 0:N], 0.5, xT[:, c, 0:N], ALU.mult, ALU.add, [("xT", xi, c), br], [("xT", xi, c)])
                yield
            if l == L - 1:
                store_y(mode, st, N, xT, xi)
            yield

        return D1_stream(), Mh_stream(), Mt_stream(), D2_stream()

    sb_eglF = sb("egl_b", [128, 2, 8], BF16)
    sb_egl = sb_eglF[0:4]
    mset(sb_eglF[:], 0.0, ["eglb", "eglb2"])

    NTILE = SEQ // 128
    order = [(0, 0)] + [x for t in range(NTILE) for x in ((t + 1, 0), (t, 1))] + [(NTILE, 1)]

    def load_sample_state(l):
        barrier([("S", l), ("hist", l), ("Sb", l)])
        for s in range(2):
            dma("sp", S[:, l, s, :, :], sG[l, s].rearrange("h k v -> k h v"), [], [("S", l, s)], ("sload", l, s))
            P.add("act", lambda e, l=l, s=s: e.copy(out=Sb[:, l, s, :, :], in_=S[:, l, s, :, :]), reads=[("S", l, s)], writes=[("Sb", l, s)])
            for c in range(12):
                dma("sp", hist[:, l, c, s, :], sC[l, s, :, c * 128:(c + 1) * 128].rearrange("t p -> p t"), [], [("hist", l, c, s)], ("hload", l), slow=True)
        barrier([("hist", l)])

    prev = None
    pre = set()
    for k, (tile, l) in enumerate(order):
        if tile == NTILE:
            if prev is not None:
                interleave([prev[0]])
            load_sample_state(l)
            D1g, Mh, Mt, D2g = stl("s", 0, l, xTs[tile % 3], tile % 3, k % 2, preloaded=(k in pre))
            interleave([D1g])
        else:
            D1g, Mh, Mt, D2g = stl("p", tile, l, xTs[tile % 3], tile % 3, k % 2, preloaded=(k in pre))
            interleave([D1g] + ([prev[0]] if prev is not None else []))
        if k + 1 < len(order) and order[k + 1][1] == 0:
            nt = order[k + 1][0]
            dpool[0] = [3, 4, 5]
            if nt == NTILE:
                load_x("s", 0, 128, xTs[nt % 3], nt % 3)
            else:
                load_x("p", nt, 256, xTs[nt % 3], nt % 3)
            pre.add(k + 1)
        interleave(([prev[1]] if prev is not None else []) + [Mh])
        prev = (Mt, D2g)
    interleave([prev[0]])
    interleave([prev[1]])
    assert used[0] == len(slabs), (used[0], len(slabs))

    P.finalize()
    out_keys = sorted(store_keys, key=str)
    with nc.Block() as block:
        @block.sync
        def _(e):
            P.run_engine("sp", e, sems, dma_sems)
            for k in out_keys:
                e.wait_ge(dma_sems[k], 16 * P.dma_count[k])

        @block.scalar
        def _(e):
            P.run_engine("act", e, sems, dma_sems)

        @block.vector
        def _(e):
            P.run_engine("dve", e, sems, dma_sems)

        @block.gpsimd
        def _(e):
            P.run_engine("pool", e, sems, dma_sems)

        @block.tensor
        def _(e):
            P.run_engine("pe", e, sems, dma_sems)
    es.close()
    return nc, P


_CACHE = {}


def kernel(**inputs):
    if "nc" not in _CACHE:
        _CACHE["nc"] = build()
    nc, P = _CACHE["nc"]
    cst = _consts()
    wl = _layout_weights(inputs)
    f = lambda k: np.asarray(inputs[k], np.float32)
    xpa = f("x_prompt"); xsa = f("x_sample")
    ck = f("cache_swa_k").reshape(L, 16, 128, 128); cvv = f("cache_swa_v").reshape(L, 16, 128, 128)
    sg_ = f("state_gdn"); sc_ = f("state_gdn_conv")
    in_maps = []
    for c in range(NCORE):
        b = slice(2 * c, 2 * c + 2)
        m = {"xp": np.ascontiguousarray(xpa[b]), "xs": np.ascontiguousarray(xsa[b]),
             "cK": np.ascontiguousarray(ck[:, b]), "cV": np.ascontiguousarray(cvv[:, b]),
             "sG": np.ascontiguousarray(sg_[:, b]), "sC": np.ascontiguousarray(sc_[:, b])}
        m.update(wl); m.update(cst)
        in_maps.append(m)
    res = run_bass_kernel_spmd(nc, in_maps, core_ids=list(range(NCORE)))
    r = res.results
    cat = lambda k, ax: np.concatenate([np.asarray(r[c][k], np.float32) for c in range(NCORE)], axis=ax)
    yp = cat("yp", 0); ys = cat("ys", 0)
    kp = cat("o_kp", 1).reshape(L, 16, 128, 2, 64); vp = cat("o_vp", 1).reshape(L, 16, 128, 2, 64)
    gp = cat("o_gp", 1); cpp = cat("o_cp", 1)
    ks = cat("o_ks", 1).reshape(L, 16, 128, 2, 64); vs = cat("o_vs", 1).reshape(L, 16, 128, 2, 64)
    gs = cat("o_gs", 1); cs = cat("o_cs", 1)
    ms = cat("o_ms", 1).reshape(L, 16, TS, 4, 64)
    return (yp, ys, kp, vp, gp, cpp, ks, vs, gs, cs, ms)
```
